# Optimizing a Trainium2 kernel written in Bass

```python
import jax, jax.numpy as jnp
from jax import lax
import numpy as np

D_MODEL = 2048
BATCH = 16
SEQ = 2048
DEPTH = 2

CHUNK = 64
N_EVEN = (DEPTH + 1) // 2
N_ODD = DEPTH // 2

D_A = D_MODEL // 2
A_HEADS = 8
A_HEAD_DIM = D_A // A_HEADS
CONV_W = 4
LRU_C = 8.0
D_B = D_MODEL // 2
B_GROUPS = 8
B_GROUP_DIM = D_B // B_GROUPS
B_BLOCK = 128

C_HEADS = 8
C_HEAD_DIM = (D_MODEL // 2) // C_HEADS
D_C = C_HEADS * C_HEAD_DIM
SB_QBLOCK = 128
D_D = D_MODEL // 2
POOL_WINDOWS = (2, 4, 8, 16)
D_GROUPS = len(POOL_WINDOWS)
D_GROUP_DIM = D_D // D_GROUPS

D_FF = -(-8 * D_MODEL // (3 * 256)) * 256
ALPHA = (2 * DEPTH) ** 0.25
BETA = (8 * DEPTH) ** -0.25
EPS = 1e-5

kernel_name = "hybrid_rglru_gmlp_stickbreak_pool_deepnorm"


def layer_norm(x, g, b):
    xf = x.astype(jnp.float32)
    mu = xf.mean(-1, keepdims=True)
    var = jnp.square(xf - mu).mean(-1, keepdims=True)
    return ((xf - mu) * lax.rsqrt(var + EPS) * g + b).astype(x.dtype)


def adaln_terms(c, w, b):
    m = jax.nn.silu(c) @ w + b
    return jnp.split(m[:, None, :], 6, axis=-1)


def causal_dwconv(x, w, b):
    k, ch = w.shape
    y = lax.conv_general_dilated(x, w[:, None, :], window_strides=(1,), padding=((k - 1, 0),),
                                 dimension_numbers=('NWC', 'WIO', 'NWC'), feature_group_count=ch)
    return y + b


def rg_lru(x, w_r, b_r, w_i, b_i, lam):
    bsz, s, _ = x.shape
    xh = x.reshape(bsz, s, A_HEADS, A_HEAD_DIM)
    r = jax.nn.sigmoid(jnp.einsum('bshd,hde->bshe', xh, w_r) + b_r).reshape(bsz, s, D_A)
    i = jax.nn.sigmoid(jnp.einsum('bshd,hde->bshe', xh, w_i) + b_i).reshape(bsz, s, D_A)
    log_a = LRU_C * r.astype(jnp.float32) * jax.nn.log_sigmoid(lam.astype(jnp.float32))
    a = jnp.exp(log_a)
    u = jnp.sqrt(-jnp.expm1(2.0 * log_a)) * (i * x).astype(jnp.float32)

    def combine(lhs, rhs):
        a1, b1 = lhs
        a2, b2 = rhs
        return a1 * a2, a2 * b1 + b2

    _, h = lax.associative_scan(combine, (a, u), axis=1)
    return h.astype(x.dtype)


def chunk_spatial_gate(u, v, ln_g, ln_b, w_s, b_s):
    bsz, s, _ = v.shape
    v = layer_norm(v, ln_g, ln_b)
    vb = v.reshape(bsz, s // B_BLOCK, B_BLOCK, B_GROUPS, B_GROUP_DIM)
    pos = jnp.arange(B_BLOCK) // CHUNK
    mask = pos[None, :] <= pos[:, None]
    w = jnp.where(mask, w_s, 0.0)
    sv = jnp.einsum('gts,bnsgd->bntgd', w, vb) + b_s.T[:, :, None]
    return u * sv.reshape(bsz, s, D_B)


def stick_breaking_attention(q, k, v):
    s = q.shape[2]
    scale = C_HEAD_DIM ** -0.5
    outs = []
    for blk in range(s // SB_QBLOCK):
        q0 = blk * SB_QBLOCK
        q1 = q0 + SB_QBLOCK
        z = jnp.einsum('bhtd,bhsd->bhts', q[:, :, q0:q1], k[:, :, :q1]).astype(jnp.float32) * scale
        causal = jnp.arange(q1)[None, :] < jnp.arange(q0, q1)[:, None]
        log_1m = jnp.where(causal, jax.nn.log_sigmoid(-z), 0.0)
        between = lax.cumsum(log_1m, axis=3, reverse=True) - log_1m
        weight = jnp.where(causal, jnp.exp(jax.nn.log_sigmoid(z) + between), 0.0)
        outs.append(jnp.einsum('bhts,bhsd->bhtd', weight.astype(v.dtype), v[:, :, :q1]))
    return jnp.concatenate(outs, axis=2)


def multiscale_pool(x, w_g, scale):
    bsz, s, _ = x.shape
    xg = x.reshape(bsz, s, D_GROUPS, D_GROUP_DIM).astype(jnp.float32)
    cs = jnp.concatenate([jnp.zeros_like(xg[:, :1]), jnp.cumsum(xg, axis=1)], axis=1)
    t = jnp.arange(s)[:, None]
    win = jnp.array(POOL_WINDOWS)[None, :]
    lo = jnp.maximum(t + 1 - win, 0)
    cnt = jnp.minimum(t + 1, win).astype(jnp.float32)
    window_sum = cs[:, 1:] - cs[:, lo, jnp.arange(D_GROUPS)[None, :]]
    p = window_sum / cnt[None, :, :, None] - xg
    y = jnp.einsum('bsgd,gde->bsge', p.astype(x.dtype), w_g)
    return y.reshape(bsz, s, D_D) * scale


def mixer_ab(h, w_in, conv_w, conv_b, w_r, b_r, w_i, b_i, lam, vn_g, vn_b, w_s, b_s, w_out):
    z = h @ w_in
    xa, ga, ub, vb = jnp.split(z, [D_A, 2 * D_A, 2 * D_A + D_B], axis=-1)
    ya = rg_lru(causal_dwconv(xa, conv_w, conv_b), w_r, b_r, w_i, b_i, lam) * jax.nn.gelu(ga)
    yb = chunk_spatial_gate(jax.nn.gelu(ub), jax.nn.gelu(vb), vn_g, vn_b, w_s, b_s)
    return jnp.concatenate([ya, yb], axis=-1) @ w_out


def mixer_cd(h, w_in, w_pool, pool_scale, w_out):
    bsz, s, _ = h.shape
    z = h @ w_in
    q, k, v, p = jnp.split(z, [D_C, 2 * D_C, 3 * D_C], axis=-1)
    heads = lambda t: t.reshape(bsz, s, C_HEADS, C_HEAD_DIM).transpose(0, 2, 1, 3)
    yc = stick_breaking_attention(heads(q), heads(k), heads(v)).transpose(0, 2, 1, 3).reshape(bsz, s, D_C)
    yd = multiscale_pool(p, w_pool, pool_scale)
    return jnp.concatenate([yc, yd], axis=-1) @ w_out


def swiglu(h, w_gate, w_up, w_down):
    return (jax.nn.silu(h @ w_gate) * (h @ w_up)) @ w_down


def setup_inputs(seed: int = 0) -> dict:
    key = jax.random.key(seed)
    ks = jax.random.split(key, 32)
    nrm = lambda k, shape, s: jax.random.normal(k, shape, jnp.float32) * s
    d = D_MODEL
    a0 = jax.random.uniform(ks[12], (N_EVEN, D_A), jnp.float32, 0.9, 0.999)
    sig = a0 ** (1.0 / LRU_C)
    cd_w_in = nrm(ks[20], (N_ODD, d, 3 * D_C + D_D), d ** -0.5)
    cd_w_in = cd_w_in.at[:, :, 2 * D_C:3 * D_C].multiply(BETA)
    return {
        "x": nrm(ks[0], (BATCH, SEQ, d), 1.0),
        "c": nrm(ks[1], (BATCH, d), 1.0),
        "ada_w": nrm(ks[2], (DEPTH, d, 6 * d), 0.5 * d ** -0.5),
        "ada_b": nrm(ks[3], (DEPTH, 6 * d), 0.01),
        "norm_g": 1.0 + nrm(ks[4], (DEPTH, 2, d), 0.01),
        "norm_b": nrm(ks[5], (DEPTH, 2, d), 0.01),
        "ffn_w_gate": nrm(ks[6], (DEPTH, d, D_FF), d ** -0.5),
        "ffn_w_up": nrm(ks[7], (DEPTH, d, D_FF), BETA * d ** -0.5),
        "ffn_w_down": nrm(ks[8], (DEPTH, D_FF, d), BETA * D_FF ** -0.5),
        "ab_w_in": nrm(ks[9], (N_EVEN, d, 2 * D_A + 2 * D_B), d ** -0.5),
        "ab_conv_w": nrm(ks[10], (N_EVEN, CONV_W, D_A), CONV_W ** -0.5),
        "ab_conv_b": nrm(ks[11], (N_EVEN, D_A), 0.01),
        "ab_w_r": nrm(ks[13], (N_EVEN, A_HEADS, A_HEAD_DIM, A_HEAD_DIM), A_HEAD_DIM ** -0.5),
        "ab_b_r": nrm(ks[14], (N_EVEN, A_HEADS, A_HEAD_DIM), 0.01),
        "ab_w_i": nrm(ks[15], (N_EVEN, A_HEADS, A_HEAD_DIM, A_HEAD_DIM), A_HEAD_DIM ** -0.5),
        "ab_b_i": nrm(ks[16], (N_EVEN, A_HEADS, A_HEAD_DIM), 0.01),
        "ab_lambda": jnp.log(sig) - jnp.log1p(-sig),
        "ab_vnorm_g": 1.0 + nrm(ks[17], (N_EVEN, D_B), 0.01),
        "ab_vnorm_b": nrm(ks[18], (N_EVEN, D_B), 0.01),
        "ab_w_s": nrm(ks[19], (N_EVEN, B_GROUPS, B_BLOCK, B_BLOCK), 0.5 * B_BLOCK ** -0.5),
        "ab_b_s": 1.0 + nrm(ks[21], (N_EVEN, B_GROUPS, B_BLOCK), 0.01),
        "ab_w_out": nrm(ks[22], (N_EVEN, D_A + D_B, d), BETA * (D_A + D_B) ** -0.5),
        "cd_w_in": cd_w_in,
        "cd_w_pool": nrm(ks[23], (N_ODD, D_GROUPS, D_GROUP_DIM, D_GROUP_DIM), D_GROUP_DIM ** -0.5),
        "cd_pool_scale": 1.0 + nrm(ks[24], (N_ODD, D_D), 0.01),
        "cd_w_out": nrm(ks[25], (N_ODD, D_C + D_D, d), BETA * (D_C + D_D) ** -0.5),
    }


def reference(x, c, ada_w, ada_b, norm_g, norm_b, ffn_w_gate, ffn_w_up, ffn_w_down,
              ab_w_in, ab_conv_w, ab_conv_b, ab_w_r, ab_b_r, ab_w_i, ab_b_i, ab_lambda,
              ab_vnorm_g, ab_vnorm_b, ab_w_s, ab_b_s, ab_w_out,
              cd_w_in, cd_w_pool, cd_pool_scale, cd_w_out):
    for layer in range(DEPTH):
        shift1, scale1, gate1, shift2, scale2, gate2 = adaln_terms(c, ada_w[layer], ada_b[layer])
        j = layer // 2
        h = x * (1.0 + scale1) + shift1
        if layer % 2 == 0:
            out = mixer_ab(h, ab_w_in[j], ab_conv_w[j], ab_conv_b[j], ab_w_r[j], ab_b_r[j],
                           ab_w_i[j], ab_b_i[j], ab_lambda[j], ab_vnorm_g[j], ab_vnorm_b[j],
                           ab_w_s[j], ab_b_s[j], ab_w_out[j])
        else:
            out = mixer_cd(h, cd_w_in[j], cd_w_pool[j], cd_pool_scale[j], cd_w_out[j])
        x = layer_norm(ALPHA * x + (1.0 + gate1) * out, norm_g[layer, 0], norm_b[layer, 0])
        h = x * (1.0 + scale2) + shift2
        ffn_out = swiglu(h, ffn_w_gate[layer], ffn_w_up[layer], ffn_w_down[layer])
        x = layer_norm(ALPHA * x + (1.0 + gate2) * ffn_out, norm_g[layer, 1], norm_b[layer, 1])
    return x
```

```python
from contextlib import ExitStack, contextmanager
import numpy as np
import concourse.bass as bass
import concourse.mybir as mybir
from concourse.bass_utils import run_bass_kernel_spmd

F32 = mybir.dt.float32
BF16 = mybir.dt.bfloat16
AF = mybir.ActivationFunctionType
ALU = mybir.AluOpType

DM = 2048
KC = DM // 128
DFF = 5632
FC = DFF // 128
NL = 2
NCORES = 8
ALPHA = (2 * NL) ** 0.25
EPS = 1e-5
POOL_W = (2, 4, 8, 16)
SEM_LIMIT = 30000
import os
XTF = os.environ.get('XTF', '').split(',')


class StopBuild(Exception):
    pass


class Buf:
    __slots__ = ("name", "w", "r")

    def __init__(self, name=""):
        self.name = name
        self.w = None
        self.r = []


class Eng:
    def __init__(self, name, h, is_pe=False):
        self.name = name
        self.h = h
        self.semidx = None
        self.tick = 0
        self.waited = {}
        self.prog = []
        self.pending = []
        self.is_pe = is_pe
        self.ring = []
        self.ring_pos = 0


class K:
    def __init__(self, nc, same_sync=True, ring=6):
        self.nc = nc
        self.same_sync = same_sync
        self.sems = []
        self.ring_n = ring
        self.live = {}
        self.halted = False

    def newsem(self, name):
        s = self.stack.enter_context(self.nc.semaphore(name))
        self.sems.append(s)
        return len(self.sems) - 1

    def start(self, stack):
        nc = self.nc
        self.stack = stack
        self.pe = Eng("pe", nc.tensor, is_pe=True)
        self.act = Eng("act", nc.scalar)
        self.dve = Eng("dve", nc.vector)
        self.pool = Eng("pool", nc.gpsimd)
        self.sp = Eng("sp", nc.sync)
        self.engs = [self.pe, self.act, self.dve, self.pool, self.sp]
        for e in self.engs:
            e.semidx = self.newsem(f"t_{e.name}0")
            e.nsem = 1
        for e in (self.sp, self.act, self.pool):
            e.ring = [[self.newsem(f"r_{e.name}{i}"), 0] for i in range(self.ring_n)]

    def _deps(self, e, reads, writes):
        deps = {}

        def need(tok):
            if tok is None:
                return
            s, v = tok[0], tok[1]
            if tok[2] is e and (e.is_pe or not self.same_sync):
                return
            if v is None:
                raise RuntimeError(f"dependency on pending op ({tok[2].name}) from {e.name}")
            if deps.get(s, 0) < v:
                deps[s] = v

        for b in reads:
            need(b.w)
        for b in writes:
            need(b.w)
            for t in b.r:
                need(t)
        waits = []
        for s, v in deps.items():
            if e.waited.get(s, 0) < v:
                e.waited[s] = v
                waits.append((s, v))
        return waits

    def _record(self, tok, reads, writes):
        for b in reads:
            b.r.append(tok)
        for b in writes:
            b.w = tok
            b.r = []

    def op(self, e, fn, reads=(), writes=(), tick=True):
        if self.halted:
            return
        waits = self._deps(e, reads, writes)
        if tick:
            if e.tick >= SEM_LIMIT:
                if e.pending:
                    raise RuntimeError("sem rollover with pending ops")
                e.semidx = self.newsem(f"t_{e.name}{e.nsem}")
                e.nsem += 1
                e.tick = 0
            e.tick += 1
            tok = [e.semidx, e.tick, e]
            for p in e.pending:
                p[0] = e.semidx
                p[1] = e.tick
            e.pending = []
            inc = (e.semidx, 1)
            self.live[e.semidx] = e.tick
        else:
            tok = [e.semidx, None, e]
            e.pending.append(tok)
            inc = None
        e.prog.append((waits, fn, inc))
        self._record(tok, reads, writes)

    def dma(self, q, out, in_, reads=(), writes=(), **kw):
        self.dma_multi(q, [(out, in_)], reads, writes, **kw)

    def dma_multi(self, q, pairs, reads=(), writes=(), **kw):
        if self.halted:
            return
        waits = self._deps(q, reads, writes)
        slot = q.ring[q.ring_pos]
        if slot[1] + 16 * len(pairs) > SEM_LIMIT:
            slot = q.ring[q.ring_pos] = [self.newsem(f"r_{q.name}x{len(self.sems)}"), 0]
        q.ring_pos = (q.ring_pos + 1) % len(q.ring)
        s = slot[0]
        if slot[1] > 0 and q.waited.get(s, 0) < slot[1]:
            q.waited[s] = slot[1]
            waits.append((s, slot[1]))
        for i, (o, i_) in enumerate(pairs):
            slot[1] += 16

            def fn(o=o, i_=i_):
                return q.h.dma_start(out=o, in_=i_, **kw)
            q.prog.append((waits if i == 0 else [], fn, (s, 16)))
        self.live[s] = slot[1]
        tok = [s, slot[1], None]
        self._record(tok, reads, writes)

    def barrier(self):
        for e in self.engs:
            if e.pending:
                raise RuntimeError(f"barrier with pending ops on {e.name}")
        for e in self.engs:
            waits = []
            for s, v in self.live.items():
                if e.waited.get(s, 0) < v:
                    e.waited[s] = v
                    waits.append((s, v))
            if waits:
                e.prog.append((waits, None, None))

    def finish(self):
        nc = self.nc
        sems = self.sems
        self.barrier()

        def replay(e):
            def run(h):
                for waits, fn, inc in e.prog:
                    for s, v in waits:
                        h.wait_ge(sems[s], v)
                    if fn is None:
                        continue
                    ins = fn()
                    if inc is not None:
                        ins.then_inc(sems[inc[0]], inc[1])
            return run

        with nc.Block() as block:
            block.tensor(replay(self.pe))
            block.scalar(replay(self.act))
            block.vector(replay(self.dve))
            block.gpsimd(replay(self.pool))
            block.sync(replay(self.sp))


def build(NB, S, dbg=False, stop_after=None, same_sync=True):
    nc = bass.Bass("TRN2", target_bir_lowering=False)
    T = NB * S
    NSB = S // 512
    NST = S // 128
    skind = "ExternalOutput" if dbg else "Internal"

    def din(name, shape):
        return nc.dram_tensor(name, list(shape), F32, kind="ExternalInput").ap()

    x = din("x", [T, DM])
    c = din("c", [NB, DM])
    ada_w = din("ada_w", [NL, DM, 6 * DM])
    ada_b = din("ada_b", [NL, 6 * DM])
    norm_g = din("norm_g", [NL, 2, DM])
    norm_b = din("norm_b", [NL, 2, DM])
    ffn_w_gate = din("ffn_w_gate", [NL, DM, DFF])
    ffn_w_up = din("ffn_w_up", [NL, DM, DFF])
    ffn_w_down = din("ffn_w_down", [NL, DFF, DM])
    ab_w_in = din("ab_w_in", [1, DM, 4096])
    ab_conv_w = din("ab_conv_w", [1, 4, 1024])
    ab_conv_b = din("ab_conv_b", [1, 1024])
    ab_w_r = din("ab_w_r", [1, 8, 128, 128])
    ab_b_r = din("ab_b_r", [1, 8, 128])
    ab_w_i = din("ab_w_i", [1, 8, 128, 128])
    ab_b_i = din("ab_b_i", [1, 8, 128])
    ab_lambda = din("ab_lambda", [1, 1024])
    ab_vnorm_g = din("ab_vnorm_g", [1, 1024])
    ab_vnorm_b = din("ab_vnorm_b", [1, 1024])
    ab_w_s = din("ab_w_s", [1, 8, 128, 128])
    ab_b_s = din("ab_b_s", [1, 8, 128])
    ab_w_out = din("ab_w_out", [1, DM, DM])
    cd_w_in = din("cd_w_in", [1, DM, 4096])
    cd_w_pool = din("cd_w_pool", [1, 4, 256, 256])
    cd_pool_scale = din("cd_pool_scale", [1, 1024])
    cd_w_out = din("cd_w_out", [1, DM, DM])
    out = nc.dram_tensor("out", [T, DM], F32, kind="ExternalOutput").ap()

    xs_d = nc.dram_tensor("xs_d", [T // 512, 128, KC, 512], F32, kind=skind).ap()
    hT_d = nc.dram_tensor("hT_d", [T // 512, 128, KC, 512], BF16, kind=skind).ap()
    yT_d = nc.dram_tensor("yT_d", [T // 512, 128, KC, 512], BF16, kind=skind).ap()
    aT_d = nc.dram_tensor("aT_d", [T // 512, 128, FC, 512], BF16, kind=skind).ap()
    B_xs = [Buf(f"xs{i}") for i in range(T // 512)]
    B_hT = [Buf(f"hT{i}") for i in range(T // 512)]
    B_yT = [[Buf() for _ in range(KC)] for _ in range(NB)]
    B_aT = [[Buf() for _ in range(FC)] for _ in range(NB)]

    k = K(nc, same_sync=same_sync)
    pe, act, dve, pool, sp = None, None, None, None, None

    with ExitStack() as st:
        k.start(st)
        pe, act, dve, pool, sp = k.pe, k.act, k.dve, k.pool, k.sp

        try:
            uniq = [0]

            @contextmanager
            def scope():
                with ExitStack() as s2:
                    class Sc:
                        def sb(self, name, shape, dt=F32):
                            uniq[0] += 1
                            return s2.enter_context(nc.sbuf_tensor(f"{name}_{uniq[0]}", list(shape), dt))

                        def ps(self, name, shape, dt=F32):
                            uniq[0] += 1
                            return s2.enter_context(nc.psum_tensor(f"{name}_{uniq[0]}", list(shape), dt))
                    yield Sc()
                    k.barrier()

            def ACT(out_, in_, func, reads, writes, bias=None, scale=1.0):
                if bias is None:
                    k.op(act, lambda: nc.scalar.activation(out=out_, in_=in_, func=func, scale=scale), reads, writes)
                else:
                    k.op(act, lambda: nc.scalar.activation(out=out_, in_=in_, func=func, bias=bias, scale=scale), reads, writes)

            def MM(out_, lhsT, rhs, start, stop, reads, writes, tick):
                k.op(pe, lambda: nc.tensor.matmul(out_, lhsT=lhsT, rhs=rhs, start=start, stop=stop, skip_group_check=True),
                     reads, writes, tick=tick)

            def TR(out_, in_, ident, reads, writes, tick):
                k.op(pe, lambda: nc.tensor.transpose(out_, in_, ident), reads, writes, tick=tick)

            def TT(e, out_, in0, in1, op_, reads, writes):
                k.op(e, lambda: e.h.tensor_tensor(out=out_, in0=in0, in1=in1, op=op_), reads, writes)

            def TS(e, out_, in0, s1, s2, op0, op1, reads, writes):
                k.op(e, lambda: e.h.tensor_scalar(out=out_, in0=in0, scalar1=s1, scalar2=s2, op0=op0, op1=op1), reads, writes)

            def STT(out_, in0, scalar, in1, op0, op1, reads, writes):
                k.op(dve, lambda: nc.vector.scalar_tensor_tensor(out=out_, in0=in0, scalar=scalar, in1=in1, op0=op0, op1=op1),
                     reads, writes)

            def CP(e, out_, in_, reads, writes):
                if e is act:
                    k.op(e, lambda: nc.scalar.activation(out=out_, in_=in_, func=AF.Copy), reads, writes)
                else:
                    k.op(e, lambda: e.h.tensor_copy(out=out_, in_=in_), reads, writes)

            def SCAN(out_, d0, d1, init, reads, writes):
                k.op(dve, lambda: nc.vector.tensor_tensor_scan(out=out_, data0=d0, data1=d1, initial=init, op0=ALU.mult, op1=ALU.add),
                     reads, writes)

            def RECIP(out_, in_, reads, writes):
                k.op(dve, lambda: nc.vector.reciprocal(out=out_, in_=in_), reads, writes)

            def BNS(out_, in_, reads, writes):
                k.op(dve, lambda: nc.vector.bn_stats(out=out_, in_=in_), reads, writes)

            def BNA(out_, in_, reads, writes):
                k.op(dve, lambda: nc.vector.bn_aggr(out=out_, in_=in_), reads, writes)

            def MEMSET(e, ap, val, writes):
                k.op(e, lambda: e.h.memset(ap, val), [], writes)

            def dma_blk(q, sb_tile, X_d, tb, k0, k1, load, reads, writes, kg=16):
                pairs = []
                for a0 in range(k0, k1, kg):
                    a1 = min(k1, a0 + kg)
                    d = X_d[tb, :, a0:a1, :]
                    s_ = sb_tile[:, a0 - k0:a1 - k0, :]
                    pairs.append((s_, d) if load else (d, s_))
                k.dma_multi(q, pairs, reads, writes)

            def dma_rows(q, X_d, b, ch, sb_row, reads, writes):
                pairs = [(X_d[b * NSB + tbk, :, ch, :], sb_row[:, tbk * 512:(tbk + 1) * 512]) for tbk in range(NSB)]
                k.dma_multi(q, pairs, reads, writes)

            def checkpoint(name):
                if stop_after == name:
                    k.halted = True

            sbp = lambda name, shape, dt=F32: st.enter_context(nc.sbuf_tensor(name, list(shape), dt))
            cols = {}
            ncol = [0]

            def colgrp(name, n=16):
                cols[name] = ncol[0]
                ncol[0] += n
                return cols[name]

            raw_groups = []
            for l in range(NL):
                for i in range(2):
                    raw_groups.append((f"ng{l}{i}", norm_g[l, i], 16))
                    raw_groups.append((f"nb{l}{i}", norm_b[l, i], 16))
            for l in range(NL):
                raw_groups.append((f"adab{l}", ada_b[l], 96))
            for j in range(4):
                raw_groups.append((f"cw{j}", ab_conv_w[0, j], 8))
            raw_groups.append(("cb", ab_conv_b[0], 8))
            raw_groups.append(("br", ab_b_r[0].rearrange("h e -> (h e)"), 8))
            raw_groups.append(("bi", ab_b_i[0].rearrange("h e -> (h e)"), 8))
            raw_groups.append(("lam", ab_lambda[0], 8))
            raw_groups.append(("psc", cd_pool_scale[0], 8))
            for b in range(NB):
                raw_groups.append((f"cT{b}", c[b], 16))
            for nm, _, n_ in raw_groups:
                colgrp(nm, n_)
            NRAW = ncol[0]
            for l in range(NL):
                for i in range(2):
                    colgrp(f"xss{l}{i}")
                    colgrp(f"xsb{l}{i}")
                    for b in range(NB):
                        colgrp(f"hs{l}{i}{b}")
                        colgrp(f"hb{l}{i}{b}")
                for b in range(NB):
                    colgrp(f"g1{l}{b}")
                    colgrp(f"g2{l}{b}")
                    colgrp(f"s1p{l}{b}")
                    colgrp(f"s2p{l}{b}")
            for b in range(NB):
                colgrp(f"hs_in{b}")
            colgrp("lam2", 8)
            colgrp("zero", 1)
            colgrp("eps", 1)
            colgrp("one", 1)
            NTAB = ncol[0]
            tab = sbp("tab", [128, NTAB])
            B_tab = Buf("tab")
            modr = sbp("modr", [128, NL, NB, 96])
            B_modr = Buf("modr")
            ident = sbp("ident", [128, 128])
            ident_bf = sbp("ident_bf", [128, 128], BF16)
            ones_f = sbp("ones_f", [128, 128])
            B_const = Buf("const")

            def col(name, i=0, n=1):
                c0 = cols[name] + i
                return tab[:, c0:c0 + n]

            MEMSET(pool, ident[:], 1.0, [B_const])
            k.op(pool, lambda: nc.gpsimd.affine_select(out=ident[:], in_=ident[:], compare_op=ALU.is_equal, fill=0.0,
                                                       base=0, pattern=[[-1, 128]], channel_multiplier=1), [B_const], [B_const])
            CP(pool, ident_bf[:], ident[:], [B_const], [B_const])
            MEMSET(pool, ones_f[:], 1.0, [B_const])
            MEMSET(dve, col("zero"), 0.0, [B_tab])
            MEMSET(dve, col("eps"), EPS, [B_tab])

            MEMSET(dve, col("one"), 1.0, [B_tab])
            with scope() as sc:
                nst_ = (NRAW + 127) // 128
                stg = [sc.sb(f"stg{j}", [128, 128]) for j in range(nst_)]
                B_stg = [Buf() for _ in range(nst_)]
                tps_ = sc.ps("tps_", [128, 128])
                B_tps_ = Buf()
                for j in range(nst_):
                    MEMSET(dve, stg[j][:], 0.0, [B_stg[j]])
                for nm, vec, n_ in raw_groups:
                    c0 = cols[nm]
                    r = 0
                    while r < n_:
                        j, p0 = divmod(c0 + r, 128)
                        m_ = min(n_ - r, 128 - p0)
                        k.dma(sp, stg[j][p0:p0 + m_, :], vec.rearrange("(c p) -> c p", p=128)[r:r + m_, :], writes=[B_stg[j]])
                        r += m_
                for j in range(nst_):
                    w_ = min(128, NRAW - j * 128)
                    TR(tps_[:], stg[j][:], ident[:], [B_stg[j], B_const], [B_tps_], tick=True)
                    CP(dve, tab[:, j * 128:j * 128 + w_], tps_[:, 0:w_], [B_tps_], [B_tab])
            checkpoint("pro_a")
            ACT(col("lam2", 0, 8), col("lam", 0, 8), AF.Exp, [B_tab], [B_tab], scale=-1.0)
            ACT(col("lam2", 0, 8), col("lam2", 0, 8), AF.Ln, [B_tab], [B_tab], bias=col("one"))
            TS(dve, col("lam", 0, 8), col("lam2", 0, 8), -8.0, None, ALU.mult, ALU.bypass, [B_tab], [B_tab])
            TS(dve, col("lam2", 0, 8), col("lam2", 0, 8), -16.0, None, ALU.mult, ALU.bypass, [B_tab], [B_tab])

            checkpoint("pro_b")
            with scope() as sc:
                cT = sc.sb("cT", [128, KC, NB])
                B_cT = Buf("cT")
                for b in range(NB):
                    ACT(cT[:, :, b], col(f"cT{b}", 0, 16), AF.Silu, [B_tab], [B_cT])
                CBW = 3072
                aw = [sc.sb(f"aw{i}", [128, CBW]) for i in range(2)]
                B_aw = [Buf(f"aw{i}") for i in range(2)]
                mp = [sc.ps(f"mp{l}", [128, 96 * NB]) for l in range(NL)]
                B_mp = [Buf(f"mp{l}") for l in range(NL)]
                n = 0
                for l in range(NL):
                    for kk in range(KC):
                        for cb in range(6 * DM // CBW):
                            t = aw[n % 2]
                            bt = B_aw[n % 2]
                            n += 1
                            k.dma(sp if n % 2 else act, t[:], ada_w[l, kk * 128:(kk + 1) * 128, cb * CBW:(cb + 1) * CBW], writes=[bt])
                            for jj in range(CBW // 128):
                                j = cb * (CBW // 128) + jj
                                last = (kk == KC - 1) and (j == 95)
                                MM(mp[l][:, j * NB:(j + 1) * NB], t[:, jj * 128:(jj + 1) * 128], cT[:, kk, :],
                                   start=(kk == 0 and j == 0), stop=(kk == KC - 1), reads=[bt, B_cT], writes=[B_mp[l]],
                                   tick=(jj == CBW // 128 - 1))
                    for b in range(NB):
                        TT(dve, modr[:, l, b, :], mp[l][:].rearrange("p (j b) -> p j b", b=NB)[:, :, b],
                           col(f"adab{l}", 0, 96), ALU.add, [B_mp[l], B_tab], [B_modr])
                checkpoint("pro_c")
                for l in range(NL):
                    for b in range(NB):
                        TS(dve, col(f"g1{l}{b}", 0, 16), modr[:, l, b, 32:48], 1.0, None, ALU.add, ALU.bypass, [B_modr], [B_tab])
                        TS(dve, col(f"g2{l}{b}", 0, 16), modr[:, l, b, 80:96], 1.0, None, ALU.add, ALU.bypass, [B_modr], [B_tab])
                        TS(dve, col(f"s1p{l}{b}", 0, 16), modr[:, l, b, 16:32], 1.0, None, ALU.add, ALU.bypass, [B_modr], [B_tab])
                        TS(dve, col(f"s2p{l}{b}", 0, 16), modr[:, l, b, 64:80], 1.0, None, ALU.add, ALU.bypass, [B_modr], [B_tab])
                for l in range(NL):
                    for i in range(2):
                        g_ = col(f"ng{l}{i}", 0, 16)
                        b_ = col(f"nb{l}{i}", 0, 16)
                        TS(dve, col(f"xss{l}{i}", 0, 16), g_, ALPHA, None, ALU.mult, ALU.bypass, [B_tab], [B_tab])
                        TS(dve, col(f"xsb{l}{i}", 0, 16), b_, ALPHA, None, ALU.mult, ALU.bypass, [B_tab], [B_tab])
                        if i == 1 and l == NL - 1:
                            continue
                        for b in range(NB):
                            if i == 0:
                                sp_ = col(f"s2p{l}{b}", 0, 16)
                                sh_ = modr[:, l, b, 48:64]
                            else:
                                sp_ = col(f"s1p{l + 1}{b}", 0, 16)
                                sh_ = modr[:, l + 1, b, 0:16]
                            TT(dve, col(f"hs{l}{i}{b}", 0, 16), g_, sp_, ALU.mult, [B_tab], [B_tab])
                            TT(dve, col(f"hb{l}{i}{b}", 0, 16), b_, sp_, ALU.mult, [B_tab], [B_tab])
                            TT(dve, col(f"hb{l}{i}{b}", 0, 16), col(f"hb{l}{i}{b}", 0, 16), sh_, ALU.add, [B_tab, B_modr], [B_tab])

            for b in range(NB):
                TS(dve, col(f"hs_in{b}", 0, 16), col(f"s1p0{b}", 0, 16), 1.0 / ALPHA, None, ALU.mult, ALU.bypass, [B_tab], [B_tab])
            checkpoint("pro_d")
            with scope() as sc:
                xt = [sc.sb(f"xt{i}", [128, DM]) for i in range(2)]
                B_xt = [Buf() for _ in range(2)]
                xsb = sc.sb("xsb", [128, KC, 512])
                hb = sc.sb("hb", [128, KC, 512], F32 if "hbf32" in XTF else BF16)
                B_xsb = [[Buf() for _ in range(4)] for _ in range(KC)]
                B_hb = [[Buf() for _ in range(4)] for _ in range(KC)]
                allx = [B_xsb[cc_][q_] for cc_ in range(KC) for q_ in range(4)]
                allh = [B_hb[cc_][q_] for cc_ in range(KC) for q_ in range(4)]
                tp = [sc.ps(f"tp{i}", [128, 4, 128]) for i in range(4)]
                B_tp = [Buf() for _ in range(4)]
                n = 0
                for tb in range(T // 512):
                    b = tb // NSB
                    for q in range(4):
                        tt = tb * 4 + q
                        t, bt = xt[tt % 2], B_xt[tt % 2]
                        k.dma(sp, t[:], x[tt * 128:(tt + 1) * 128, :], writes=[bt])
                        for c4 in range(4):
                            p_, bp = tp[n % 4], B_tp[n % 4]
                            n += 1
                            for i in range(4):
                                cc = c4 * 4 + i
                                TR(p_[:, i, :], t[:, cc * 128:(cc + 1) * 128], ident[:], [bt, B_const], [bp], tick=(i == 3))
                            ACT(xsb[:, c4 * 4:(c4 + 1) * 4, q * 128:(q + 1) * 128], p_[:], AF.Copy, [bp], [B_xsb[c4 * 4 + i_][q] for i_ in range(4)], scale=ALPHA)
                            for i in range(4):
                                cc = c4 * 4 + i
                                TS(dve, hb[:, cc, q * 128:(q + 1) * 128], xsb[:, cc, q * 128:(q + 1) * 128], col(f"hs_in{b}", cc), None,
                                   ALU.mult, ALU.bypass, [B_xsb[cc][q], B_tab], [B_hb[cc][q]])
                                TS(dve, hb[:, cc, q * 128:(q + 1) * 128], hb[:, cc, q * 128:(q + 1) * 128], modr[:, 0, b, cc:cc + 1], None,
                                   ALU.add, ALU.bypass, [B_hb[cc][q], B_modr], [B_hb[cc][q]])
                    if "nost" not in XTF:
                        dma_blk(sp, xsb, xs_d, tb, 0, KC, False, allx, [B_xs[tb]])
                        dma_blk(act, hb, hT_d, tb, 0, KC, False, allh, [B_hT[tb]])

            def load_w_bf(dst, src, nk, kstep, breads, bw):
                v = src.rearrange("(k p) n -> p k n", p=128)
                for k0 in range(0, nk, kstep):
                    k1 = min(nk, k0 + kstep)
                    k.dma(pool, dst[:, k0:k1, :], v[:, k0:k1, :], reads=breads, writes=[bw])

            def gemm_res_ln(name, W, nk, src_d, B_src, l, i, final):
                ND = 2 if nk <= 16 else 4
                CPD = KC // ND
                gname = "g1" if i == 0 else "g2"
                with scope() as so:
                    ssum = so.sb("ssum", [128, T])
                    ssq = so.sb("ssq", [128, T])
                    B_st = [Buf() for _ in range(T // 512)]
                    with scope() as sc:
                        NWB = 2 if nk <= 16 else 1
                        Wsbs = [sc.sb(f"Wsb{j}", [128, nk, CPD * 128], BF16) for j in range(NWB)]
                        B_Ws = [Buf("W") for _ in range(NWB)]
                        ab = [sc.sb(f"ab{j}", [128, nk, 512], BF16) for j in range(2)]
                        B_ab = [Buf() for _ in range(2)]
                        rb = [sc.sb(f"rb{j}", [128, CPD, 512]) for j in range(2)]
                        B_rb = [[Buf() for _ in range(CPD)] for _ in range(2)]
                        sq = [sc.sb(f"sq{j}", [128, 512]) for j in range(3)]
                        B_sq = [Buf() for _ in range(3)]
                        acc = [sc.ps(f"acc{j}", [128, 512]) for j in range(3)]
                        B_acc = [Buf() for _ in range(3)]
                        pss = [sc.ps(f"pss{j}", [128, 512]) for j in range(2)]
                        psq = [sc.ps(f"psq{j}", [128, 512]) for j in range(2)]
                        B_pst = [Buf() for _ in range(2)]
                        nacc = nst = 0
                        items = [(dq, tb) for dq in range(ND) for tb in range(T // 512)]

                        def issue_loads(n):
                            dq, tb = items[n]
                            if tb == 0 and (NWB == 2 or dq == 0):
                                load_w_bf(Wsbs[dq % NWB], W[:, dq * CPD * 128:(dq + 1) * CPD * 128], nk, 4, [], B_Ws[dq % NWB])
                            b_ = tb // NSB
                            srcb = [B_src[b_][kk] for kk in range(nk)]
                            dma_blk(sp, ab[n % 2], src_d, tb, 0, nk, True, srcb, [B_ab[n % 2]])
                            dma_blk(sp, rb[n % 2], xs_d, tb, dq * CPD, (dq + 1) * CPD, True, [B_xs[tb]], B_rb[n % 2])

                        issue_loads(0)
                        for n, (dq, tb) in enumerate(items):
                            if n + 1 < len(items):
                                issue_loads(n + 1)
                            Wsb, B_W = Wsbs[dq % NWB], B_Ws[dq % NWB]
                            b = tb // NSB
                            a_, ba = ab[n % 2], B_ab[n % 2]
                            r_, br_ = rb[n % 2], B_rb[n % 2]
                            tsl = slice(tb * 512, (tb + 1) * 512)
                            ps_, pq_, bst = pss[nst % 2], psq[nst % 2], B_pst[nst % 2]
                            nst += 1
                            prev = None
                            for c8 in range(CPD + 1):
                                if c8 < CPD:
                                    cc = dq * CPD + c8
                                    ac, bac = acc[nacc % 3], B_acc[nacc % 3]
                                    s_, bs_ = sq[nacc % 3], B_sq[nacc % 3]
                                    nacc += 1
                                    for kk in range(nk):
                                        MM(ac[:], Wsb[:, kk, c8 * 128:(c8 + 1) * 128], a_[:, kk, :], kk == 0, kk == nk - 1,
                                           [B_W, ba], [bac], tick=(kk == nk - 1))
                                if prev is not None:
                                    pc8, ps_sq, pbs = prev
                                    MM(ps_[:], ones_f[:], r_[:, pc8, :], pc8 == 0, pc8 == CPD - 1, [B_const, br_[pc8]], [bst], tick=False)
                                    MM(pq_[:], ones_f[:], ps_sq[:], pc8 == 0, pc8 == CPD - 1, [B_const, pbs], [bst], tick=True)
                                if c8 < CPD:
                                    STT(r_[:, c8, :], ac[:], col(f"{gname}{l}{b}", cc), r_[:, c8, :], ALU.mult, ALU.add,
                                        [bac, br_[c8], B_tab], [br_[c8]])
                                    ACT(s_[:], r_[:, c8, :], AF.Square, [br_[c8]], [bs_])
                                    prev = (c8, s_, bs_)
                            if dq == 0:
                                CP(act, ssum[:, tsl], ps_[:], [bst], [B_st[tb]])
                                CP(dve, ssq[:, tsl], pq_[:], [bst], [B_st[tb]])
                            else:
                                TT(dve, ssum[:, tsl], ssum[:, tsl], ps_[:], ALU.add, [bst, B_st[tb]], [B_st[tb]])
                                TT(dve, ssq[:, tsl], ssq[:, tsl], pq_[:], ALU.add, [bst, B_st[tb]], [B_st[tb]])
                            dma_blk(sp, r_, xs_d, tb, dq * CPD, (dq + 1) * CPD, False, br_, [B_xs[tb]])
                            if NWB == 1 and n + 1 < len(items) and items[n + 1][1] == 0:
                                dq2 = items[n + 1][0]
                                load_w_bf(Wsbs[0], W[:, dq2 * CPD * 128:(dq2 + 1) * CPD * 128], nk, 4, [], B_Ws[0])
                    with scope() as sc:
                        r16s = [sc.sb(f"r16{j}", [128, KC, 512]) for j in range(2)]
                        B_r16s = [[Buf() for _ in range(KC)] for _ in range(2)]
                        h16s = [sc.sb(f"h16{j}", [128, KC, 512], BF16) for j in range(2)]
                        B_h16s = [[Buf() for _ in range(KC)] for _ in range(2)]
                        mt = sc.sb("mt", [128, 512])
                        rstd = sc.sb("rstd", [128, 512])
                        nmr = sc.sb("nmr", [128, 512])
                        B_ms = Buf()
                        if final:
                            otok = [sc.sb(f"otok{j}", [128, DM]) for j in range(2)]
                            B_ot = [Buf() for _ in range(2)]
                            tpo = [sc.ps(f"tpo{j}", [128, 4, 128]) for j in range(2)]
                            B_tpo = [Buf() for _ in range(2)]
                        dma_blk(sp, r16s[0], xs_d, 0, 0, KC, True, [B_xs[0]], B_r16s[0])
                        for tb in range(T // 512):
                            b = tb // NSB
                            tsl = slice(tb * 512, (tb + 1) * 512)
                            r16, B_r16 = r16s[tb % 2], B_r16s[tb % 2]
                            h16, B_h16 = h16s[tb % 2], B_h16s[tb % 2]
                            if tb + 1 < T // 512:
                                dma_blk(sp, r16s[(tb + 1) % 2], xs_d, tb + 1, 0, KC, True, [B_xs[tb + 1]], B_r16s[(tb + 1) % 2])
                            TS(dve, mt[:], ssum[:, tsl], 1.0 / DM, None, ALU.mult, ALU.bypass, [B_st[tb]], [B_ms])
                            TT(dve, nmr[:], mt[:], mt[:], ALU.mult, [B_ms], [B_ms])
                            STT(rstd[:], ssq[:, tsl], 1.0 / DM, nmr[:], ALU.mult, ALU.subtract, [B_st[tb], B_ms], [B_ms])
                            ACT(rstd[:], rstd[:], AF.Sqrt, [B_ms, B_tab], [B_ms], bias=col("eps"))
                            RECIP(rstd[:], rstd[:], [B_ms], [B_ms])
                            STT(nmr[:], mt[:], -1.0, rstd[:], ALU.mult, ALU.mult, [B_ms], [B_ms])
                            for cc in range(KC):
                                rc = r16[:, cc, :]
                                TT(dve, rc, rc, rstd[:], ALU.mult, [B_r16[cc], B_ms], [B_r16[cc]])
                                TT(pool, rc, rc, nmr[:], ALU.add, [B_r16[cc], B_ms], [B_r16[cc]])
                                if final:
                                    ACT(rc, rc, AF.Identity, [B_r16[cc], B_tab], [B_r16[cc]], bias=col(f"nb{l}{i}", cc), scale=col(f"ng{l}{i}", cc))
                                else:
                                    ACT(h16[:, cc, :], rc, AF.Identity, [B_r16[cc], B_tab], [B_h16[cc]],
                                        bias=col(f"hb{l}{i}{b}", cc), scale=col(f"hs{l}{i}{b}", cc))
                                    ACT(rc, rc, AF.Identity, [B_r16[cc], B_tab], [B_r16[cc]], bias=col(f"xsb{l}{i}", cc), scale=col(f"xss{l}{i}", cc))
                            if final:
                                for q in range(4):
                                    tt = tb * 4 + q
                                    ot, bot = otok[tt % 2], B_ot[tt % 2]
                                    for c4 in range(4):
                                        p_, bp = tpo[c4 % 2], B_tpo[c4 % 2]
                                        for ii in range(4):
                                            cc = c4 * 4 + ii
                                            TR(p_[:, ii, :], r16[:, cc, q * 128:(q + 1) * 128], ident[:], [B_r16[cc], B_const], [bp], tick=(ii == 3))
                                        CP(act if c4 % 2 else dve, ot[:, c4 * 512:(c4 + 1) * 512], p_[:].rearrange("p a b -> p (a b)"), [bp], [bot])
                                    k.dma(sp, out[tt * 128:(tt + 1) * 128, :], ot[:], reads=[bot])
                            else:
                                dma_blk(sp, r16, xs_d, tb, 0, KC, False, B_r16, [B_xs[tb]])
                                dma_blk(sp, h16, hT_d, tb, 0, KC, False, B_h16, [B_hT[tb]])

            def ffn1(l):
                JG = 4
                with scope() as sc:
                    hseq = sc.sb("hseq", [128, NSB, KC, 512], BF16)
                    B_hs = Buf()
                    wg = [sc.sb(f"wg{j}", [128, KC, JG * 128], BF16) for j in range(2)]
                    wu = [sc.sb(f"wu{j}", [128, KC, JG * 128], BF16) for j in range(2)]
                    B_wg = [Buf() for _ in range(2)]
                    sg = [sc.sb(f"sg{j}", [128, 512]) for j in range(2)]
                    B_sg = [Buf() for _ in range(2)]
                    aj = [sc.sb(f"aj{j}", [128, S], BF16) for j in range(2)]
                    B_aj = [Buf() for _ in range(2)]
                    pg = [sc.ps(f"pg{j}", [128, 512]) for j in range(2)]
                    pu = [sc.ps(f"pu{j}", [128, 512]) for j in range(2)]
                    B_pg = [Buf() for _ in range(2)]
                    B_pu = [Buf() for _ in range(2)]
                    nw = nj = np_ = 0
                    for b in range(NB):
                        for tbk in range(NSB):
                            tb = b * NSB + tbk
                            dma_blk(sp, hseq[:, tbk], hT_d, tb, 0, KC, True, [B_hT[tb]], [B_hs])
                        for jg in range(FC // JG):
                            wg_, wu_, bw = wg[nw % 2], wu[nw % 2], B_wg[nw % 2]
                            nw += 1
                            csl = slice(jg * JG * 128, (jg + 1) * JG * 128)
                            load_w_bf(wg_, ffn_w_gate[l][:, csl], KC, 8, [], bw)
                            load_w_bf(wu_, ffn_w_up[l][:, csl], KC, 8, [], bw)
                            for jj in range(JG):
                                j = jg * JG + jj
                                a_, ba = aj[nj % 2], B_aj[nj % 2]
                                nj += 1
                                for tbk in range(NSB):
                                    g_, bg = pg[np_ % 2], B_pg[np_ % 2]
                                    u_, bu = pu[np_ % 2], B_pu[np_ % 2]
                                    s_, bs_ = sg[np_ % 2], B_sg[np_ % 2]
                                    np_ += 1
                                    tsl = slice(tbk * 512, (tbk + 1) * 512)
                                    for kk in range(KC):
                                        MM(g_[:], wg_[:, kk, jj * 128:(jj + 1) * 128], hseq[:, tbk, kk, :], kk == 0, kk == KC - 1,
                                           [bw, B_hs], [bg], tick=(kk == KC - 1))
                                    for kk in range(KC):
                                        MM(u_[:], wu_[:, kk, jj * 128:(jj + 1) * 128], hseq[:, tbk, kk, :], kk == 0, kk == KC - 1,
                                           [bw, B_hs], [bu], tick=(kk == KC - 1))
                                    ACT(s_[:], g_[:], AF.Silu, [bg], [bs_])
                                    TT(dve, a_[:, tsl], s_[:], u_[:], ALU.mult, [bs_, bu], [ba])
                                dma_rows(sp, aT_d, b, j, a_, [ba], [B_aT[b][j]])

            def mixer_ab():
                w_in = ab_w_in[0]
                with scope() as sc:
                    hseq = sc.sb("hseq", [128, NSB, KC, 512], BF16)
                    B_hs = Buf()
                    wr = sc.sb("wr", [128, 8, 128], BF16)
                    wi = sc.sb("wi", [128, 8, 128], BF16)
                    wsT = sc.sb("wsT", [128, 8, 128], BF16)
                    bsrow = sc.sb("bsrow", [1, 1024], BF16)
                    onesrow = sc.sb("onesrow", [1, 128], BF16)
                    vng = sc.sb("vng", [128, 1024])
                    vnb = sc.sb("vnb", [128, 1024])
                    B_c0 = Buf("l0const")
                    k.dma(pool, wr[:], ab_w_r[0].rearrange("h d e -> d h e"), writes=[B_c0])
                    k.dma(pool, wi[:], ab_w_i[0].rearrange("h d e -> d h e"), writes=[B_c0])
                    k.dma(pool, bsrow[:], ab_b_s[0].rearrange("g t -> (g t)").rearrange("(o n) -> o n", o=1), writes=[B_c0])
                    MEMSET(dve, onesrow[:], 1.0, [B_c0])
                    k.dma(sp, vng[:], ab_vnorm_g[0].partition_broadcast(128), writes=[B_c0])
                    k.dma(sp, vnb[:], ab_vnorm_b[0].partition_broadcast(128), writes=[B_c0])
                    with scope() as s2:
                        wsf = s2.sb("wsf", [128, 8, 128])
                        B_wsf = Buf()
                        tps = s2.ps("tps", [128, 4, 128])
                        B_tps = Buf()
                        k.dma(sp, wsf[:], ab_w_s[0].rearrange("g t s -> t g s"), writes=[B_wsf])
                        for g4 in range(2):
                            for ii in range(4):
                                TR(tps[:, ii, :], wsf[:, g4 * 4 + ii, :], ident[:], [B_wsf, B_const], [B_tps], tick=(ii == 3))
                            CP(dve, wsT[:, g4 * 4:(g4 + 1) * 4, :], tps[:], [B_tps], [B_c0])
                        MEMSET(dve, wsT[64:128, :, 0:64], 0.0, [B_c0])

                    for b in range(NB):
                        for tbk in range(NSB):
                            tb = b * NSB + tbk
                            dma_blk(sp, hseq[:, tbk], hT_d, tb, 0, KC, True, [B_hT[tb]], [B_hs])
                        with scope() as s2:
                            wxa = [s2.sb(f"wxa{j}", [128, KC, 128], BF16) for j in range(2)]
                            wga = [s2.sb(f"wga{j}", [128, KC, 128], BF16) for j in range(2)]
                            B_wx = [Buf() for _ in range(2)]
                            xa = [s2.sb(f"xa{j}", [128, 3 + S]) for j in range(2)]
                            gg = [s2.sb(f"gg{j}", [128, S], BF16) for j in range(2)]
                            xc = [s2.sb(f"xc{j}", [128, S]) for j in range(2)]
                            xcb = [s2.sb(f"xcb{j}", [128, S], BF16) for j in range(2)]
                            rr = [s2.sb(f"rr{j}", [128, S]) for j in range(2)]
                            iu = [s2.sb(f"iu{j}", [128, S]) for j in range(2)]
                            aa = [s2.sb(f"aa{j}", [128, S]) for j in range(2)]
                            ya = [s2.sb(f"ya{j}", [128, S], BF16) for j in range(2)]
                            B_xa, B_gg, B_xc, B_xcb, B_rr, B_iu, B_aa = ([Buf() for _ in range(2)] for _ in range(7))
                            B_ya = [Buf() for _ in range(2)]
                            pxa = [s2.ps(f"pxa{j}", [128, 512]) for j in range(2)]
                            pga = [s2.ps(f"pga{j}", [128, 512]) for j in range(2)]
                            pr = [s2.ps(f"pr{j}", [128, 512]) for j in range(2)]
                            pi = [s2.ps(f"pi{j}", [128, 512]) for j in range(2)]
                            B_pxa, B_pga, B_pr, B_pi = ([Buf() for _ in range(2)] for _ in range(4))
                            for j in range(2):
                                MEMSET(dve, xa[j][:, 0:3], 0.0, [B_xa[j]])
                            cntl = {"px": 0, "pg": 0}

                            def stP(h):
                                st_ = h % 2
                                wx_, wg_, bw = wxa[st_], wga[st_], B_wx[st_]
                                load_w_bf(wx_, w_in[:, h * 128:(h + 1) * 128], KC, 16, [], bw)
                                load_w_bf(wg_, w_in[:, 1024 + h * 128:1024 + (h + 1) * 128], KC, 16, [], bw)
                                for tbk in range(NSB):
                                    tsl = slice(tbk * 512, (tbk + 1) * 512)
                                    j_ = cntl["px"] % 2
                                    cntl["px"] += 1
                                    p1, b1 = pxa[j_], B_pxa[j_]
                                    p2, b2 = pga[j_], B_pga[j_]
                                    for kk in range(KC):
                                        MM(p1[:], wx_[:, kk, :], hseq[:, tbk, kk, :], kk == 0, kk == KC - 1, [bw, B_hs], [b1], tick=(kk == KC - 1))
                                    for kk in range(KC):
                                        MM(p2[:], wg_[:, kk, :], hseq[:, tbk, kk, :], kk == 0, kk == KC - 1, [bw, B_hs], [b2], tick=(kk == KC - 1))
                                    ACT(xa[st_][:, 3 + tbk * 512:3 + (tbk + 1) * 512], p1[:], AF.Copy, [b1], [B_xa[st_]])
                                    ACT(gg[st_][:, tsl], p2[:], AF.Gelu_apprx_tanh, [b2], [B_gg[st_]])

                            def stE(h):
                                st_ = h % 2
                                xa_, gg_, xc_, xcb_, rr_, iu_, aa_ = xa[st_], gg[st_], xc[st_], xcb[st_], rr[st_], iu[st_], aa[st_]
                                bxa, bgg, bxc, bxcb, brr, biu, baa = B_xa[st_], B_gg[st_], B_xc[st_], B_xcb[st_], B_rr[st_], B_iu[st_], B_aa[st_]
                                ACT(xc_[:], xa_[:, 3:3 + S], AF.Identity, [bxa, B_tab], [bxc], scale=col("cw3", h), bias=col("cb", h))
                                for j in (2, 1, 0):
                                    STT(xc_[:], xa_[:, j:j + S], col(f"cw{j}", h), xc_[:], ALU.mult, ALU.add, [bxa, bxc, B_tab], [bxc])
                                CP(pool, xcb_[:], xc_[:], [bxc], [bxcb])
                                for tbk in range(NSB):
                                    tsl = slice(tbk * 512, (tbk + 1) * 512)
                                    j_ = cntl["pg"] % 2
                                    cntl["pg"] += 1
                                    p1, b1 = pr[j_], B_pr[j_]
                                    p2, b2 = pi[j_], B_pi[j_]
                                    MM(p1[:], wr[:, h, :], xcb_[:, tsl], True, True, [B_c0, bxcb], [b1], tick=True)
                                    MM(p2[:], wi[:, h, :], xcb_[:, tsl], True, True, [B_c0, bxcb], [b2], tick=True)
                                    ACT(rr_[:, tsl], p1[:], AF.Sigmoid, [b1, B_tab], [brr], bias=col("br", h))
                                    ACT(iu_[:, tsl], p2[:], AF.Sigmoid, [b2, B_tab], [biu], bias=col("bi", h))
                                ACT(aa_[:], rr_[:], AF.Exp, [brr, B_tab], [baa], scale=col("lam", h))
                                ACT(rr_[:], rr_[:], AF.Exp, [brr, B_tab], [brr], scale=col("lam2", h))
                                ACT(rr_[:], rr_[:], AF.Sqrt, [brr, B_tab], [brr], scale=-1.0, bias=col("one"))
                                TT(dve, iu_[:], iu_[:], xc_[:], ALU.mult, [biu, bxc], [biu])
                                TT(dve, iu_[:], iu_[:], rr_[:], ALU.mult, [biu, brr], [biu])
                                SCAN(xa_[:, 3:3 + S], aa_[:], iu_[:], col("zero"), [baa, biu, B_tab], [bxa])
                                y_, by = ya[st_], B_ya[st_]
                                TT(dve, y_[:], xa_[:, 3:3 + S], gg_[:], ALU.mult, [bxa, bgg], [by])
                                dma_rows(sp, yT_d, b, h, y_, [by], [B_yT[b][h]])

                            stP(0)
                            for h in range(8):
                                if h + 1 < 8:
                                    stP(h + 1)
                                stE(h)
                        with scope() as s2:
                            wub = s2.sb("wub", [128, KC, 1024], BF16)
                            wvb = s2.sb("wvb", [128, KC, 1024], BF16)
                            B_wb = Buf()
                            load_w_bf(wub, w_in[:, 2048:3072], KC, 4, [], B_wb)
                            load_w_bf(wvb, w_in[:, 3072:4096], KC, 4, [], B_wb)
                            ug = s2.sb("ug", [128, 8, 512])
                            B_ug = Buf()
                            vf = [s2.sb(f"vf{j}", [128, 1024]) for j in range(2)]
                            vt = [s2.sb(f"vt{j}", [128, 1024], BF16) for j in range(2)]
                            B_vf = [Buf() for _ in range(2)]
                            B_vt = [Buf() for _ in range(2)]
                            st6 = s2.sb("st6", [128, 2, 6])
                            mv = s2.sb("mv", [128, 2])
                            B_mv = Buf()
                            yb = [s2.sb(f"yb{j}", [128, 8, 512], BF16) for j in range(2)]
                            B_yb = [Buf() for _ in range(2)]
                            pu = [s2.ps(f"pu{j}", [128, 512]) for j in range(2)]
                            B_pu = [Buf() for _ in range(2)]
                            pv = [s2.ps(f"pv{j}", [128, 1024]) for j in range(2)]
                            B_pv = [Buf() for _ in range(2)]
                            psv = s2.ps("psv", [128, 8, 128])
                            B_psv = Buf()
                            npu = 0
                            for tbk in range(NSB):
                                tsl = slice(tbk * 512, (tbk + 1) * 512)
                                y_, by = yb[tbk % 2], B_yb[tbk % 2]
                                for g in range(8):
                                    p_, bp = pu[npu % 2], B_pu[npu % 2]
                                    npu += 1
                                    for kk in range(KC):
                                        MM(p_[:], wub[:, kk, g * 128:(g + 1) * 128], hseq[:, tbk, kk, :], kk == 0, kk == KC - 1,
                                           [B_wb, B_hs], [bp], tick=(kk == KC - 1))
                                    ACT(ug[:, g, :], p_[:], AF.Gelu_apprx_tanh, [bp], [B_ug])
                                def vmm(tt_):
                                    p__, bp__ = pv[tt_ % 2], B_pv[tt_ % 2]
                                    for hf in range(2):
                                        for kk in range(KC):
                                            MM(p__[:, hf * 512:(hf + 1) * 512], hseq[:, tt_ // 4, kk, (tt_ % 4) * 128:(tt_ % 4 + 1) * 128],
                                               wvb[:, kk, hf * 512:(hf + 1) * 512], kk == 0, kk == KC - 1, [B_wb, B_hs], [bp__],
                                               tick=(kk == KC - 1))

                                def stG(tt_):
                                    p__, bp__ = pv[tt_ % 2], B_pv[tt_ % 2]
                                    v__, bv__ = vf[tt_ % 2], B_vf[tt_ % 2]
                                    for hf in range(2):
                                        ACT(v__[:, hf * 512:(hf + 1) * 512], p__[:, hf * 512:(hf + 1) * 512], AF.Gelu_apprx_tanh, [bp__], [bv__])

                                def stN(tt_):
                                    v__, bv__ = vf[tt_ % 2], B_vf[tt_ % 2]
                                    vt__, bvt__ = vt[tt_ % 2], B_vt[tt_ % 2]
                                    for hf in range(2):
                                        BNS(st6[:, hf, :], v__[:, hf * 512:(hf + 1) * 512], [bv__], [B_mv])
                                    BNA(mv[:], st6[:].rearrange("p a b -> p (a b)"), [B_mv], [B_mv])
                                    ACT(mv[:, 1:2], mv[:, 1:2], AF.Sqrt, [B_mv, B_tab], [B_mv], bias=col("eps"))
                                    RECIP(mv[:, 1:2], mv[:, 1:2], [B_mv], [B_mv])
                                    TS(dve, v__[:], v__[:], mv[:, 0:1], None, ALU.subtract, ALU.bypass, [bv__, B_mv], [bv__])
                                    TS(dve, v__[:], v__[:], mv[:, 1:2], None, ALU.mult, ALU.bypass, [bv__, B_mv], [bv__])
                                    TT(pool, v__[:], v__[:], vng[:], ALU.mult, [bv__, B_c0], [bv__])
                                    TT(pool, vt__[:], v__[:], vnb[:], ALU.add, [bv__, B_c0], [bvt__])

                                if tbk == 0:
                                    vmm(0)
                                    if NST > 1:
                                        vmm(1)
                                    stG(0)
                                    stN(0)
                                for q in range(4):
                                    tt = tbk * 4 + q
                                    vt_, bvt = vt[tt % 2], B_vt[tt % 2]
                                    if tt + 1 < NST:
                                        stG(tt + 1)
                                    if tt + 2 < NST:
                                        vmm(tt + 2)
                                    if tt + 1 < NST:
                                        stN(tt + 1)
                                    for g in range(8):
                                        MM(psv[:, g, :], vt_[:, g * 128:(g + 1) * 128], wsT[:, g, :], True, False, [bvt, B_c0], [B_psv], tick=False)
                                        MM(psv[:, g, :], onesrow[0:1, :], bsrow[0:1, g * 128:(g + 1) * 128], False, True, [B_c0], [B_psv],
                                           tick=(g == 7))
                                    TT(dve, y_[:, :, q * 128:(q + 1) * 128], psv[:], ug[:, :, q * 128:(q + 1) * 128], ALU.mult,
                                       [B_psv, B_ug], [by])
                                dma_blk(sp, y_, yT_d, b * NSB + tbk, 8, 16, False, [by], [B_yT[b][8 + g] for g in range(8)])

            def mixer_cd():
                w_in = cd_w_in[0]
                zscale = 128 ** -0.5
                with scope() as sc:
                    hseq = sc.sb("hseq", [128, NSB, KC, 512], BF16)
                    B_hs = Buf()
                    wpl = sc.sb("wpl", [128, 4, 2, 256], BF16)
                    tri = sc.sb("tri", [128, 128])
                    ones_s = sc.sb("ones_s", [128, S], BF16)
                    invc = sc.sb("invc", [128, 16])
                    B_c1 = Buf("l1const")
                    k.dma(pool, wpl[:], cd_w_pool[0].rearrange("g (dc p) e -> p g dc e", p=128), writes=[B_c1])
                    MEMSET(pool, tri[:], 1.0, [B_c1])
                    k.op(pool, lambda: nc.gpsimd.affine_select(out=tri[:], in_=tri[:], compare_op=ALU.is_gt, fill=0.0,
                                                               base=0, pattern=[[-1, 128]], channel_multiplier=1), [B_c1], [B_c1])
                    MEMSET(pool, ones_s[:], 1.0, [B_c1])
                    for t_ in range(16):
                        MEMSET(dve, invc[:, t_:t_ + 1], 1.0 / (t_ + 1), [B_c1])
                    v_sb = sc.sb("v_sb", [128, NST, 1024], BF16)
                    B_v = Buf()
                    for b in range(NB):
                        for tbk in range(NSB):
                            tb = b * NSB + tbk
                            dma_blk(sp, hseq[:, tbk], hT_d, tb, 0, KC, True, [B_hT[tb]], [B_hs])
                        with scope() as s2:
                            wv = s2.sb("wv", [128, KC, 1024], BF16)
                            B_wv = Buf()
                            load_w_bf(wv, w_in[:, 2048:3072], KC, 4, [], B_wv)
                            pv = [s2.ps(f"pv{j}", [128, 1024]) for j in range(2)]
                            B_pv = [Buf() for _ in range(2)]
                            for tt in range(NST):
                                p_, bp = pv[tt % 2], B_pv[tt % 2]
                                for hf in range(2):
                                    for kk in range(KC):
                                        MM(p_[:, hf * 512:(hf + 1) * 512], hseq[:, tt // 4, kk, (tt % 4) * 128:(tt % 4 + 1) * 128],
                                           wv[:, kk, hf * 512:(hf + 1) * 512], kk == 0, kk == KC - 1, [B_wv, B_hs], [bp], tick=(kk == KC - 1))
                                ACT(v_sb[:, tt, 0:512], p_[:, 0:512], AF.Copy, [bp], [B_v])
                                ACT(v_sb[:, tt, 512:1024], p_[:, 512:1024], AF.Copy, [bp], [B_v])
                        with scope() as s2:
                            wp = [s2.sb(f"wp{j}", [128, KC, 128], BF16) for j in range(2)]
                            B_wp = [Buf() for _ in range(2)]
                            pp = s2.sb("pp", [128, 16 + S])
                            sa = s2.sb("sa", [128, 16 + S])
                            sbb = s2.sb("sbb", [128, 16 + S])
                            pt = s2.sb("pt", [128, S])
                            tmpc = s2.sb("tmpc", [128, 16])
                            pbf = s2.sb("pbf", [128, 2, S], BF16)
                            yd = [s2.sb(f"yd{j}", [128, S], BF16) for j in range(2)]
                            B_pp, B_sa, B_sbb, B_pt, B_pbf = [Buf() for _ in range(5)]
                            B_yd = [Buf() for _ in range(2)]
                            ppj = [s2.ps(f"ppj{j}", [128, 512]) for j in range(2)]
                            B_ppj = [Buf() for _ in range(2)]
                            pyd = [s2.ps(f"pyd{j}", [128, 512]) for j in range(2)]
                            B_pyd = [Buf() for _ in range(2)]
                            MEMSET(dve, pp[:, 0:16], 0.0, [B_pp])
                            MEMSET(dve, sa[:, 0:16], 0.0, [B_sa])
                            MEMSET(dve, sbb[:, 0:16], 0.0, [B_sbb])
                            npp = nyd = 0
                            for pc in range(8):
                                g = pc // 2
                                w = POOL_W[g]
                                w_, bw = wp[pc % 2], B_wp[pc % 2]
                                load_w_bf(w_, w_in[:, 3072 + pc * 128:3072 + (pc + 1) * 128], KC, 16, [], bw)
                                for tbk in range(NSB):
                                    p_, bp = ppj[npp % 2], B_ppj[npp % 2]
                                    npp += 1
                                    for kk in range(KC):
                                        MM(p_[:], w_[:, kk, :], hseq[:, tbk, kk, :], kk == 0, kk == KC - 1,
                                           [bw, B_hs], [bp], tick=(kk == KC - 1))
                                    ACT(pp[:, 16 + tbk * 512:16 + (tbk + 1) * 512], p_[:], AF.Copy, [bp], [B_pp])
                                TT(dve, sa[:, 16:], pp[:, 16:], pp[:, 15:15 + S], ALU.add, [B_pp], [B_sa])
                                cur, bcur = sa, B_sa
                                oth, both = sbb, B_sbb
                                sh = 2
                                while sh < w:
                                    TT(dve, oth[:, 16:], cur[:, 16:], cur[:, 16 - sh:16 - sh + S], ALU.add, [bcur], [both])
                                    cur, bcur, oth, both = oth, both, cur, bcur
                                    sh *= 2
                                STT(pt[:], cur[:, 16:], 1.0 / w, pp[:, 16:], ALU.mult, ALU.subtract, [bcur, B_pp], [B_pt])
                                TT(dve, tmpc[:, 0:w - 1], cur[:, 16:16 + w - 1], invc[:, 0:w - 1], ALU.mult, [bcur, B_c1], [B_pt])
                                TT(dve, pt[:, 0:w - 1], tmpc[:, 0:w - 1], pp[:, 16:16 + w - 1], ALU.subtract, [B_pt, B_pp], [B_pt])
                                CP(pool, pbf[:, pc % 2, :], pt[:], [B_pt], [B_pbf])
                                if pc % 2 == 1:
                                    for ec in range(2):
                                        y_, by = yd[nyd % 2], B_yd[nyd % 2]
                                        nyd += 1
                                        for tbk in range(NSB):
                                            tsl = slice(tbk * 512, (tbk + 1) * 512)
                                            p_, bp = pyd[tbk % 2], B_pyd[tbk % 2]
                                            MM(p_[:], wpl[:, g, 0, ec * 128:(ec + 1) * 128], pbf[:, 0, tsl], True, False, [B_c1, B_pbf], [bp], tick=False)
                                            MM(p_[:], wpl[:, g, 1, ec * 128:(ec + 1) * 128], pbf[:, 1, tsl], False, True, [B_c1, B_pbf], [bp], tick=True)
                                            ACT(y_[:, tsl], p_[:], AF.Identity, [bp, B_tab], [by], scale=col("psc", g * 2 + ec))
                                        ch = 8 + g * 2 + ec
                                        dma_rows(sp, yT_d, b, ch, y_, [by], [B_yT[b][ch]])
                        with scope() as s2:
                            wq = [s2.sb(f"wq{j}", [128, KC, 128], BF16) for j in range(2)]
                            wk = [s2.sb(f"wk{j}", [128, KC, 128], BF16) for j in range(2)]
                            B_wq = [Buf() for _ in range(2)]
                            qT = s2.sb("qT", [128, S], BF16)
                            kT = s2.sb("kT", [128, S], BF16)
                            B_q, B_k = Buf(), Buf()
                            ee = [s2.sb(f"ee{j}", [128, S]) for j in range(2)]
                            ff = [s2.sb(f"ff{j}", [128, S]) for j in range(2)]
                            zs = [s2.sb(f"zs{j}", [128, S]) for j in range(2)]
                            negt = [s2.sb(f"negt{j}", [128, 1]) for j in range(2)]
                            wbf = [s2.sb(f"wbf{j}", [128, S], BF16) for j in range(2)]
                            wT = [s2.sb(f"wT{j}", [128, NST, 128], BF16) for j in range(2)]
                            B_ee, B_ff, B_zs, B_nt, B_wbf, B_wT = ([Buf() for _ in range(2)] for _ in range(6))
                            yc = [s2.sb(f"yc{j}", [128, S], BF16) for j in range(2)]
                            B_yc = [Buf() for _ in range(2)]
                            zps = [s2.ps(f"zps{j}", [128, 2, 512]) for j in range(2)]
                            B_z = [Buf() for _ in range(2)]
                            pqk = s2.ps("pqk", [128, 512])
                            B_pqk = Buf()
                            wtp = [s2.ps(f"wtp{j}", [128, 4, 128], BF16) for j in range(2)]
                            B_wtp = [Buf() for _ in range(2)]
                            ycp = s2.ps("ycp", [128, 128])
                            B_ycp = Buf()
                            cnt = {"z": 0, "wt": 0}
                            for h in range(8):
                                wq_, wk_, bw = wq[h % 2], wk[h % 2], B_wq[h % 2]
                                load_w_bf(wq_, w_in[:, h * 128:(h + 1) * 128], KC, 16, [], bw)
                                load_w_bf(wk_, w_in[:, 1024 + h * 128:1024 + (h + 1) * 128], KC, 16, [], bw)
                                for tbk in range(NSB):
                                    tsl = slice(tbk * 512, (tbk + 1) * 512)
                                    for kk in range(KC):
                                        MM(pqk[:], wq_[:, kk, :], hseq[:, tbk, kk, :], kk == 0, kk == KC - 1, [bw, B_hs], [B_pqk], tick=(kk == KC - 1))
                                    ACT(qT[:, tsl], pqk[:], AF.Copy, [B_pqk], [B_q])
                                    for kk in range(KC):
                                        MM(pqk[:], wk_[:, kk, :], hseq[:, tbk, kk, :], kk == 0, kk == KC - 1, [bw, B_hs], [B_pqk], tick=(kk == KC - 1))
                                    ACT(kT[:, tsl], pqk[:], AF.Copy, [B_pqk], [B_k])
                                y_, by = yc[h % 2], B_yc[h % 2]

                                def stA(qi):
                                    st_ = qi % 2
                                    nk_ = (qi + 1) * 128
                                    dsl = slice(qi * 128, (qi + 1) * 128)
                                    for hf in range((nk_ + 1023) // 1024):
                                        zp, bz = zps[cnt["z"] % 2], B_z[cnt["z"] % 2]
                                        cnt["z"] += 1
                                        k0h = hf * 1024
                                        k1h = min(nk_, k0h + 1024)
                                        nchh = (k1h - k0h + 511) // 512
                                        for kc in range(nchh):
                                            a0 = k0h + kc * 512
                                            a1 = min(k1h, a0 + 512)
                                            MM(zp[:, kc, 0:a1 - a0], qT[:, dsl], kT[:, a0:a1], True, True, [B_q, B_k], [bz], tick=(kc == nchh - 1))
                                        for kc in range(nchh):
                                            a0 = k0h + kc * 512
                                            a1 = min(k1h, a0 + 512)
                                            ACT(ee[st_][:, a0:a1], zp[:, kc, 0:a1 - a0], AF.Exp, [bz], [B_ee[st_]], scale=zscale)
                                            ACT(zs[st_][:, a0:a1], zp[:, kc, 0:a1 - a0], AF.Copy, [bz], [B_zs[st_]], scale=zscale)
                                    ACT(ee[st_][:, 0:nk_], ee[st_][:, 0:nk_], AF.Ln, [B_ee[st_], B_tab], [B_ee[st_]], bias=col("one"))

                                def stB(qi):
                                    st_ = qi % 2
                                    nk_ = (qi + 1) * 128
                                    dsl = slice(qi * 128, (qi + 1) * 128)
                                    e_, f_, z_, w_ = ee[st_], ff[st_], zs[st_], wbf[st_]
                                    TT(pool, e_[:, dsl], e_[:, dsl], tri[:], ALU.mult, [B_ee[st_], B_c1], [B_ee[st_]])
                                    TT(pool, z_[:, 0:nk_], z_[:, 0:nk_], e_[:, 0:nk_], ALU.subtract, [B_zs[st_], B_ee[st_]], [B_zs[st_]])
                                    SCAN(f_[:, 0:nk_], ones_s[:, 0:nk_], e_[:, 0:nk_], col("zero"), [B_ee[st_], B_c1, B_tab], [B_ff[st_]])
                                    TS(pool, negt[st_][:], f_[:, nk_ - 1:nk_], -1.0, 0.0, ALU.mult, ALU.add, [B_ff[st_]], [B_nt[st_]])
                                    TT(dve, f_[:, 0:nk_], f_[:, 0:nk_], z_[:, 0:nk_], ALU.add, [B_zs[st_], B_ff[st_]], [B_ff[st_]])
                                    ACT(w_[:, 0:nk_], f_[:, 0:nk_], AF.Exp, [B_ff[st_], B_nt[st_]], [B_wbf[st_]], bias=negt[st_][:])
                                    TT(pool, w_[:, dsl], w_[:, dsl], tri[:], ALU.mult, [B_wbf[st_], B_c1], [B_wbf[st_]])

                                def stC1(qi):
                                    st_ = qi % 2
                                    for kb4 in range((qi + 4) // 4):
                                        j_ = cnt["wt"] % 2
                                        cnt["wt"] += 1
                                        p_, bp = wtp[j_], B_wtp[j_]
                                        nb_ = min(4, qi + 1 - kb4 * 4)
                                        for ii in range(nb_):
                                            kb = kb4 * 4 + ii
                                            TR(p_[:, ii, :], wbf[st_][:, kb * 128:(kb + 1) * 128], ident_bf[:], [B_wbf[st_], B_const], [bp], tick=(ii == nb_ - 1))
                                        CP(dve, wT[st_][:, kb4 * 4:kb4 * 4 + nb_, :], p_[:, 0:nb_, :], [bp], [B_wT[st_]])

                                def stC2(qi):
                                    st_ = qi % 2
                                    dsl = slice(qi * 128, (qi + 1) * 128)
                                    for kb in range(qi + 1):
                                        MM(ycp[:], v_sb[:, kb, h * 128:(h + 1) * 128], wT[st_][:, kb, :], kb == 0, kb == qi, [B_v, B_wT[st_]], [B_ycp],
                                           tick=(kb == qi))
                                    CP(dve, y_[:, dsl], ycp[:], [B_ycp], [by])

                                stA(0)
                                if NST > 1:
                                    stA(1)
                                stB(0)
                                for qi in range(NST):
                                    stC1(qi)
                                    if qi + 2 < NST:
                                        stA(qi + 2)
                                    if qi + 1 < NST:
                                        stB(qi + 1)
                                    stC2(qi)
                                dma_rows(sp, yT_d, b, h, y_, [by], [B_yT[b][h]])

            stages = []
            if stop_after != "xt":
                mixer_ab()
                if stop_after != "mix0":
                    gemm_res_ln("o0", ab_w_out[0], KC, yT_d, B_yT, 0, 0, False)
                    if stop_after != "ln00":
                        ffn1(0)
                        if stop_after != "ffn10":
                            gemm_res_ln("d0", ffn_w_down[0], FC, aT_d, B_aT, 0, 1, False)
                            if stop_after != "ln01":
                                mixer_cd()
                                if stop_after != "mix1":
                                    gemm_res_ln("o1", cd_w_out[0], KC, yT_d, B_yT, 1, 0, False)
                                    ffn1(1)
                                    gemm_res_ln("d1", ffn_w_down[1], FC, aT_d, B_aT, 1, 1, True)

        except StopBuild:
            pass
        k.finish()
    return nc


INPUT_NAMES = ["x", "c", "ada_w", "ada_b", "norm_g", "norm_b", "ffn_w_gate", "ffn_w_up", "ffn_w_down",
               "ab_w_in", "ab_conv_w", "ab_conv_b", "ab_w_r", "ab_b_r", "ab_w_i", "ab_b_i", "ab_lambda",
               "ab_vnorm_g", "ab_vnorm_b", "ab_w_s", "ab_b_s", "ab_w_out",
               "cd_w_in", "cd_w_pool", "cd_pool_scale", "cd_w_out"]


def kernel(**inputs):
    x = np.ascontiguousarray(np.asarray(inputs["x"], dtype=np.float32))
    B, S, D = x.shape
    NB = B // NCORES
    nc = build(NB, S)
    shared = {n: np.ascontiguousarray(np.asarray(inputs[n], dtype=np.float32)) for n in INPUT_NAMES if n not in ("x", "c")}
    cfull = np.ascontiguousarray(np.asarray(inputs["c"], dtype=np.float32))
    in_maps = []
    for i in range(NCORES):
        m = dict(shared)
        m["x"] = x[i * NB:(i + 1) * NB].reshape(NB * S, D)
        m["c"] = cfull[i * NB:(i + 1) * NB]
        in_maps.append(m)
    res = run_bass_kernel_spmd(nc, in_maps, core_ids=list(range(NCORES)))
    outs = [np.asarray(r["out"]).reshape(NB, S, D) for r in res.results]
    return np.concatenate(outs, axis=0).astype(np.float32)
```

```python
from contextlib import ExitStack, contextmanager
import numpy as np
import concourse.bass as bass
import concourse.mybir as mybir
from concourse.bass_utils import run_bass_kernel_spmd

F32 = mybir.dt.float32
BF16 = mybir.dt.bfloat16
AF = mybir.ActivationFunctionType
ALU = mybir.AluOpType

DM = 2048
KC = DM // 128
DFF = 5632
FC = DFF // 128
NL = 2
NCORES = 8
ALPHA = (2 * NL) ** 0.25
EPS = 1e-5
POOL_W = (2, 4, 8, 16)
SEM_LIMIT = 30000
import os
XTF = os.environ.get('XTF', '').split(',')


class StopBuild(Exception):
    pass


class Buf:
    __slots__ = ("name", "w", "r")

    def __init__(self, name=""):
        self.name = name
        self.w = None
        self.r = []


class Eng:
    def __init__(self, name, h, is_pe=False):
        self.name = name
        self.h = h
        self.semidx = None
        self.tick = 0
        self.waited = {}
        self.prog = []
        self.pending = []
        self.is_pe = is_pe
        self.ring = []
        self.ring_pos = 0


class K:
    def __init__(self, nc, same_sync=True, ring=6):
        self.nc = nc
        self.same_sync = same_sync
        self.sems = []
        self.ring_n = ring
        self.live = {}
        self.halted = False

    def newsem(self, name):
        s = self.stack.enter_context(self.nc.semaphore(name))
        self.sems.append(s)
        return len(self.sems) - 1

    def start(self, stack):
        nc = self.nc
        self.stack = stack
        self.pe = Eng("pe", nc.tensor, is_pe=True)
        self.act = Eng("act", nc.scalar)
        self.dve = Eng("dve", nc.vector)
        self.pool = Eng("pool", nc.gpsimd)
        self.sp = Eng("sp", nc.sync)
        self.engs = [self.pe, self.act, self.dve, self.pool, self.sp]
        for e in self.engs:
            e.semidx = self.newsem(f"t_{e.name}0")
            e.nsem = 1
        for e in (self.sp, self.act, self.pool):
            e.ring = [[self.newsem(f"r_{e.name}{i}"), 0] for i in range(self.ring_n)]

    def _deps(self, e, reads, writes):
        deps = {}

        def need(tok):
            if tok is None:
                return
            s, v = tok[0], tok[1]
            if tok[2] is e and (e.is_pe or not self.same_sync):
                return
            if v is None:
                raise RuntimeError(f"dependency on pending op ({tok[2].name}) from {e.name}")
            if deps.get(s, 0) < v:
                deps[s] = v

        for b in reads:
            need(b.w)
        for b in writes:
            need(b.w)
            for t in b.r:
                need(t)
        waits = []
        for s, v in deps.items():
            if e.waited.get(s, 0) < v:
                e.waited[s] = v
                waits.append((s, v))
        return waits

    def _record(self, tok, reads, writes):
        for b in reads:
            b.r.append(tok)
        for b in writes:
            b.w = tok
            b.r = []

    def op(self, e, fn, reads=(), writes=(), tick=True):
        if self.halted:
            return
        waits = self._deps(e, reads, writes)
        if tick:
            if e.tick >= SEM_LIMIT:
                if e.pending:
                    raise RuntimeError("sem rollover with pending ops")
                e.semidx = self.newsem(f"t_{e.name}{e.nsem}")
                e.nsem += 1
                e.tick = 0
            e.tick += 1
            tok = [e.semidx, e.tick, e]
            for p in e.pending:
                p[0] = e.semidx
                p[1] = e.tick
            e.pending = []
            inc = (e.semidx, 1)
            self.live[e.semidx] = e.tick
        else:
            tok = [e.semidx, None, e]
            e.pending.append(tok)
            inc = None
        e.prog.append((waits, fn, inc))
        self._record(tok, reads, writes)

    def dma(self, q, out, in_, reads=(), writes=(), **kw):
        self.dma_multi(q, [(out, in_)], reads, writes, **kw)

    def dma_multi(self, q, pairs, reads=(), writes=(), **kw):
        if self.halted:
            return
        waits = self._deps(q, reads, writes)
        slot = q.ring[q.ring_pos]
        if slot[1] + 16 * len(pairs) > SEM_LIMIT:
            slot = q.ring[q.ring_pos] = [self.newsem(f"r_{q.name}x{len(self.sems)}"), 0]
        q.ring_pos = (q.ring_pos + 1) % len(q.ring)
        s = slot[0]
        if slot[1] > 0 and q.waited.get(s, 0) < slot[1]:
            q.waited[s] = slot[1]
            waits.append((s, slot[1]))
        for i, (o, i_) in enumerate(pairs):
            slot[1] += 16

            def fn(o=o, i_=i_):
                return q.h.dma_start(out=o, in_=i_, **kw)
            q.prog.append((waits if i == 0 else [], fn, (s, 16)))
        self.live[s] = slot[1]
        tok = [s, slot[1], None]
        self._record(tok, reads, writes)

    def barrier(self):
        for e in self.engs:
            if e.pending:
                raise RuntimeError(f"barrier with pending ops on {e.name}")
        for e in self.engs:
            waits = []
            for s, v in self.live.items():
                if e.waited.get(s, 0) < v:
                    e.waited[s] = v
                    waits.append((s, v))
            if waits:
                e.prog.append((waits, None, None))

    def finish(self):
        nc = self.nc
        sems = self.sems
        self.barrier()

        def replay(e):
            def run(h):
                for waits, fn, inc in e.prog:
                    for s, v in waits:
                        h.wait_ge(sems[s], v)
                    if fn is None:
                        continue
                    ins = fn()
                    if inc is not None:
                        ins.then_inc(sems[inc[0]], inc[1])
            return run

        with nc.Block() as block:
            block.tensor(replay(self.pe))
            block.scalar(replay(self.act))
            block.vector(replay(self.dve))
            block.gpsimd(replay(self.pool))
            block.sync(replay(self.sp))


def build(NB, S, dbg=False, stop_after=None, same_sync=True):
    nc = bass.Bass("TRN2", target_bir_lowering=False)
    T = NB * S
    NSB = S // 512
    NST = S // 128
    skind = "ExternalOutput" if dbg else "Internal"

    def din(name, shape):
        return nc.dram_tensor(name, list(shape), F32, kind="ExternalInput").ap()

    x = din("x", [T, DM])
    c = din("c", [NB, DM])
    ada_w = din("ada_w", [NL, DM, 6 * DM])
    ada_b = din("ada_b", [NL, 6 * DM])
    norm_g = din("norm_g", [NL, 2, DM])
    norm_b = din("norm_b", [NL, 2, DM])
    ffn_w_gate = din("ffn_w_gate", [NL, DM, DFF])
    ffn_w_up = din("ffn_w_up", [NL, DM, DFF])
    ffn_w_down = din("ffn_w_down", [NL, DFF, DM])
    ab_w_in = din("ab_w_in", [1, DM, 4096])
    ab_conv_w = din("ab_conv_w", [1, 4, 1024])
    ab_conv_b = din("ab_conv_b", [1, 1024])
    ab_w_r = din("ab_w_r", [1, 8, 128, 128])
    ab_b_r = din("ab_b_r", [1, 8, 128])
    ab_w_i = din("ab_w_i", [1, 8, 128, 128])
    ab_b_i = din("ab_b_i", [1, 8, 128])
    ab_lambda = din("ab_lambda", [1, 1024])
    ab_vnorm_g = din("ab_vnorm_g", [1, 1024])
    ab_vnorm_b = din("ab_vnorm_b", [1, 1024])
    ab_w_s = din("ab_w_s", [1, 8, 128, 128])
    ab_b_s = din("ab_b_s", [1, 8, 128])
    ab_w_out = din("ab_w_out", [1, DM, DM])
    cd_w_in = din("cd_w_in", [1, DM, 4096])
    cd_w_pool = din("cd_w_pool", [1, 4, 256, 256])
    cd_pool_scale = din("cd_pool_scale", [1, 1024])
    cd_w_out = din("cd_w_out", [1, DM, DM])
    out = nc.dram_tensor("out", [T, DM], F32, kind="ExternalOutput").ap()

    xs_d = nc.dram_tensor("xs_d", [T // 512, 128, KC, 512], F32, kind=skind).ap()
    hT_d = nc.dram_tensor("hT_d", [T // 512, 128, KC, 512], BF16, kind=skind).ap()
    yT_d = nc.dram_tensor("yT_d", [T // 512, 128, KC, 512], BF16, kind=skind).ap()
    aT_d = nc.dram_tensor("aT_d", [T // 512, 128, FC, 512], BF16, kind=skind).ap()
    stat_d = nc.dram_tensor("stat_d", [T // 512, 128, 2, 512], F32, kind="Internal").ap()
    B_xs = [Buf(f"xs{i}") for i in range(T // 512)]
    B_hT = [Buf(f"hT{i}") for i in range(T // 512)]
    B_yT = [[Buf() for _ in range(KC)] for _ in range(NB)]
    B_aT = [[Buf() for _ in range(FC)] for _ in range(NB)]

    k = K(nc, same_sync=same_sync)
    pe, act, dve, pool, sp = None, None, None, None, None

    with ExitStack() as st:
        k.start(st)
        pe, act, dve, pool, sp = k.pe, k.act, k.dve, k.pool, k.sp

        try:
            uniq = [0]

            @contextmanager
            def scope():
                with ExitStack() as s2:
                    class Sc:
                        def sb(self, name, shape, dt=F32):
                            uniq[0] += 1
                            return s2.enter_context(nc.sbuf_tensor(f"{name}_{uniq[0]}", list(shape), dt))

                        def ps(self, name, shape, dt=F32):
                            uniq[0] += 1
                            return s2.enter_context(nc.psum_tensor(f"{name}_{uniq[0]}", list(shape), dt))
                    yield Sc()
                    k.barrier()

            def ACT(out_, in_, func, reads, writes, bias=None, scale=1.0):
                if bias is None:
                    k.op(act, lambda: nc.scalar.activation(out=out_, in_=in_, func=func, scale=scale), reads, writes)
                else:
                    k.op(act, lambda: nc.scalar.activation(out=out_, in_=in_, func=func, bias=bias, scale=scale), reads, writes)

            def MM(out_, lhsT, rhs, start, stop, reads, writes, tick):
                k.op(pe, lambda: nc.tensor.matmul(out_, lhsT=lhsT, rhs=rhs, start=start, stop=stop, skip_group_check=True),
                     reads, writes, tick=tick)

            def TR(out_, in_, ident, reads, writes, tick):
                k.op(pe, lambda: nc.tensor.transpose(out_, in_, ident), reads, writes, tick=tick)

            def TT(e, out_, in0, in1, op_, reads, writes):
                k.op(e, lambda: e.h.tensor_tensor(out=out_, in0=in0, in1=in1, op=op_), reads, writes)

            def TS(e, out_, in0, s1, s2, op0, op1, reads, writes):
                k.op(e, lambda: e.h.tensor_scalar(out=out_, in0=in0, scalar1=s1, scalar2=s2, op0=op0, op1=op1), reads, writes)

            def STT(out_, in0, scalar, in1, op0, op1, reads, writes):
                k.op(dve, lambda: nc.vector.scalar_tensor_tensor(out=out_, in0=in0, scalar=scalar, in1=in1, op0=op0, op1=op1),
                     reads, writes)

            def CP(e, out_, in_, reads, writes):
                if e is act:
                    k.op(e, lambda: nc.scalar.activation(out=out_, in_=in_, func=AF.Copy), reads, writes)
                else:
                    k.op(e, lambda: e.h.tensor_copy(out=out_, in_=in_), reads, writes)

            def SCAN(out_, d0, d1, init, reads, writes):
                k.op(dve, lambda: nc.vector.tensor_tensor_scan(out=out_, data0=d0, data1=d1, initial=init, op0=ALU.mult, op1=ALU.add),
                     reads, writes)

            def RECIP(out_, in_, reads, writes):
                k.op(dve, lambda: nc.vector.reciprocal(out=out_, in_=in_), reads, writes)

            def BNS(out_, in_, reads, writes):
                k.op(dve, lambda: nc.vector.bn_stats(out=out_, in_=in_), reads, writes)

            def BNA(out_, in_, reads, writes):
                k.op(dve, lambda: nc.vector.bn_aggr(out=out_, in_=in_), reads, writes)

            def MEMSET(e, ap, val, writes):
                k.op(e, lambda: e.h.memset(ap, val), [], writes)

            def dma_blk(q, sb_tile, X_d, tb, k0, k1, load, reads, writes, kg=16):
                pairs = []
                for a0 in range(k0, k1, kg):
                    a1 = min(k1, a0 + kg)
                    d = X_d[tb, :, a0:a1, :]
                    s_ = sb_tile[:, a0 - k0:a1 - k0, :]
                    pairs.append((s_, d) if load else (d, s_))
                k.dma_multi(q, pairs, reads, writes)

            def dma_rows(q, X_d, b, ch, sb_row, reads, writes):
                pairs = [(X_d[b * NSB + tbk, :, ch, :], sb_row[:, tbk * 512:(tbk + 1) * 512]) for tbk in range(NSB)]
                k.dma_multi(q, pairs, reads, writes)

            def checkpoint(name):
                if stop_after == name:
                    k.halted = True

            sbp = lambda name, shape, dt=F32: st.enter_context(nc.sbuf_tensor(name, list(shape), dt))
            cols = {}
            ncol = [0]

            def colgrp(name, n=16):
                cols[name] = ncol[0]
                ncol[0] += n
                return cols[name]

            raw_groups = []
            for l in range(NL):
                for i in range(2):
                    raw_groups.append((f"ng{l}{i}", norm_g[l, i], 16))
                    raw_groups.append((f"nb{l}{i}", norm_b[l, i], 16))
            for l in range(NL):
                raw_groups.append((f"adab{l}", ada_b[l], 96))
            for j in range(4):
                raw_groups.append((f"cw{j}", ab_conv_w[0, j], 8))
            raw_groups.append(("cb", ab_conv_b[0], 8))
            raw_groups.append(("br", ab_b_r[0].rearrange("h e -> (h e)"), 8))
            raw_groups.append(("bi", ab_b_i[0].rearrange("h e -> (h e)"), 8))
            raw_groups.append(("lam", ab_lambda[0], 8))
            raw_groups.append(("psc", cd_pool_scale[0], 8))
            for b in range(NB):
                raw_groups.append((f"cT{b}", c[b], 16))
            for nm, _, n_ in raw_groups:
                colgrp(nm, n_)
            NRAW = ncol[0]
            for l in range(NL):
                for i in range(2):
                    colgrp(f"xss{l}{i}")
                    colgrp(f"xsb{l}{i}")
                    for b in range(NB):
                        colgrp(f"hs{l}{i}{b}")
                        colgrp(f"hb{l}{i}{b}")
                for b in range(NB):
                    colgrp(f"g1{l}{b}")
                    colgrp(f"g2{l}{b}")
                    colgrp(f"s1p{l}{b}")
                    colgrp(f"s2p{l}{b}")
            for b in range(NB):
                colgrp(f"hs_in{b}")
            colgrp("lam2", 8)
            colgrp("zero", 1)
            colgrp("eps", 1)
            colgrp("one", 1)
            NTAB = ncol[0]
            tab = sbp("tab", [128, NTAB])
            B_tab = Buf("tab")
            modr = sbp("modr", [128, NL, NB, 96])
            B_modr = Buf("modr")
            ident = sbp("ident", [128, 128])
            ident_bf = sbp("ident_bf", [128, 128], BF16)
            ones_f = sbp("ones_f", [128, 128])
            B_const = Buf("const")

            def col(name, i=0, n=1):
                c0 = cols[name] + i
                return tab[:, c0:c0 + n]

            MEMSET(pool, ident[:], 1.0, [B_const])
            k.op(pool, lambda: nc.gpsimd.affine_select(out=ident[:], in_=ident[:], compare_op=ALU.is_equal, fill=0.0,
                                                       base=0, pattern=[[-1, 128]], channel_multiplier=1), [B_const], [B_const])
            CP(pool, ident_bf[:], ident[:], [B_const], [B_const])
            MEMSET(pool, ones_f[:], 1.0, [B_const])
            MEMSET(dve, col("zero"), 0.0, [B_tab])
            MEMSET(dve, col("eps"), EPS, [B_tab])

            MEMSET(dve, col("one"), 1.0, [B_tab])
            with scope() as sc:
                nst_ = (NRAW + 127) // 128
                stg = [sc.sb(f"stg{j}", [128, 128]) for j in range(nst_)]
                B_stg = [Buf() for _ in range(nst_)]
                tps_ = sc.ps("tps_", [128, 128])
                B_tps_ = Buf()
                for j in range(nst_):
                    MEMSET(dve, stg[j][:], 0.0, [B_stg[j]])
                for nm, vec, n_ in raw_groups:
                    c0 = cols[nm]
                    r = 0
                    while r < n_:
                        j, p0 = divmod(c0 + r, 128)
                        m_ = min(n_ - r, 128 - p0)
                        k.dma(sp, stg[j][p0:p0 + m_, :], vec.rearrange("(c p) -> c p", p=128)[r:r + m_, :], writes=[B_stg[j]])
                        r += m_
                for j in range(nst_):
                    w_ = min(128, NRAW - j * 128)
                    TR(tps_[:], stg[j][:], ident[:], [B_stg[j], B_const], [B_tps_], tick=True)
                    CP(dve, tab[:, j * 128:j * 128 + w_], tps_[:, 0:w_], [B_tps_], [B_tab])
            checkpoint("pro_a")
            ACT(col("lam2", 0, 8), col("lam", 0, 8), AF.Exp, [B_tab], [B_tab], scale=-1.0)
            ACT(col("lam2", 0, 8), col("lam2", 0, 8), AF.Ln, [B_tab], [B_tab], bias=col("one"))
            TS(dve, col("lam", 0, 8), col("lam2", 0, 8), -8.0, None, ALU.mult, ALU.bypass, [B_tab], [B_tab])
            TS(dve, col("lam2", 0, 8), col("lam2", 0, 8), -16.0, None, ALU.mult, ALU.bypass, [B_tab], [B_tab])

            checkpoint("pro_b")
            with scope() as sc:
                cT = sc.sb("cT", [128, KC, NB])
                B_cT = Buf("cT")
                for b in range(NB):
                    ACT(cT[:, :, b], col(f"cT{b}", 0, 16), AF.Silu, [B_tab], [B_cT])
                CBW = 3072
                aw = [sc.sb(f"aw{i}", [128, CBW]) for i in range(2)]
                B_aw = [Buf(f"aw{i}") for i in range(2)]
                mp = [sc.ps(f"mp{l}", [128, 96 * NB]) for l in range(NL)]
                B_mp = [Buf(f"mp{l}") for l in range(NL)]
                n = 0
                for l in range(NL):
                    for kk in range(KC):
                        for cb in range(6 * DM // CBW):
                            t = aw[n % 2]
                            bt = B_aw[n % 2]
                            n += 1
                            k.dma(sp if n % 2 else act, t[:], ada_w[l, kk * 128:(kk + 1) * 128, cb * CBW:(cb + 1) * CBW], writes=[bt])
                            for jj in range(CBW // 128):
                                j = cb * (CBW // 128) + jj
                                last = (kk == KC - 1) and (j == 95)
                                MM(mp[l][:, j * NB:(j + 1) * NB], t[:, jj * 128:(jj + 1) * 128], cT[:, kk, :],
                                   start=(kk == 0 and j == 0), stop=(kk == KC - 1), reads=[bt, B_cT], writes=[B_mp[l]],
                                   tick=(jj == CBW // 128 - 1))
                    for b in range(NB):
                        TT(dve, modr[:, l, b, :], mp[l][:].rearrange("p (j b) -> p j b", b=NB)[:, :, b],
                           col(f"adab{l}", 0, 96), ALU.add, [B_mp[l], B_tab], [B_modr])
                checkpoint("pro_c")
                for l in range(NL):
                    for b in range(NB):
                        TS(dve, col(f"g1{l}{b}", 0, 16), modr[:, l, b, 32:48], 1.0, None, ALU.add, ALU.bypass, [B_modr], [B_tab])
                        TS(dve, col(f"g2{l}{b}", 0, 16), modr[:, l, b, 80:96], 1.0, None, ALU.add, ALU.bypass, [B_modr], [B_tab])
                        TS(dve, col(f"s1p{l}{b}", 0, 16), modr[:, l, b, 16:32], 1.0, None, ALU.add, ALU.bypass, [B_modr], [B_tab])
                        TS(dve, col(f"s2p{l}{b}", 0, 16), modr[:, l, b, 64:80], 1.0, None, ALU.add, ALU.bypass, [B_modr], [B_tab])
                for l in range(NL):
                    for i in range(2):
                        g_ = col(f"ng{l}{i}", 0, 16)
                        b_ = col(f"nb{l}{i}", 0, 16)
                        TS(dve, col(f"xss{l}{i}", 0, 16), g_, ALPHA, None, ALU.mult, ALU.bypass, [B_tab], [B_tab])
                        TS(dve, col(f"xsb{l}{i}", 0, 16), b_, ALPHA, None, ALU.mult, ALU.bypass, [B_tab], [B_tab])
                        if i == 1 and l == NL - 1:
                            continue
                        for b in range(NB):
                            if i == 0:
                                sp_ = col(f"s2p{l}{b}", 0, 16)
                                sh_ = modr[:, l, b, 48:64]
                            else:
                                sp_ = col(f"s1p{l + 1}{b}", 0, 16)
                                sh_ = modr[:, l + 1, b, 0:16]
                            TT(dve, col(f"hs{l}{i}{b}", 0, 16), g_, sp_, ALU.mult, [B_tab], [B_tab])
                            TT(dve, col(f"hb{l}{i}{b}", 0, 16), b_, sp_, ALU.mult, [B_tab], [B_tab])
                            TT(dve, col(f"hb{l}{i}{b}", 0, 16), col(f"hb{l}{i}{b}", 0, 16), sh_, ALU.add, [B_tab, B_modr], [B_tab])

            for b in range(NB):
                TS(dve, col(f"hs_in{b}", 0, 16), col(f"s1p0{b}", 0, 16), 1.0 / ALPHA, None, ALU.mult, ALU.bypass, [B_tab], [B_tab])
            checkpoint("pro_d")
            with scope() as sc:
                xt = [sc.sb(f"xt{i}", [128, DM]) for i in range(2)]
                B_xt = [Buf() for _ in range(2)]
                xsb = sc.sb("xsb", [128, KC, 512])
                hb = sc.sb("hb", [128, KC, 512], F32 if "hbf32" in XTF else BF16)
                B_xsb = [[Buf() for _ in range(4)] for _ in range(KC)]
                B_hb = [[Buf() for _ in range(4)] for _ in range(KC)]
                allx = [B_xsb[cc_][q_] for cc_ in range(KC) for q_ in range(4)]
                allh = [B_hb[cc_][q_] for cc_ in range(KC) for q_ in range(4)]
                tp = [sc.ps(f"tp{i}", [128, 4, 128]) for i in range(4)]
                B_tp = [Buf() for _ in range(4)]
                n = 0
                for tb in range(T // 512):
                    b = tb // NSB
                    for q in range(4):
                        tt = tb * 4 + q
                        t, bt = xt[tt % 2], B_xt[tt % 2]
                        k.dma(sp, t[:], x[tt * 128:(tt + 1) * 128, :], writes=[bt])
                        for c4 in range(4):
                            p_, bp = tp[n % 4], B_tp[n % 4]
                            n += 1
                            for i in range(4):
                                cc = c4 * 4 + i
                                TR(p_[:, i, :], t[:, cc * 128:(cc + 1) * 128], ident[:], [bt, B_const], [bp], tick=(i == 3))
                            ACT(xsb[:, c4 * 4:(c4 + 1) * 4, q * 128:(q + 1) * 128], p_[:], AF.Copy, [bp], [B_xsb[c4 * 4 + i_][q] for i_ in range(4)], scale=ALPHA)
                            for i in range(4):
                                cc = c4 * 4 + i
                                TS(dve, hb[:, cc, q * 128:(q + 1) * 128], xsb[:, cc, q * 128:(q + 1) * 128], col(f"hs_in{b}", cc), None,
                                   ALU.mult, ALU.bypass, [B_xsb[cc][q], B_tab], [B_hb[cc][q]])
                                TS(dve, hb[:, cc, q * 128:(q + 1) * 128], hb[:, cc, q * 128:(q + 1) * 128], modr[:, 0, b, cc:cc + 1], None,
                                   ALU.add, ALU.bypass, [B_hb[cc][q], B_modr], [B_hb[cc][q]])
                    if "nost" not in XTF:
                        dma_blk(sp, xsb, xs_d, tb, 0, KC, False, allx, [B_xs[tb]])
                        dma_blk(act, hb, hT_d, tb, 0, KC, False, allh, [B_hT[tb]])

            def load_w_bf(dst, src, nk, kstep, breads, bw):
                v = src.rearrange("(k p) n -> p k n", p=128)
                for k0 in range(0, nk, kstep):
                    k1 = min(nk, k0 + kstep)
                    k.dma(pool, dst[:, k0:k1, :], v[:, k0:k1, :], reads=breads, writes=[bw])

            def gemm_res_ln(name, W, nk, src_d, B_src, l, i, final):
                ND = 2 if nk <= 16 else 4
                CPD = KC // ND
                gname = "g1" if i == 0 else "g2"
                with scope() as so:
                    B_st = [Buf() for _ in range(T // 512)]
                    with scope() as sc:
                        NWB = 2
                        stg = [sc.sb("stg0", [128, 2, 512])]
                        B_stg = [Buf()]
                        Wsbs = [sc.sb(f"Wsb{j}", [128, nk, CPD * 128], BF16) for j in range(NWB)]
                        B_Ws = [Buf("W") for _ in range(NWB)]
                        ab = [sc.sb(f"ab{j}", [128, nk, 512], BF16) for j in range(2)]
                        B_ab = [Buf() for _ in range(2)]
                        rb = [sc.sb(f"rb{j}", [128, CPD, 512]) for j in range(2)]
                        B_rb = [[Buf() for _ in range(CPD)] for _ in range(2)]
                        sq = [sc.sb(f"sq{j}", [128, 512]) for j in range(2)]
                        B_sq = [Buf() for _ in range(2)]
                        acc = [sc.ps(f"acc{j}", [128, 512]) for j in range(3)]
                        B_acc = [Buf() for _ in range(3)]
                        pss = [sc.ps(f"pss{j}", [128, 512]) for j in range(2)]
                        psq = [sc.ps(f"psq{j}", [128, 512]) for j in range(2)]
                        B_pst = [Buf() for _ in range(2)]
                        nacc = nst = 0
                        items = [(dq, tb) for dq in range(ND) for tb in range(T // 512)]

                        def issue_loads(n):
                            dq, tb = items[n]
                            if tb == 0 and (NWB == 2 or dq == 0):
                                load_w_bf(Wsbs[dq % NWB], W[:, dq * CPD * 128:(dq + 1) * CPD * 128], nk, 4, [], B_Ws[dq % NWB])
                            b_ = tb // NSB
                            srcb = [B_src[b_][kk] for kk in range(nk)]
                            dma_blk(sp, ab[n % 2], src_d, tb, 0, nk, True, srcb, [B_ab[n % 2]])
                            dma_blk(sp, rb[n % 2], xs_d, tb, dq * CPD, (dq + 1) * CPD, True, [B_xs[tb]], B_rb[n % 2])

                        issue_loads(0)
                        for n, (dq, tb) in enumerate(items):
                            if n + 1 < len(items):
                                issue_loads(n + 1)
                            Wsb, B_W = Wsbs[dq % NWB], B_Ws[dq % NWB]
                            b = tb // NSB
                            a_, ba = ab[n % 2], B_ab[n % 2]
                            r_, br_ = rb[n % 2], B_rb[n % 2]
                            tsl = slice(tb * 512, (tb + 1) * 512)
                            ps_, pq_, bst = pss[nst % 2], psq[nst % 2], B_pst[nst % 2]
                            nst += 1
                            prev = None
                            for c8 in range(CPD + 1):
                                if c8 < CPD:
                                    cc = dq * CPD + c8
                                    ac, bac = acc[nacc % 3], B_acc[nacc % 3]
                                    s_, bs_ = sq[nacc % 2], B_sq[nacc % 2]
                                    nacc += 1
                                    for kk in range(nk):
                                        MM(ac[:], Wsb[:, kk, c8 * 128:(c8 + 1) * 128], a_[:, kk, :], kk == 0, kk == nk - 1,
                                           [B_W, ba], [bac], tick=(kk == nk - 1))
                                if prev is not None:
                                    pc8, ps_sq, pbs = prev
                                    MM(ps_[:], ones_f[:], r_[:, pc8, :], pc8 == 0, pc8 == CPD - 1, [B_const, br_[pc8]], [bst], tick=False)
                                    MM(pq_[:], ones_f[:], ps_sq[:], pc8 == 0, pc8 == CPD - 1, [B_const, pbs], [bst], tick=True)
                                if c8 < CPD:
                                    STT(r_[:, c8, :], ac[:], col(f"{gname}{l}{b}", cc), r_[:, c8, :], ALU.mult, ALU.add,
                                        [bac, br_[c8], B_tab], [br_[c8]])
                                    ACT(s_[:], r_[:, c8, :], AF.Square, [br_[c8]], [bs_])
                                    prev = (c8, s_, bs_)
                            sg_, bsg = stg[0], B_stg[0]
                            if dq == 0:
                                CP(act, sg_[:, 0, :], ps_[:], [bst], [bsg])
                                CP(dve, sg_[:, 1, :], pq_[:], [bst], [bsg])
                            else:
                                TT(dve, sg_[:, 0, :], sg_[:, 0, :], ps_[:], ALU.add, [bst, bsg], [bsg])
                                TT(dve, sg_[:, 1, :], sg_[:, 1, :], pq_[:], ALU.add, [bst, bsg], [bsg])
                            k.dma(sp, stat_d[tb], sg_[:], reads=[bsg], writes=[B_st[tb]])
                            if n + 1 < len(items) and items[n + 1][0] > 0:
                                tbn = items[n + 1][1]
                                k.dma(sp, sg_[:], stat_d[tbn], reads=[B_st[tbn]], writes=[bsg])
                            dma_blk(sp, r_, xs_d, tb, dq * CPD, (dq + 1) * CPD, False, br_, [B_xs[tb]])
                            if NWB == 1 and n + 1 < len(items) and items[n + 1][1] == 0:
                                dq2 = items[n + 1][0]
                                load_w_bf(Wsbs[0], W[:, dq2 * CPD * 128:(dq2 + 1) * CPD * 128], nk, 4, [], B_Ws[0])
                    with scope() as sc:
                        r16s = [sc.sb(f"r16{j}", [128, KC, 512]) for j in range(2)]
                        B_r16s = [[Buf() for _ in range(KC)] for _ in range(2)]
                        h16s = [sc.sb(f"h16{j}", [128, KC, 512], BF16) for j in range(2)]
                        B_h16s = [[Buf() for _ in range(KC)] for _ in range(2)]
                        mt = sc.sb("mt", [128, 512])
                        rstd = sc.sb("rstd", [128, 512])
                        nmr = sc.sb("nmr", [128, 512])
                        B_ms = Buf()
                        if final:
                            otok = [sc.sb(f"otok{j}", [128, DM]) for j in range(2)]
                            B_ot = [Buf() for _ in range(2)]
                            tpo = [sc.ps(f"tpo{j}", [128, 4, 128]) for j in range(2)]
                            B_tpo = [Buf() for _ in range(2)]
                        stb = [sc.sb(f"stb{j}", [128, 2, 512]) for j in range(2)]
                        B_stb = [Buf() for _ in range(2)]
                        dma_blk(sp, r16s[0], xs_d, 0, 0, KC, True, [B_xs[0]], B_r16s[0])
                        k.dma(sp, stb[0][:], stat_d[0], reads=[B_st[0]], writes=[B_stb[0]])
                        for tb in range(T // 512):
                            b = tb // NSB
                            tsl = slice(tb * 512, (tb + 1) * 512)
                            r16, B_r16 = r16s[tb % 2], B_r16s[tb % 2]
                            h16, B_h16 = h16s[tb % 2], B_h16s[tb % 2]
                            if tb + 1 < T // 512:
                                dma_blk(sp, r16s[(tb + 1) % 2], xs_d, tb + 1, 0, KC, True, [B_xs[tb + 1]], B_r16s[(tb + 1) % 2])
                                k.dma(sp, stb[(tb + 1) % 2][:], stat_d[tb + 1], reads=[B_st[tb + 1]], writes=[B_stb[(tb + 1) % 2]])
                            ssum_, ssq_, bstb = stb[tb % 2][:, 0, :], stb[tb % 2][:, 1, :], B_stb[tb % 2]
                            TS(dve, mt[:], ssum_, 1.0 / DM, None, ALU.mult, ALU.bypass, [bstb], [B_ms])
                            TT(dve, nmr[:], mt[:], mt[:], ALU.mult, [B_ms], [B_ms])
                            STT(rstd[:], ssq_, 1.0 / DM, nmr[:], ALU.mult, ALU.subtract, [bstb, B_ms], [B_ms])
                            ACT(rstd[:], rstd[:], AF.Sqrt, [B_ms, B_tab], [B_ms], bias=col("eps"))
                            RECIP(rstd[:], rstd[:], [B_ms], [B_ms])
                            STT(nmr[:], mt[:], -1.0, rstd[:], ALU.mult, ALU.mult, [B_ms], [B_ms])
                            for cc in range(KC):
                                rc = r16[:, cc, :]
                                TT(dve, rc, rc, rstd[:], ALU.mult, [B_r16[cc], B_ms], [B_r16[cc]])
                                TT(pool, rc, rc, nmr[:], ALU.add, [B_r16[cc], B_ms], [B_r16[cc]])
                                if final:
                                    ACT(rc, rc, AF.Identity, [B_r16[cc], B_tab], [B_r16[cc]], bias=col(f"nb{l}{i}", cc), scale=col(f"ng{l}{i}", cc))
                                else:
                                    ACT(h16[:, cc, :], rc, AF.Identity, [B_r16[cc], B_tab], [B_h16[cc]],
                                        bias=col(f"hb{l}{i}{b}", cc), scale=col(f"hs{l}{i}{b}", cc))
                                    ACT(rc, rc, AF.Identity, [B_r16[cc], B_tab], [B_r16[cc]], bias=col(f"xsb{l}{i}", cc), scale=col(f"xss{l}{i}", cc))
                            if final:
                                for q in range(4):
                                    tt = tb * 4 + q
                                    ot, bot = otok[tt % 2], B_ot[tt % 2]
                                    for c4 in range(4):
                                        p_, bp = tpo[c4 % 2], B_tpo[c4 % 2]
                                        for ii in range(4):
                                            cc = c4 * 4 + ii
                                            TR(p_[:, ii, :], r16[:, cc, q * 128:(q + 1) * 128], ident[:], [B_r16[cc], B_const], [bp], tick=(ii == 3))
                                        CP(act if c4 % 2 else dve, ot[:, c4 * 512:(c4 + 1) * 512], p_[:].rearrange("p a b -> p (a b)"), [bp], [bot])
                                    k.dma(sp, out[tt * 128:(tt + 1) * 128, :], ot[:], reads=[bot])
                            else:
                                dma_blk(sp, r16, xs_d, tb, 0, KC, False, B_r16, [B_xs[tb]])
                                dma_blk(sp, h16, hT_d, tb, 0, KC, False, B_h16, [B_hT[tb]])

            def ffn1(l):
                JG = 4
                with scope() as sc:
                    hseq = sc.sb("hseq", [128, NSB, KC, 512], BF16)
                    B_hs = Buf()
                    wg = [sc.sb(f"wg{j}", [128, KC, JG * 128], BF16) for j in range(2)]
                    wu = [sc.sb(f"wu{j}", [128, KC, JG * 128], BF16) for j in range(2)]
                    B_wg = [Buf() for _ in range(2)]
                    sg = [sc.sb(f"sg{j}", [128, 512]) for j in range(2)]
                    B_sg = [Buf() for _ in range(2)]
                    aj = [sc.sb(f"aj{j}", [128, S], BF16) for j in range(2)]
                    B_aj = [Buf() for _ in range(2)]
                    pg = [sc.ps(f"pg{j}", [128, 512]) for j in range(2)]
                    pu = [sc.ps(f"pu{j}", [128, 512]) for j in range(2)]
                    B_pg = [Buf() for _ in range(2)]
                    B_pu = [Buf() for _ in range(2)]
                    nw = nj = np_ = 0
                    for b in range(NB):
                        for tbk in range(NSB):
                            tb = b * NSB + tbk
                            dma_blk(sp, hseq[:, tbk], hT_d, tb, 0, KC, True, [B_hT[tb]], [B_hs])
                        for jg in range(FC // JG):
                            wg_, wu_, bw = wg[nw % 2], wu[nw % 2], B_wg[nw % 2]
                            nw += 1
                            csl = slice(jg * JG * 128, (jg + 1) * JG * 128)
                            load_w_bf(wg_, ffn_w_gate[l][:, csl], KC, 8, [], bw)
                            load_w_bf(wu_, ffn_w_up[l][:, csl], KC, 8, [], bw)
                            for jj in range(JG):
                                j = jg * JG + jj
                                a_, ba = aj[nj % 2], B_aj[nj % 2]
                                nj += 1
                                for tbk in range(NSB):
                                    g_, bg = pg[np_ % 2], B_pg[np_ % 2]
                                    u_, bu = pu[np_ % 2], B_pu[np_ % 2]
                                    s_, bs_ = sg[np_ % 2], B_sg[np_ % 2]
                                    np_ += 1
                                    tsl = slice(tbk * 512, (tbk + 1) * 512)
                                    for kk in range(KC):
                                        MM(g_[:], wg_[:, kk, jj * 128:(jj + 1) * 128], hseq[:, tbk, kk, :], kk == 0, kk == KC - 1,
                                           [bw, B_hs], [bg], tick=(kk == KC - 1))
                                    for kk in range(KC):
                                        MM(u_[:], wu_[:, kk, jj * 128:(jj + 1) * 128], hseq[:, tbk, kk, :], kk == 0, kk == KC - 1,
                                           [bw, B_hs], [bu], tick=(kk == KC - 1))
                                    ACT(s_[:], g_[:], AF.Silu, [bg], [bs_])
                                    TT(dve, a_[:, tsl], s_[:], u_[:], ALU.mult, [bs_, bu], [ba])
                                dma_rows(sp, aT_d, b, j, a_, [ba], [B_aT[b][j]])

            def mixer_ab():
                w_in = ab_w_in[0]
                with scope() as sc:
                    hseq = sc.sb("hseq", [128, NSB, KC, 512], BF16)
                    B_hs = Buf()
                    wr = sc.sb("wr", [128, 8, 128], BF16)
                    wi = sc.sb("wi", [128, 8, 128], BF16)
                    wsT = sc.sb("wsT", [128, 8, 128], BF16)
                    bsrow = sc.sb("bsrow", [1, 1024], BF16)
                    onesrow = sc.sb("onesrow", [1, 128], BF16)
                    vng = sc.sb("vng", [128, 1024])
                    vnb = sc.sb("vnb", [128, 1024])
                    B_c0 = Buf("l0const")
                    k.dma(pool, wr[:], ab_w_r[0].rearrange("h d e -> d h e"), writes=[B_c0])
                    k.dma(pool, wi[:], ab_w_i[0].rearrange("h d e -> d h e"), writes=[B_c0])
                    k.dma(pool, bsrow[:], ab_b_s[0].rearrange("g t -> (g t)").rearrange("(o n) -> o n", o=1), writes=[B_c0])
                    MEMSET(dve, onesrow[:], 1.0, [B_c0])
                    k.dma(sp, vng[:], ab_vnorm_g[0].partition_broadcast(128), writes=[B_c0])
                    k.dma(sp, vnb[:], ab_vnorm_b[0].partition_broadcast(128), writes=[B_c0])
                    with scope() as s2:
                        wsf = s2.sb("wsf", [128, 8, 128])
                        B_wsf = Buf()
                        tps = s2.ps("tps", [128, 4, 128])
                        B_tps = Buf()
                        k.dma(sp, wsf[:], ab_w_s[0].rearrange("g t s -> t g s"), writes=[B_wsf])
                        for g4 in range(2):
                            for ii in range(4):
                                TR(tps[:, ii, :], wsf[:, g4 * 4 + ii, :], ident[:], [B_wsf, B_const], [B_tps], tick=(ii == 3))
                            CP(dve, wsT[:, g4 * 4:(g4 + 1) * 4, :], tps[:], [B_tps], [B_c0])
                        MEMSET(dve, wsT[64:128, :, 0:64], 0.0, [B_c0])

                    for b in range(NB):
                        for tbk in range(NSB):
                            tb = b * NSB + tbk
                            dma_blk(sp, hseq[:, tbk], hT_d, tb, 0, KC, True, [B_hT[tb]], [B_hs])
                        with scope() as s2:
                            wxa = [s2.sb(f"wxa{j}", [128, KC, 128], BF16) for j in range(2)]
                            wga = [s2.sb(f"wga{j}", [128, KC, 128], BF16) for j in range(2)]
                            B_wx = [Buf() for _ in range(2)]
                            xa = [s2.sb(f"xa{j}", [128, 3 + S]) for j in range(2)]
                            gg = [s2.sb(f"gg{j}", [128, S], BF16) for j in range(2)]
                            xc = [s2.sb(f"xc{j}", [128, S]) for j in range(2)]
                            xcb = [s2.sb(f"xcb{j}", [128, S], BF16) for j in range(2)]
                            rr = [s2.sb(f"rr{j}", [128, S]) for j in range(2)]
                            iu = [s2.sb(f"iu{j}", [128, S]) for j in range(2)]
                            aa = [s2.sb(f"aa{j}", [128, S]) for j in range(2)]
                            ya = [s2.sb(f"ya{j}", [128, S], BF16) for j in range(2)]
                            B_xa, B_gg, B_xc, B_xcb, B_rr, B_iu, B_aa = ([Buf() for _ in range(2)] for _ in range(7))
                            B_ya = [Buf() for _ in range(2)]
                            pxa = [s2.ps(f"pxa{j}", [128, 512]) for j in range(2)]
                            pga = [s2.ps(f"pga{j}", [128, 512]) for j in range(2)]
                            pr = [s2.ps(f"pr{j}", [128, 512]) for j in range(2)]
                            pi = [s2.ps(f"pi{j}", [128, 512]) for j in range(2)]
                            B_pxa, B_pga, B_pr, B_pi = ([Buf() for _ in range(2)] for _ in range(4))
                            for j in range(2):
                                MEMSET(dve, xa[j][:, 0:3], 0.0, [B_xa[j]])
                            cntl = {"px": 0, "pg": 0}

                            def stP(h):
                                st_ = h % 2
                                wx_, wg_, bw = wxa[st_], wga[st_], B_wx[st_]
                                load_w_bf(wx_, w_in[:, h * 128:(h + 1) * 128], KC, 16, [], bw)
                                load_w_bf(wg_, w_in[:, 1024 + h * 128:1024 + (h + 1) * 128], KC, 16, [], bw)
                                for tbk in range(NSB):
                                    tsl = slice(tbk * 512, (tbk + 1) * 512)
                                    j_ = cntl["px"] % 2
                                    cntl["px"] += 1
                                    p1, b1 = pxa[j_], B_pxa[j_]
                                    p2, b2 = pga[j_], B_pga[j_]
                                    for kk in range(KC):
                                        MM(p1[:], wx_[:, kk, :], hseq[:, tbk, kk, :], kk == 0, kk == KC - 1, [bw, B_hs], [b1], tick=(kk == KC - 1))
                                    for kk in range(KC):
                                        MM(p2[:], wg_[:, kk, :], hseq[:, tbk, kk, :], kk == 0, kk == KC - 1, [bw, B_hs], [b2], tick=(kk == KC - 1))
                                    ACT(xa[st_][:, 3 + tbk * 512:3 + (tbk + 1) * 512], p1[:], AF.Copy, [b1], [B_xa[st_]])
                                    ACT(gg[st_][:, tsl], p2[:], AF.Gelu_apprx_tanh, [b2], [B_gg[st_]])

                            def stE(h):
                                st_ = h % 2
                                xa_, gg_, xc_, xcb_, rr_, iu_, aa_ = xa[st_], gg[st_], xc[st_], xcb[st_], rr[st_], iu[st_], aa[st_]
                                bxa, bgg, bxc, bxcb, brr, biu, baa = B_xa[st_], B_gg[st_], B_xc[st_], B_xcb[st_], B_rr[st_], B_iu[st_], B_aa[st_]
                                ACT(xc_[:], xa_[:, 3:3 + S], AF.Identity, [bxa, B_tab], [bxc], scale=col("cw3", h), bias=col("cb", h))
                                for j in (2, 1, 0):
                                    STT(xc_[:], xa_[:, j:j + S], col(f"cw{j}", h), xc_[:], ALU.mult, ALU.add, [bxa, bxc, B_tab], [bxc])
                                CP(pool, xcb_[:], xc_[:], [bxc], [bxcb])
                                for tbk in range(NSB):
                                    tsl = slice(tbk * 512, (tbk + 1) * 512)
                                    j_ = cntl["pg"] % 2
                                    cntl["pg"] += 1
                                    p1, b1 = pr[j_], B_pr[j_]
                                    p2, b2 = pi[j_], B_pi[j_]
                                    MM(p1[:], wr[:, h, :], xcb_[:, tsl], True, True, [B_c0, bxcb], [b1], tick=True)
                                    MM(p2[:], wi[:, h, :], xcb_[:, tsl], True, True, [B_c0, bxcb], [b2], tick=True)
                                    ACT(rr_[:, tsl], p1[:], AF.Sigmoid, [b1, B_tab], [brr], bias=col("br", h))
                                    ACT(iu_[:, tsl], p2[:], AF.Sigmoid, [b2, B_tab], [biu], bias=col("bi", h))
                                ACT(aa_[:], rr_[:], AF.Exp, [brr, B_tab], [baa], scale=col("lam", h))
                                ACT(rr_[:], rr_[:], AF.Exp, [brr, B_tab], [brr], scale=col("lam2", h))
                                ACT(rr_[:], rr_[:], AF.Sqrt, [brr, B_tab], [brr], scale=-1.0, bias=col("one"))
                                TT(dve, iu_[:], iu_[:], xc_[:], ALU.mult, [biu, bxc], [biu])
                                TT(dve, iu_[:], iu_[:], rr_[:], ALU.mult, [biu, brr], [biu])
                                SCAN(xa_[:, 3:3 + S], aa_[:], iu_[:], col("zero"), [baa, biu, B_tab], [bxa])
                                y_, by = ya[st_], B_ya[st_]
                                TT(dve, y_[:], xa_[:, 3:3 + S], gg_[:], ALU.mult, [bxa, bgg], [by])
                                dma_rows(sp, yT_d, b, h, y_, [by], [B_yT[b][h]])

                            stP(0)
                            for h in range(8):
                                if h + 1 < 8:
                                    stP(h + 1)
                                stE(h)
                        with scope() as s2:
                            wub = s2.sb("wub", [128, KC, 1024], BF16)
                            wvb = s2.sb("wvb", [128, KC, 1024], BF16)
                            B_wb = Buf()
                            load_w_bf(wub, w_in[:, 2048:3072], KC, 4, [], B_wb)
                            load_w_bf(wvb, w_in[:, 3072:4096], KC, 4, [], B_wb)
                            ug = s2.sb("ug", [128, 8, 512])
                            B_ug = Buf()
                            vf = [s2.sb(f"vf{j}", [128, 1024]) for j in range(2)]
                            vt = [s2.sb(f"vt{j}", [128, 1024], BF16) for j in range(2)]
                            B_vf = [Buf() for _ in range(2)]
                            B_vt = [Buf() for _ in range(2)]
                            st6 = s2.sb("st6", [128, 2, 6])
                            mv = s2.sb("mv", [128, 2])
                            B_mv = Buf()
                            yb = [s2.sb(f"yb{j}", [128, 8, 512], BF16) for j in range(2)]
                            B_yb = [Buf() for _ in range(2)]
                            pu = [s2.ps(f"pu{j}", [128, 512]) for j in range(2)]
                            B_pu = [Buf() for _ in range(2)]
                            pv = [s2.ps(f"pv{j}", [128, 1024]) for j in range(2)]
                            B_pv = [Buf() for _ in range(2)]
                            psv = s2.ps("psv", [128, 8, 128])
                            B_psv = Buf()
                            npu = 0
                            for tbk in range(NSB):
                                tsl = slice(tbk * 512, (tbk + 1) * 512)
                                y_, by = yb[tbk % 2], B_yb[tbk % 2]
                                for g in range(8):
                                    p_, bp = pu[npu % 2], B_pu[npu % 2]
                                    npu += 1
                                    for kk in range(KC):
                                        MM(p_[:], wub[:, kk, g * 128:(g + 1) * 128], hseq[:, tbk, kk, :], kk == 0, kk == KC - 1,
                                           [B_wb, B_hs], [bp], tick=(kk == KC - 1))
                                    ACT(ug[:, g, :], p_[:], AF.Gelu_apprx_tanh, [bp], [B_ug])
                                def vmm(tt_):
                                    p__, bp__ = pv[tt_ % 2], B_pv[tt_ % 2]
                                    for hf in range(2):
                                        for kk in range(KC):
                                            MM(p__[:, hf * 512:(hf + 1) * 512], hseq[:, tt_ // 4, kk, (tt_ % 4) * 128:(tt_ % 4 + 1) * 128],
                                               wvb[:, kk, hf * 512:(hf + 1) * 512], kk == 0, kk == KC - 1, [B_wb, B_hs], [bp__],
                                               tick=(kk == KC - 1))

                                def stG(tt_):
                                    p__, bp__ = pv[tt_ % 2], B_pv[tt_ % 2]
                                    v__, bv__ = vf[tt_ % 2], B_vf[tt_ % 2]
                                    for hf in range(2):
                                        ACT(v__[:, hf * 512:(hf + 1) * 512], p__[:, hf * 512:(hf + 1) * 512], AF.Gelu_apprx_tanh, [bp__], [bv__])

                                def stN(tt_):
                                    v__, bv__ = vf[tt_ % 2], B_vf[tt_ % 2]
                                    vt__, bvt__ = vt[tt_ % 2], B_vt[tt_ % 2]
                                    for hf in range(2):
                                        BNS(st6[:, hf, :], v__[:, hf * 512:(hf + 1) * 512], [bv__], [B_mv])
                                    BNA(mv[:], st6[:].rearrange("p a b -> p (a b)"), [B_mv], [B_mv])
                                    ACT(mv[:, 1:2], mv[:, 1:2], AF.Sqrt, [B_mv, B_tab], [B_mv], bias=col("eps"))
                                    RECIP(mv[:, 1:2], mv[:, 1:2], [B_mv], [B_mv])
                                    TS(dve, v__[:], v__[:], mv[:, 0:1], None, ALU.subtract, ALU.bypass, [bv__, B_mv], [bv__])
                                    TS(dve, v__[:], v__[:], mv[:, 1:2], None, ALU.mult, ALU.bypass, [bv__, B_mv], [bv__])
                                    TT(pool, v__[:], v__[:], vng[:], ALU.mult, [bv__, B_c0], [bv__])
                                    TT(pool, vt__[:], v__[:], vnb[:], ALU.add, [bv__, B_c0], [bvt__])

                                if tbk == 0:
                                    vmm(0)
                                    if NST > 1:
                                        vmm(1)
                                    stG(0)
                                    stN(0)
                                for q in range(4):
                                    tt = tbk * 4 + q
                                    vt_, bvt = vt[tt % 2], B_vt[tt % 2]
                                    if tt + 1 < NST:
                                        stG(tt + 1)
                                    if tt + 2 < NST:
                                        vmm(tt + 2)
                                    if tt + 1 < NST:
                                        stN(tt + 1)
                                    for g in range(8):
                                        MM(psv[:, g, :], vt_[:, g * 128:(g + 1) * 128], wsT[:, g, :], True, False, [bvt, B_c0], [B_psv], tick=False)
                                        MM(psv[:, g, :], onesrow[0:1, :], bsrow[0:1, g * 128:(g + 1) * 128], False, True, [B_c0], [B_psv],
                                           tick=(g == 7))
                                    TT(dve, y_[:, :, q * 128:(q + 1) * 128], psv[:], ug[:, :, q * 128:(q + 1) * 128], ALU.mult,
                                       [B_psv, B_ug], [by])
                                dma_blk(sp, y_, yT_d, b * NSB + tbk, 8, 16, False, [by], [B_yT[b][8 + g] for g in range(8)])

            def mixer_cd():
                w_in = cd_w_in[0]
                zscale = 128 ** -0.5
                with scope() as sc:
                    hseq = sc.sb("hseq", [128, NSB, KC, 512], BF16)
                    B_hs = Buf()
                    wpl = sc.sb("wpl", [128, 4, 2, 256], BF16)
                    tri = sc.sb("tri", [128, 128])
                    ones_s = sc.sb("ones_s", [128, S], BF16)
                    invc = sc.sb("invc", [128, 16])
                    B_c1 = Buf("l1const")
                    k.dma(pool, wpl[:], cd_w_pool[0].rearrange("g (dc p) e -> p g dc e", p=128), writes=[B_c1])
                    MEMSET(pool, tri[:], 1.0, [B_c1])
                    k.op(pool, lambda: nc.gpsimd.affine_select(out=tri[:], in_=tri[:], compare_op=ALU.is_gt, fill=0.0,
                                                               base=0, pattern=[[-1, 128]], channel_multiplier=1), [B_c1], [B_c1])
                    MEMSET(pool, ones_s[:], 1.0, [B_c1])
                    for t_ in range(16):
                        MEMSET(dve, invc[:, t_:t_ + 1], 1.0 / (t_ + 1), [B_c1])
                    v_sb = sc.sb("v_sb", [128, NST, 1024], BF16)
                    B_v = Buf()
                    for b in range(NB):
                        for tbk in range(NSB):
                            tb = b * NSB + tbk
                            dma_blk(sp, hseq[:, tbk], hT_d, tb, 0, KC, True, [B_hT[tb]], [B_hs])
                        with scope() as s2:
                            wv = s2.sb("wv", [128, KC, 1024], BF16)
                            B_wv = Buf()
                            load_w_bf(wv, w_in[:, 2048:3072], KC, 4, [], B_wv)
                            pv = [s2.ps(f"pv{j}", [128, 1024]) for j in range(2)]
                            B_pv = [Buf() for _ in range(2)]
                            for tt in range(NST):
                                p_, bp = pv[tt % 2], B_pv[tt % 2]
                                for hf in range(2):
                                    for kk in range(KC):
                                        MM(p_[:, hf * 512:(hf + 1) * 512], hseq[:, tt // 4, kk, (tt % 4) * 128:(tt % 4 + 1) * 128],
                                           wv[:, kk, hf * 512:(hf + 1) * 512], kk == 0, kk == KC - 1, [B_wv, B_hs], [bp], tick=(kk == KC - 1))
                                ACT(v_sb[:, tt, 0:512], p_[:, 0:512], AF.Copy, [bp], [B_v])
                                ACT(v_sb[:, tt, 512:1024], p_[:, 512:1024], AF.Copy, [bp], [B_v])
                        with scope() as s2:
                            wp = [s2.sb(f"wp{j}", [128, KC, 128], BF16) for j in range(2)]
                            B_wp = [Buf() for _ in range(2)]
                            pp = s2.sb("pp", [128, 16 + S])
                            sa = s2.sb("sa", [128, 16 + S])
                            sbb = s2.sb("sbb", [128, 16 + S])
                            pt = s2.sb("pt", [128, S])
                            tmpc = s2.sb("tmpc", [128, 16])
                            pbf = s2.sb("pbf", [128, 2, S], BF16)
                            yd = [s2.sb(f"yd{j}", [128, S], BF16) for j in range(2)]
                            B_pp, B_sa, B_sbb, B_pt, B_pbf = [Buf() for _ in range(5)]
                            B_yd = [Buf() for _ in range(2)]
                            ppj = [s2.ps(f"ppj{j}", [128, 512]) for j in range(2)]
                            B_ppj = [Buf() for _ in range(2)]
                            pyd = [s2.ps(f"pyd{j}", [128, 512]) for j in range(2)]
                            B_pyd = [Buf() for _ in range(2)]
                            MEMSET(dve, pp[:, 0:16], 0.0, [B_pp])
                            MEMSET(dve, sa[:, 0:16], 0.0, [B_sa])
                            MEMSET(dve, sbb[:, 0:16], 0.0, [B_sbb])
                            npp = nyd = 0
                            for pc in range(8):
                                g = pc // 2
                                w = POOL_W[g]
                                w_, bw = wp[pc % 2], B_wp[pc % 2]
                                load_w_bf(w_, w_in[:, 3072 + pc * 128:3072 + (pc + 1) * 128], KC, 16, [], bw)
                                for tbk in range(NSB):
                                    p_, bp = ppj[npp % 2], B_ppj[npp % 2]
                                    npp += 1
                                    for kk in range(KC):
                                        MM(p_[:], w_[:, kk, :], hseq[:, tbk, kk, :], kk == 0, kk == KC - 1,
                                           [bw, B_hs], [bp], tick=(kk == KC - 1))
                                    ACT(pp[:, 16 + tbk * 512:16 + (tbk + 1) * 512], p_[:], AF.Copy, [bp], [B_pp])
                                TT(dve, sa[:, 16:], pp[:, 16:], pp[:, 15:15 + S], ALU.add, [B_pp], [B_sa])
                                cur, bcur = sa, B_sa
                                oth, both = sbb, B_sbb
                                sh = 2
                                while sh < w:
                                    TT(dve, oth[:, 16:], cur[:, 16:], cur[:, 16 - sh:16 - sh + S], ALU.add, [bcur], [both])
                                    cur, bcur, oth, both = oth, both, cur, bcur
                                    sh *= 2
                                STT(pt[:], cur[:, 16:], 1.0 / w, pp[:, 16:], ALU.mult, ALU.subtract, [bcur, B_pp], [B_pt])
                                TT(dve, tmpc[:, 0:w - 1], cur[:, 16:16 + w - 1], invc[:, 0:w - 1], ALU.mult, [bcur, B_c1], [B_pt])
                                TT(dve, pt[:, 0:w - 1], tmpc[:, 0:w - 1], pp[:, 16:16 + w - 1], ALU.subtract, [B_pt, B_pp], [B_pt])
                                CP(pool, pbf[:, pc % 2, :], pt[:], [B_pt], [B_pbf])
                                if pc % 2 == 1:
                                    for ec in range(2):
                                        y_, by = yd[nyd % 2], B_yd[nyd % 2]
                                        nyd += 1
                                        for tbk in range(NSB):
                                            tsl = slice(tbk * 512, (tbk + 1) * 512)
                                            p_, bp = pyd[tbk % 2], B_pyd[tbk % 2]
                                            MM(p_[:], wpl[:, g, 0, ec * 128:(ec + 1) * 128], pbf[:, 0, tsl], True, False, [B_c1, B_pbf], [bp], tick=False)
                                            MM(p_[:], wpl[:, g, 1, ec * 128:(ec + 1) * 128], pbf[:, 1, tsl], False, True, [B_c1, B_pbf], [bp], tick=True)
                                            ACT(y_[:, tsl], p_[:], AF.Identity, [bp, B_tab], [by], scale=col("psc", g * 2 + ec))
                                        ch = 8 + g * 2 + ec
                                        dma_rows(sp, yT_d, b, ch, y_, [by], [B_yT[b][ch]])
                        with scope() as s2:
                            wq = [s2.sb(f"wq{j}", [128, KC, 128], BF16) for j in range(2)]
                            wk = [s2.sb(f"wk{j}", [128, KC, 128], BF16) for j in range(2)]
                            B_wq = [Buf() for _ in range(2)]
                            qT = s2.sb("qT", [128, S], BF16)
                            kT = s2.sb("kT", [128, S], BF16)
                            B_q, B_k = Buf(), Buf()
                            ee = [s2.sb(f"ee{j}", [128, S]) for j in range(2)]
                            ff = [s2.sb(f"ff{j}", [128, S]) for j in range(2)]
                            zs = [s2.sb(f"zs{j}", [128, S]) for j in range(2)]
                            negt = [s2.sb(f"negt{j}", [128, 1]) for j in range(2)]
                            wbf = [s2.sb(f"wbf{j}", [128, S], BF16) for j in range(2)]
                            wT = [s2.sb(f"wT{j}", [128, NST, 128], BF16) for j in range(2)]
                            B_ee, B_ff, B_zs, B_nt, B_wbf, B_wT = ([Buf() for _ in range(2)] for _ in range(6))
                            yc = [s2.sb(f"yc{j}", [128, S], BF16) for j in range(2)]
                            B_yc = [Buf() for _ in range(2)]
                            zps = [s2.ps(f"zps{j}", [128, 2, 512]) for j in range(2)]
                            B_z = [Buf() for _ in range(2)]
                            pqk = s2.ps("pqk", [128, 512])
                            B_pqk = Buf()
                            wtp = [s2.ps(f"wtp{j}", [128, 4, 128], BF16) for j in range(2)]
                            B_wtp = [Buf() for _ in range(2)]
                            ycp = s2.ps("ycp", [128, 128])
                            B_ycp = Buf()
                            cnt = {"z": 0, "wt": 0}
                            for h in range(8):
                                wq_, wk_, bw = wq[h % 2], wk[h % 2], B_wq[h % 2]
                                load_w_bf(wq_, w_in[:, h * 128:(h + 1) * 128], KC, 16, [], bw)
                                load_w_bf(wk_, w_in[:, 1024 + h * 128:1024 + (h + 1) * 128], KC, 16, [], bw)
                                for tbk in range(NSB):
                                    tsl = slice(tbk * 512, (tbk + 1) * 512)
                                    for kk in range(KC):
                                        MM(pqk[:], wq_[:, kk, :], hseq[:, tbk, kk, :], kk == 0, kk == KC - 1, [bw, B_hs], [B_pqk], tick=(kk == KC - 1))
                                    ACT(qT[:, tsl], pqk[:], AF.Copy, [B_pqk], [B_q])
                                    for kk in range(KC):
                                        MM(pqk[:], wk_[:, kk, :], hseq[:, tbk, kk, :], kk == 0, kk == KC - 1, [bw, B_hs], [B_pqk], tick=(kk == KC - 1))
                                    ACT(kT[:, tsl], pqk[:], AF.Copy, [B_pqk], [B_k])
                                y_, by = yc[h % 2], B_yc[h % 2]

                                def stA(qi):
                                    st_ = qi % 2
                                    nk_ = (qi + 1) * 128
                                    dsl = slice(qi * 128, (qi + 1) * 128)
                                    for hf in range((nk_ + 1023) // 1024):
                                        zp, bz = zps[cnt["z"] % 2], B_z[cnt["z"] % 2]
                                        cnt["z"] += 1
                                        k0h = hf * 1024
                                        k1h = min(nk_, k0h + 1024)
                                        nchh = (k1h - k0h + 511) // 512
                                        for kc in range(nchh):
                                            a0 = k0h + kc * 512
                                            a1 = min(k1h, a0 + 512)
                                            MM(zp[:, kc, 0:a1 - a0], qT[:, dsl], kT[:, a0:a1], True, True, [B_q, B_k], [bz], tick=(kc == nchh - 1))
                                        for kc in range(nchh):
                                            a0 = k0h + kc * 512
                                            a1 = min(k1h, a0 + 512)
                                            ACT(ee[st_][:, a0:a1], zp[:, kc, 0:a1 - a0], AF.Exp, [bz], [B_ee[st_]], scale=zscale)
                                            ACT(zs[st_][:, a0:a1], zp[:, kc, 0:a1 - a0], AF.Copy, [bz], [B_zs[st_]], scale=zscale)
                                    ACT(ee[st_][:, 0:nk_], ee[st_][:, 0:nk_], AF.Ln, [B_ee[st_], B_tab], [B_ee[st_]], bias=col("one"))

                                def stB(qi):
                                    st_ = qi % 2
                                    nk_ = (qi + 1) * 128
                                    dsl = slice(qi * 128, (qi + 1) * 128)
                                    e_, f_, z_, w_ = ee[st_], ff[st_], zs[st_], wbf[st_]
                                    TT(pool, e_[:, dsl], e_[:, dsl], tri[:], ALU.mult, [B_ee[st_], B_c1], [B_ee[st_]])
                                    TT(pool, z_[:, 0:nk_], z_[:, 0:nk_], e_[:, 0:nk_], ALU.subtract, [B_zs[st_], B_ee[st_]], [B_zs[st_]])
                                    SCAN(f_[:, 0:nk_], ones_s[:, 0:nk_], e_[:, 0:nk_], col("zero"), [B_ee[st_], B_c1, B_tab], [B_ff[st_]])
                                    TS(pool, negt[st_][:], f_[:, nk_ - 1:nk_], -1.0, 0.0, ALU.mult, ALU.add, [B_ff[st_]], [B_nt[st_]])
                                    TT(dve, f_[:, 0:nk_], f_[:, 0:nk_], z_[:, 0:nk_], ALU.add, [B_zs[st_], B_ff[st_]], [B_ff[st_]])
                                    ACT(w_[:, 0:nk_], f_[:, 0:nk_], AF.Exp, [B_ff[st_], B_nt[st_]], [B_wbf[st_]], bias=negt[st_][:])
                                    TT(pool, w_[:, dsl], w_[:, dsl], tri[:], ALU.mult, [B_wbf[st_], B_c1], [B_wbf[st_]])

                                def stC1(qi):
                                    st_ = qi % 2
                                    for kb4 in range((qi + 4) // 4):
                                        j_ = cnt["wt"] % 2
                                        cnt["wt"] += 1
                                        p_, bp = wtp[j_], B_wtp[j_]
                                        nb_ = min(4, qi + 1 - kb4 * 4)
                                        for ii in range(nb_):
                                            kb = kb4 * 4 + ii
                                            TR(p_[:, ii, :], wbf[st_][:, kb * 128:(kb + 1) * 128], ident_bf[:], [B_wbf[st_], B_const], [bp], tick=(ii == nb_ - 1))
                                        CP(act if j_ else dve, wT[st_][:, kb4 * 4:kb4 * 4 + nb_, :], p_[:, 0:nb_, :], [bp], [B_wT[st_]])

                                def stC2(qi):
                                    st_ = qi % 2
                                    dsl = slice(qi * 128, (qi + 1) * 128)
                                    for kb in range(qi + 1):
                                        MM(ycp[:], v_sb[:, kb, h * 128:(h + 1) * 128], wT[st_][:, kb, :], kb == 0, kb == qi, [B_v, B_wT[st_]], [B_ycp],
                                           tick=(kb == qi))
                                    CP(dve, y_[:, dsl], ycp[:], [B_ycp], [by])

                                stA(0)
                                if NST > 1:
                                    stA(1)
                                stB(0)
                                for qi in range(NST):
                                    stC1(qi)
                                    if qi + 2 < NST:
                                        stA(qi + 2)
                                    if qi + 1 < NST:
                                        stB(qi + 1)
                                    stC2(qi)
                                dma_rows(sp, yT_d, b, h, y_, [by], [B_yT[b][h]])

            stages = []
            if stop_after != "xt":
                mixer_ab()
                if stop_after != "mix0":
                    gemm_res_ln("o0", ab_w_out[0], KC, yT_d, B_yT, 0, 0, False)
                    if stop_after != "ln00":
                        ffn1(0)
                        if stop_after != "ffn10":
                            gemm_res_ln("d0", ffn_w_down[0], FC, aT_d, B_aT, 0, 1, False)
                            if stop_after != "ln01":
                                mixer_cd()
                                if stop_after != "mix1":
                                    gemm_res_ln("o1", cd_w_out[0], KC, yT_d, B_yT, 1, 0, False)
                                    ffn1(1)
                                    gemm_res_ln("d1", ffn_w_down[1], FC, aT_d, B_aT, 1, 1, True)

        except StopBuild:
            pass
        k.finish()
    return nc


INPUT_NAMES = ["x", "c", "ada_w", "ada_b", "norm_g", "norm_b", "ffn_w_gate", "ffn_w_up", "ffn_w_down",
               "ab_w_in", "ab_conv_w", "ab_conv_b", "ab_w_r", "ab_b_r", "ab_w_i", "ab_b_i", "ab_lambda",
               "ab_vnorm_g", "ab_vnorm_b", "ab_w_s", "ab_b_s", "ab_w_out",
               "cd_w_in", "cd_w_pool", "cd_pool_scale", "cd_w_out"]


def kernel(**inputs):
    x = np.ascontiguousarray(np.asarray(inputs["x"], dtype=np.float32))
    B, S, D = x.shape
    NB = B // NCORES
    nc = build(NB, S)
    shared = {n: np.ascontiguousarray(np.asarray(inputs[n], dtype=np.float32)) for n in INPUT_NAMES if n not in ("x", "c")}
    cfull = np.ascontiguousarray(np.asarray(inputs["c"], dtype=np.float32))
    in_maps = []
    for i in range(NCORES):
        m = dict(shared)
        m["x"] = x[i * NB:(i + 1) * NB].reshape(NB * S, D)
        m["c"] = cfull[i * NB:(i + 1) * NB]
        in_maps.append(m)
    res = run_bass_kernel_spmd(nc, in_maps, core_ids=list(range(NCORES)))
    outs = [np.asarray(r["out"]).reshape(NB, S, D) for r in res.results]
    return np.concatenate(outs, axis=0).astype(np.float32)
```

```python
from contextlib import ExitStack, contextmanager
import numpy as np
import concourse.bass as bass
import concourse.mybir as mybir
from concourse.bass_utils import run_bass_kernel_spmd

F32 = mybir.dt.float32
BF16 = mybir.dt.bfloat16
AF = mybir.ActivationFunctionType
ALU = mybir.AluOpType

DM = 2048
KC = DM // 128
DFF = 5632
FC = DFF // 128
NL = 2
NCORES = 8
ALPHA = (2 * NL) ** 0.25
EPS = 1e-5
POOL_W = (2, 4, 8, 16)
SEM_LIMIT = 30000
import os
XTF = os.environ.get('XTF', '').split(',')


class StopBuild(Exception):
    pass


class Buf:
    __slots__ = ("name", "w", "r")

    def __init__(self, name=""):
        self.name = name
        self.w = None
        self.r = []


class Eng:
    def __init__(self, name, h, is_pe=False):
        self.name = name
        self.h = h
        self.semidx = None
        self.tick = 0
        self.waited = {}
        self.prog = []
        self.pending = []
        self.is_pe = is_pe
        self.ring = []
        self.ring_pos = 0


class K:
    def __init__(self, nc, same_sync=True, ring=6):
        self.nc = nc
        self.same_sync = same_sync
        self.sems = []
        self.ring_n = ring
        self.live = {}
        self.halted = False

    def newsem(self, name):
        s = self.stack.enter_context(self.nc.semaphore(name))
        self.sems.append(s)
        return len(self.sems) - 1

    def start(self, stack):
        nc = self.nc
        self.stack = stack
        self.pe = Eng("pe", nc.tensor, is_pe=True)
        self.act = Eng("act", nc.scalar)
        self.dve = Eng("dve", nc.vector)
        self.pool = Eng("pool", nc.gpsimd)
        self.sp = Eng("sp", nc.sync)
        self.engs = [self.pe, self.act, self.dve, self.pool, self.sp]
        for e in self.engs:
            e.semidx = self.newsem(f"t_{e.name}0")
            e.nsem = 1
        for e in (self.sp, self.act, self.pool):
            e.ring = [[self.newsem(f"r_{e.name}{i}"), 0] for i in range(self.ring_n)]

    def _deps(self, e, reads, writes):
        deps = {}

        def need(tok):
            if tok is None:
                return
            s, v = tok[0], tok[1]
            if tok[2] is e and (e.is_pe or not self.same_sync):
                return
            if v is None:
                raise RuntimeError(f"dependency on pending op ({tok[2].name}) from {e.name}")
            if deps.get(s, 0) < v:
                deps[s] = v

        for b in reads:
            need(b.w)
        for b in writes:
            need(b.w)
            for t in b.r:
                need(t)
        waits = []
        for s, v in deps.items():
            if e.waited.get(s, 0) < v:
                e.waited[s] = v
                waits.append((s, v))
        return waits

    def _record(self, tok, reads, writes):
        for b in reads:
            b.r.append(tok)
        for b in writes:
            b.w = tok
            b.r = []

    def op(self, e, fn, reads=(), writes=(), tick=True):
        if self.halted:
            return
        waits = self._deps(e, reads, writes)
        if tick:
            if e.tick >= SEM_LIMIT:
                if e.pending:
                    raise RuntimeError("sem rollover with pending ops")
                e.semidx = self.newsem(f"t_{e.name}{e.nsem}")
                e.nsem += 1
                e.tick = 0
            e.tick += 1
            tok = [e.semidx, e.tick, e]
            for p in e.pending:
                p[0] = e.semidx
                p[1] = e.tick
            e.pending = []
            inc = (e.semidx, 1)
            self.live[e.semidx] = e.tick
        else:
            tok = [e.semidx, None, e]
            e.pending.append(tok)
            inc = None
        e.prog.append((waits, fn, inc))
        self._record(tok, reads, writes)

    def dma(self, q, out, in_, reads=(), writes=(), **kw):
        self.dma_multi(q, [(out, in_)], reads, writes, **kw)

    def dma_multi(self, q, pairs, reads=(), writes=(), **kw):
        if self.halted:
            return
        waits = self._deps(q, reads, writes)
        slot = q.ring[q.ring_pos]
        if slot[1] + 16 * len(pairs) > SEM_LIMIT:
            slot = q.ring[q.ring_pos] = [self.newsem(f"r_{q.name}x{len(self.sems)}"), 0]
        q.ring_pos = (q.ring_pos + 1) % len(q.ring)
        s = slot[0]
        if slot[1] > 0 and q.waited.get(s, 0) < slot[1]:
            q.waited[s] = slot[1]
            waits.append((s, slot[1]))
        for i, (o, i_) in enumerate(pairs):
            slot[1] += 16

            def fn(o=o, i_=i_):
                return q.h.dma_start(out=o, in_=i_, **kw)
            q.prog.append((waits if i == 0 else [], fn, (s, 16)))
        self.live[s] = slot[1]
        tok = [s, slot[1], None]
        self._record(tok, reads, writes)

    def barrier(self):
        for e in self.engs:
            if e.pending:
                raise RuntimeError(f"barrier with pending ops on {e.name}")
        for e in self.engs:
            waits = []
            for s, v in self.live.items():
                if e.waited.get(s, 0) < v:
                    e.waited[s] = v
                    waits.append((s, v))
            if waits:
                e.prog.append((waits, None, None))

    def finish(self):
        nc = self.nc
        sems = self.sems
        self.barrier()

        def replay(e):
            def run(h):
                for waits, fn, inc in e.prog:
                    for s, v in waits:
                        h.wait_ge(sems[s], v)
                    if fn is None:
                        continue
                    ins = fn()
                    if inc is not None:
                        ins.then_inc(sems[inc[0]], inc[1])
            return run

        with nc.Block() as block:
            block.tensor(replay(self.pe))
            block.scalar(replay(self.act))
            block.vector(replay(self.dve))
            block.gpsimd(replay(self.pool))
            block.sync(replay(self.sp))


def build(NB, S, dbg=False, stop_after=None, same_sync=True):
    nc = bass.Bass("TRN2", target_bir_lowering=False)
    T = NB * S
    NSB = S // 512
    NST = S // 128
    skind = "ExternalOutput" if dbg else "Internal"

    def din(name, shape):
        return nc.dram_tensor(name, list(shape), F32, kind="ExternalInput").ap()

    x = din("x", [T, DM])
    c = din("c", [NB, DM])
    ada_w = din("ada_w", [NL, DM, 6 * DM])
    ada_b = din("ada_b", [NL, 6 * DM])
    norm_g = din("norm_g", [NL, 2, DM])
    norm_b = din("norm_b", [NL, 2, DM])
    ffn_w_gate = din("ffn_w_gate", [NL, DM, DFF])
    ffn_w_up = din("ffn_w_up", [NL, DM, DFF])
    ffn_w_down = din("ffn_w_down", [NL, DFF, DM])
    ab_w_in = din("ab_w_in", [1, DM, 4096])
    ab_conv_w = din("ab_conv_w", [1, 4, 1024])
    ab_conv_b = din("ab_conv_b", [1, 1024])
    ab_w_r = din("ab_w_r", [1, 8, 128, 128])
    ab_b_r = din("ab_b_r", [1, 8, 128])
    ab_w_i = din("ab_w_i", [1, 8, 128, 128])
    ab_b_i = din("ab_b_i", [1, 8, 128])
    ab_lambda = din("ab_lambda", [1, 1024])
    ab_vnorm_g = din("ab_vnorm_g", [1, 1024])
    ab_vnorm_b = din("ab_vnorm_b", [1, 1024])
    ab_w_s = din("ab_w_s", [1, 8, 128, 128])
    ab_b_s = din("ab_b_s", [1, 8, 128])
    ab_w_out = din("ab_w_out", [1, DM, DM])
    cd_w_in = din("cd_w_in", [1, DM, 4096])
    cd_w_pool = din("cd_w_pool", [1, 4, 256, 256])
    cd_pool_scale = din("cd_pool_scale", [1, 1024])
    cd_w_out = din("cd_w_out", [1, DM, DM])
    out = nc.dram_tensor("out", [T, DM], F32, kind="ExternalOutput").ap()

    xs_d = nc.dram_tensor("xs_d", [T // 512, 128, KC, 512], F32, kind=skind).ap()
    hT_d = nc.dram_tensor("hT_d", [T // 512, 128, KC, 512], BF16, kind=skind).ap()
    yT_d = nc.dram_tensor("yT_d", [T // 512, 128, KC, 512], BF16, kind=skind).ap()
    aT_d = nc.dram_tensor("aT_d", [T // 512, 128, FC, 512], BF16, kind=skind).ap()
    stat_d = nc.dram_tensor("stat_d", [T // 512, 128, 2, 512], F32, kind="Internal").ap()
    B_xs = [Buf(f"xs{i}") for i in range(T // 512)]
    B_hT = [Buf(f"hT{i}") for i in range(T // 512)]
    B_yT = [[Buf() for _ in range(KC)] for _ in range(NB)]
    B_aT = [[Buf() for _ in range(FC)] for _ in range(NB)]

    k = K(nc, same_sync=same_sync)
    pe, act, dve, pool, sp = None, None, None, None, None

    with ExitStack() as st:
        k.start(st)
        pe, act, dve, pool, sp = k.pe, k.act, k.dve, k.pool, k.sp

        try:
            uniq = [0]

            @contextmanager
            def scope():
                with ExitStack() as s2:
                    class Sc:
                        def sb(self, name, shape, dt=F32):
                            uniq[0] += 1
                            return s2.enter_context(nc.sbuf_tensor(f"{name}_{uniq[0]}", list(shape), dt))

                        def ps(self, name, shape, dt=F32):
                            uniq[0] += 1
                            return s2.enter_context(nc.psum_tensor(f"{name}_{uniq[0]}", list(shape), dt))
                    yield Sc()
                    k.barrier()

            def ACT(out_, in_, func, reads, writes, bias=None, scale=1.0):
                if bias is None:
                    k.op(act, lambda: nc.scalar.activation(out=out_, in_=in_, func=func, scale=scale), reads, writes)
                else:
                    k.op(act, lambda: nc.scalar.activation(out=out_, in_=in_, func=func, bias=bias, scale=scale), reads, writes)

            def MM(out_, lhsT, rhs, start, stop, reads, writes, tick):
                k.op(pe, lambda: nc.tensor.matmul(out_, lhsT=lhsT, rhs=rhs, start=start, stop=stop, skip_group_check=True),
                     reads, writes, tick=tick)

            def TR(out_, in_, ident, reads, writes, tick):
                k.op(pe, lambda: nc.tensor.transpose(out_, in_, ident), reads, writes, tick=tick)

            def TT(e, out_, in0, in1, op_, reads, writes):
                k.op(e, lambda: e.h.tensor_tensor(out=out_, in0=in0, in1=in1, op=op_), reads, writes)

            def TS(e, out_, in0, s1, s2, op0, op1, reads, writes):
                k.op(e, lambda: e.h.tensor_scalar(out=out_, in0=in0, scalar1=s1, scalar2=s2, op0=op0, op1=op1), reads, writes)

            def STT(out_, in0, scalar, in1, op0, op1, reads, writes):
                k.op(dve, lambda: nc.vector.scalar_tensor_tensor(out=out_, in0=in0, scalar=scalar, in1=in1, op0=op0, op1=op1),
                     reads, writes)

            def CP(e, out_, in_, reads, writes):
                if e is act:
                    k.op(e, lambda: nc.scalar.activation(out=out_, in_=in_, func=AF.Copy), reads, writes)
                else:
                    k.op(e, lambda: e.h.tensor_copy(out=out_, in_=in_), reads, writes)

            def SCAN(out_, d0, d1, init, reads, writes):
                k.op(dve, lambda: nc.vector.tensor_tensor_scan(out=out_, data0=d0, data1=d1, initial=init, op0=ALU.mult, op1=ALU.add),
                     reads, writes)

            def RECIP(out_, in_, reads, writes):
                k.op(dve, lambda: nc.vector.reciprocal(out=out_, in_=in_), reads, writes)

            def BNS(out_, in_, reads, writes):
                k.op(dve, lambda: nc.vector.bn_stats(out=out_, in_=in_), reads, writes)

            def BNA(out_, in_, reads, writes):
                k.op(dve, lambda: nc.vector.bn_aggr(out=out_, in_=in_), reads, writes)

            def MEMSET(e, ap, val, writes):
                k.op(e, lambda: e.h.memset(ap, val), [], writes)

            def dma_blk(q, sb_tile, X_d, tb, k0, k1, load, reads, writes, kg=16):
                pairs = []
                for a0 in range(k0, k1, kg):
                    a1 = min(k1, a0 + kg)
                    d = X_d[tb, :, a0:a1, :]
                    s_ = sb_tile[:, a0 - k0:a1 - k0, :]
                    pairs.append((s_, d) if load else (d, s_))
                k.dma_multi(q, pairs, reads, writes)

            def dma_rows(q, X_d, b, ch, sb_row, reads, writes):
                pairs = [(X_d[b * NSB + tbk, :, ch, :], sb_row[:, tbk * 512:(tbk + 1) * 512]) for tbk in range(NSB)]
                k.dma_multi(q, pairs, reads, writes)

            def checkpoint(name):
                if stop_after == name:
                    k.halted = True

            sbp = lambda name, shape, dt=F32: st.enter_context(nc.sbuf_tensor(name, list(shape), dt))
            cols = {}
            ncol = [0]

            def colgrp(name, n=16):
                cols[name] = ncol[0]
                ncol[0] += n
                return cols[name]

            raw_groups = []
            for l in range(NL):
                for i in range(2):
                    raw_groups.append((f"ng{l}{i}", norm_g[l, i], 16))
                    raw_groups.append((f"nb{l}{i}", norm_b[l, i], 16))
            for l in range(NL):
                raw_groups.append((f"adab{l}", ada_b[l], 96))
            for j in range(4):
                raw_groups.append((f"cw{j}", ab_conv_w[0, j], 8))
            raw_groups.append(("cb", ab_conv_b[0], 8))
            raw_groups.append(("br", ab_b_r[0].rearrange("h e -> (h e)"), 8))
            raw_groups.append(("bi", ab_b_i[0].rearrange("h e -> (h e)"), 8))
            raw_groups.append(("lam", ab_lambda[0], 8))
            raw_groups.append(("psc", cd_pool_scale[0], 8))
            for b in range(NB):
                raw_groups.append((f"cT{b}", c[b], 16))
            for nm, _, n_ in raw_groups:
                colgrp(nm, n_)
            NRAW = ncol[0]
            for l in range(NL):
                for i in range(2):
                    colgrp(f"xss{l}{i}")
                    colgrp(f"xsb{l}{i}")
                    for b in range(NB):
                        colgrp(f"hs{l}{i}{b}")
                        colgrp(f"hb{l}{i}{b}")
                for b in range(NB):
                    colgrp(f"g1{l}{b}")
                    colgrp(f"g2{l}{b}")
                    colgrp(f"s1p{l}{b}")
                    colgrp(f"s2p{l}{b}")
            for b in range(NB):
                colgrp(f"hs_in{b}")
            colgrp("lam2", 8)
            colgrp("zero", 1)
            colgrp("eps", 1)
            colgrp("one", 1)
            NTAB = ncol[0]
            tab = sbp("tab", [128, NTAB])
            B_tab = Buf("tab")
            modr = sbp("modr", [128, NL, NB, 96])
            B_modr = Buf("modr")
            ident = sbp("ident", [128, 128])
            ident_bf = sbp("ident_bf", [128, 128], BF16)
            ones_f = sbp("ones_f", [128, 128])
            B_const = Buf("const")

            def col(name, i=0, n=1):
                c0 = cols[name] + i
                return tab[:, c0:c0 + n]

            MEMSET(pool, ident[:], 1.0, [B_const])
            k.op(pool, lambda: nc.gpsimd.affine_select(out=ident[:], in_=ident[:], compare_op=ALU.is_equal, fill=0.0,
                                                       base=0, pattern=[[-1, 128]], channel_multiplier=1), [B_const], [B_const])
            CP(pool, ident_bf[:], ident[:], [B_const], [B_const])
            MEMSET(pool, ones_f[:], 1.0, [B_const])
            MEMSET(dve, col("zero"), 0.0, [B_tab])
            MEMSET(dve, col("eps"), EPS, [B_tab])

            MEMSET(dve, col("one"), 1.0, [B_tab])
            with scope() as sc:
                nst_ = (NRAW + 127) // 128
                stg = [sc.sb(f"stg{j}", [128, 128]) for j in range(nst_)]
                B_stg = [Buf() for _ in range(nst_)]
                tps_ = sc.ps("tps_", [128, 128])
                B_tps_ = Buf()
                for j in range(nst_):
                    MEMSET(dve, stg[j][:], 0.0, [B_stg[j]])
                for nm, vec, n_ in raw_groups:
                    c0 = cols[nm]
                    r = 0
                    while r < n_:
                        j, p0 = divmod(c0 + r, 128)
                        m_ = min(n_ - r, 128 - p0)
                        k.dma(sp, stg[j][p0:p0 + m_, :], vec.rearrange("(c p) -> c p", p=128)[r:r + m_, :], writes=[B_stg[j]])
                        r += m_
                for j in range(nst_):
                    w_ = min(128, NRAW - j * 128)
                    TR(tps_[:], stg[j][:], ident[:], [B_stg[j], B_const], [B_tps_], tick=True)
                    CP(dve, tab[:, j * 128:j * 128 + w_], tps_[:, 0:w_], [B_tps_], [B_tab])
            checkpoint("pro_a")
            ACT(col("lam2", 0, 8), col("lam", 0, 8), AF.Exp, [B_tab], [B_tab], scale=-1.0)
            ACT(col("lam2", 0, 8), col("lam2", 0, 8), AF.Ln, [B_tab], [B_tab], bias=col("one"))
            TS(dve, col("lam", 0, 8), col("lam2", 0, 8), -8.0, None, ALU.mult, ALU.bypass, [B_tab], [B_tab])
            TS(dve, col("lam2", 0, 8), col("lam2", 0, 8), -16.0, None, ALU.mult, ALU.bypass, [B_tab], [B_tab])

            checkpoint("pro_b")
            with scope() as sc:
                cT = sc.sb("cT", [128, KC, NB])
                B_cT = Buf("cT")
                for b in range(NB):
                    ACT(cT[:, :, b], col(f"cT{b}", 0, 16), AF.Silu, [B_tab], [B_cT])
                CBW = 3072
                aw = [sc.sb(f"aw{i}", [128, CBW]) for i in range(2)]
                B_aw = [Buf(f"aw{i}") for i in range(2)]
                mp = [sc.ps(f"mp{l}", [128, 96 * NB]) for l in range(NL)]
                B_mp = [Buf(f"mp{l}") for l in range(NL)]
                n = 0
                for l in range(NL):
                    for kk in range(KC):
                        for cb in range(6 * DM // CBW):
                            t = aw[n % 2]
                            bt = B_aw[n % 2]
                            n += 1
                            k.dma(sp if n % 2 else act, t[:], ada_w[l, kk * 128:(kk + 1) * 128, cb * CBW:(cb + 1) * CBW], writes=[bt])
                            for jj in range(CBW // 128):
                                j = cb * (CBW // 128) + jj
                                last = (kk == KC - 1) and (j == 95)
                                MM(mp[l][:, j * NB:(j + 1) * NB], t[:, jj * 128:(jj + 1) * 128], cT[:, kk, :],
                                   start=(kk == 0 and j == 0), stop=(kk == KC - 1), reads=[bt, B_cT], writes=[B_mp[l]],
                                   tick=(jj == CBW // 128 - 1))
                    for b in range(NB):
                        TT(dve, modr[:, l, b, :], mp[l][:].rearrange("p (j b) -> p j b", b=NB)[:, :, b],
                           col(f"adab{l}", 0, 96), ALU.add, [B_mp[l], B_tab], [B_modr])
                checkpoint("pro_c")
                for l in range(NL):
                    for b in range(NB):
                        TS(dve, col(f"g1{l}{b}", 0, 16), modr[:, l, b, 32:48], 1.0, None, ALU.add, ALU.bypass, [B_modr], [B_tab])
                        TS(dve, col(f"g2{l}{b}", 0, 16), modr[:, l, b, 80:96], 1.0, None, ALU.add, ALU.bypass, [B_modr], [B_tab])
                        TS(dve, col(f"s1p{l}{b}", 0, 16), modr[:, l, b, 16:32], 1.0, None, ALU.add, ALU.bypass, [B_modr], [B_tab])
                        TS(dve, col(f"s2p{l}{b}", 0, 16), modr[:, l, b, 64:80], 1.0, None, ALU.add, ALU.bypass, [B_modr], [B_tab])
                for l in range(NL):
                    for i in range(2):
                        g_ = col(f"ng{l}{i}", 0, 16)
                        b_ = col(f"nb{l}{i}", 0, 16)
                        TS(dve, col(f"xss{l}{i}", 0, 16), g_, ALPHA, None, ALU.mult, ALU.bypass, [B_tab], [B_tab])
                        TS(dve, col(f"xsb{l}{i}", 0, 16), b_, ALPHA, None, ALU.mult, ALU.bypass, [B_tab], [B_tab])
                        if i == 1 and l == NL - 1:
                            continue
                        for b in range(NB):
                            if i == 0:
                                sp_ = col(f"s2p{l}{b}", 0, 16)
                                sh_ = modr[:, l, b, 48:64]
                            else:
                                sp_ = col(f"s1p{l + 1}{b}", 0, 16)
                                sh_ = modr[:, l + 1, b, 0:16]
                            TT(dve, col(f"hs{l}{i}{b}", 0, 16), g_, sp_, ALU.mult, [B_tab], [B_tab])
                            TT(dve, col(f"hb{l}{i}{b}", 0, 16), b_, sp_, ALU.mult, [B_tab], [B_tab])
                            TT(dve, col(f"hb{l}{i}{b}", 0, 16), col(f"hb{l}{i}{b}", 0, 16), sh_, ALU.add, [B_tab, B_modr], [B_tab])

            for b in range(NB):
                TS(dve, col(f"hs_in{b}", 0, 16), col(f"s1p0{b}", 0, 16), 1.0 / ALPHA, None, ALU.mult, ALU.bypass, [B_tab], [B_tab])
            checkpoint("pro_d")
            with scope() as sc:
                xt = [sc.sb(f"xt{i}", [128, DM]) for i in range(2)]
                B_xt = [Buf() for _ in range(2)]
                xsb = sc.sb("xsb", [128, KC, 512])
                hb = sc.sb("hb", [128, KC, 512], F32 if "hbf32" in XTF else BF16)
                B_xsb = [[Buf() for _ in range(4)] for _ in range(KC)]
                B_hb = [[Buf() for _ in range(4)] for _ in range(KC)]
                allx = [B_xsb[cc_][q_] for cc_ in range(KC) for q_ in range(4)]
                allh = [B_hb[cc_][q_] for cc_ in range(KC) for q_ in range(4)]
                tp = [sc.ps(f"tp{i}", [128, 4, 128]) for i in range(4)]
                B_tp = [Buf() for _ in range(4)]
                n = 0
                for tb in range(T // 512):
                    b = tb // NSB
                    for q in range(4):
                        tt = tb * 4 + q
                        t, bt = xt[tt % 2], B_xt[tt % 2]
                        k.dma(sp, t[:], x[tt * 128:(tt + 1) * 128, :], writes=[bt])
                        for c4 in range(4):
                            p_, bp = tp[n % 4], B_tp[n % 4]
                            n += 1
                            for i in range(4):
                                cc = c4 * 4 + i
                                TR(p_[:, i, :], t[:, cc * 128:(cc + 1) * 128], ident[:], [bt, B_const], [bp], tick=(i == 3))
                            ACT(xsb[:, c4 * 4:(c4 + 1) * 4, q * 128:(q + 1) * 128], p_[:], AF.Copy, [bp], [B_xsb[c4 * 4 + i_][q] for i_ in range(4)], scale=ALPHA)
                            for i in range(4):
                                cc = c4 * 4 + i
                                TS(dve, hb[:, cc, q * 128:(q + 1) * 128], xsb[:, cc, q * 128:(q + 1) * 128], col(f"hs_in{b}", cc), None,
                                   ALU.mult, ALU.bypass, [B_xsb[cc][q], B_tab], [B_hb[cc][q]])
                                TS(dve, hb[:, cc, q * 128:(q + 1) * 128], hb[:, cc, q * 128:(q + 1) * 128], modr[:, 0, b, cc:cc + 1], None,
                                   ALU.add, ALU.bypass, [B_hb[cc][q], B_modr], [B_hb[cc][q]])
                    if "nost" not in XTF:
                        dma_blk(sp, xsb, xs_d, tb, 0, KC, False, allx, [B_xs[tb]])
                        dma_blk(act, hb, hT_d, tb, 0, KC, False, allh, [B_hT[tb]])

            def load_w_bf(dst, src, nk, kstep, breads, bw):
                v = src.rearrange("(k p) n -> p k n", p=128)
                for k0 in range(0, nk, kstep):
                    k1 = min(nk, k0 + kstep)
                    k.dma(pool, dst[:, k0:k1, :], v[:, k0:k1, :], reads=breads, writes=[bw])

            def gemm_res_ln(name, W, nk, src_d, B_src, l, i, final):
                ND = 2 if nk <= 16 else 4
                CPD = KC // ND
                gname = "g1" if i == 0 else "g2"
                with scope() as so:
                    B_st = [Buf() for _ in range(T // 512)]
                    with scope() as sc:
                        NWB = 2
                        stg = [sc.sb("stg0", [128, 2, 512])]
                        B_stg = [Buf()]
                        Wsbs = [sc.sb(f"Wsb{j}", [128, nk, CPD * 128], BF16) for j in range(NWB)]
                        B_Ws = [Buf("W") for _ in range(NWB)]
                        ab = [sc.sb(f"ab{j}", [128, nk, 512], BF16) for j in range(2)]
                        B_ab = [Buf() for _ in range(2)]
                        rb = [sc.sb(f"rb{j}", [128, CPD, 512]) for j in range(2)]
                        B_rb = [[Buf() for _ in range(CPD)] for _ in range(2)]
                        sq = [sc.sb(f"sq{j}", [128, 512]) for j in range(2)]
                        B_sq = [Buf() for _ in range(2)]
                        acc = [sc.ps(f"acc{j}", [128, 512]) for j in range(3)]
                        B_acc = [Buf() for _ in range(3)]
                        pss = [sc.ps(f"pss{j}", [128, 512]) for j in range(2)]
                        psq = [sc.ps(f"psq{j}", [128, 512]) for j in range(2)]
                        B_pst = [Buf() for _ in range(2)]
                        nacc = nst = 0
                        items = [(dq, tb) for dq in range(ND) for tb in range(T // 512)]

                        def issue_loads(n):
                            dq, tb = items[n]
                            if tb == 0 and (NWB == 2 or dq == 0):
                                load_w_bf(Wsbs[dq % NWB], W[:, dq * CPD * 128:(dq + 1) * CPD * 128], nk, 4, [], B_Ws[dq % NWB])
                            b_ = tb // NSB
                            srcb = [B_src[b_][kk] for kk in range(nk)]
                            dma_blk(sp, ab[n % 2], src_d, tb, 0, nk, True, srcb, [B_ab[n % 2]])
                            dma_blk(sp, rb[n % 2], xs_d, tb, dq * CPD, (dq + 1) * CPD, True, [B_xs[tb]], B_rb[n % 2])

                        issue_loads(0)
                        for n, (dq, tb) in enumerate(items):
                            if n + 1 < len(items):
                                issue_loads(n + 1)
                            Wsb, B_W = Wsbs[dq % NWB], B_Ws[dq % NWB]
                            b = tb // NSB
                            a_, ba = ab[n % 2], B_ab[n % 2]
                            r_, br_ = rb[n % 2], B_rb[n % 2]
                            tsl = slice(tb * 512, (tb + 1) * 512)
                            ps_, pq_, bst = pss[nst % 2], psq[nst % 2], B_pst[nst % 2]
                            nst += 1
                            prev = None
                            for c8 in range(CPD + 1):
                                if c8 < CPD:
                                    cc = dq * CPD + c8
                                    ac, bac = acc[nacc % 3], B_acc[nacc % 3]
                                    s_, bs_ = sq[nacc % 2], B_sq[nacc % 2]
                                    nacc += 1
                                    for kk in range(nk):
                                        MM(ac[:], Wsb[:, kk, c8 * 128:(c8 + 1) * 128], a_[:, kk, :], kk == 0, kk == nk - 1,
                                           [B_W, ba], [bac], tick=(kk == nk - 1))
                                if prev is not None:
                                    pc8, ps_sq, pbs = prev
                                    MM(ps_[:], ones_f[:], r_[:, pc8, :], pc8 == 0, pc8 == CPD - 1, [B_const, br_[pc8]], [bst], tick=False)
                                    MM(pq_[:], ones_f[:], ps_sq[:], pc8 == 0, pc8 == CPD - 1, [B_const, pbs], [bst], tick=True)
                                if c8 < CPD:
                                    STT(r_[:, c8, :], ac[:], col(f"{gname}{l}{b}", cc), r_[:, c8, :], ALU.mult, ALU.add,
                                        [bac, br_[c8], B_tab], [br_[c8]])
                                    ACT(s_[:], r_[:, c8, :], AF.Square, [br_[c8]], [bs_])
                                    prev = (c8, s_, bs_)
                            sg_, bsg = stg[0], B_stg[0]
                            if dq == 0:
                                CP(act, sg_[:, 0, :], ps_[:], [bst], [bsg])
                                CP(dve, sg_[:, 1, :], pq_[:], [bst], [bsg])
                            else:
                                TT(dve, sg_[:, 0, :], sg_[:, 0, :], ps_[:], ALU.add, [bst, bsg], [bsg])
                                TT(dve, sg_[:, 1, :], sg_[:, 1, :], pq_[:], ALU.add, [bst, bsg], [bsg])
                            k.dma(sp, stat_d[tb], sg_[:], reads=[bsg], writes=[B_st[tb]])
                            if n + 1 < len(items) and items[n + 1][0] > 0:
                                tbn = items[n + 1][1]
                                k.dma(sp, sg_[:], stat_d[tbn], reads=[B_st[tbn]], writes=[bsg])
                            dma_blk(sp, r_, xs_d, tb, dq * CPD, (dq + 1) * CPD, False, br_, [B_xs[tb]])
                            if NWB == 1 and n + 1 < len(items) and items[n + 1][1] == 0:
                                dq2 = items[n + 1][0]
                                load_w_bf(Wsbs[0], W[:, dq2 * CPD * 128:(dq2 + 1) * CPD * 128], nk, 4, [], B_Ws[0])
                    with scope() as sc:
                        r16s = [sc.sb(f"r16{j}", [128, KC, 512]) for j in range(2)]
                        B_r16s = [[Buf() for _ in range(KC)] for _ in range(2)]
                        h16s = [sc.sb(f"h16{j}", [128, KC, 512], BF16) for j in range(2)]
                        B_h16s = [[Buf() for _ in range(KC)] for _ in range(2)]
                        mt = sc.sb("mt", [128, 512])
                        rstd = sc.sb("rstd", [128, 512])
                        nmr = sc.sb("nmr", [128, 512])
                        B_ms = Buf()
                        if final:
                            otok = [sc.sb(f"otok{j}", [128, DM]) for j in range(2)]
                            B_ot = [Buf() for _ in range(2)]
                            tpo = [sc.ps(f"tpo{j}", [128, 4, 128]) for j in range(2)]
                            B_tpo = [Buf() for _ in range(2)]
                        stb = [sc.sb(f"stb{j}", [128, 2, 512]) for j in range(2)]
                        B_stb = [Buf() for _ in range(2)]
                        dma_blk(sp, r16s[0], xs_d, 0, 0, KC, True, [B_xs[0]], B_r16s[0])
                        k.dma(sp, stb[0][:], stat_d[0], reads=[B_st[0]], writes=[B_stb[0]])
                        for tb in range(T // 512):
                            b = tb // NSB
                            tsl = slice(tb * 512, (tb + 1) * 512)
                            r16, B_r16 = r16s[tb % 2], B_r16s[tb % 2]
                            h16, B_h16 = h16s[tb % 2], B_h16s[tb % 2]
                            if tb + 1 < T // 512:
                                dma_blk(sp, r16s[(tb + 1) % 2], xs_d, tb + 1, 0, KC, True, [B_xs[tb + 1]], B_r16s[(tb + 1) % 2])
                                k.dma(sp, stb[(tb + 1) % 2][:], stat_d[tb + 1], reads=[B_st[tb + 1]], writes=[B_stb[(tb + 1) % 2]])
                            ssum_, ssq_, bstb = stb[tb % 2][:, 0, :], stb[tb % 2][:, 1, :], B_stb[tb % 2]
                            TS(dve, mt[:], ssum_, 1.0 / DM, None, ALU.mult, ALU.bypass, [bstb], [B_ms])
                            TT(dve, nmr[:], mt[:], mt[:], ALU.mult, [B_ms], [B_ms])
                            STT(rstd[:], ssq_, 1.0 / DM, nmr[:], ALU.mult, ALU.subtract, [bstb, B_ms], [B_ms])
                            ACT(rstd[:], rstd[:], AF.Sqrt, [B_ms, B_tab], [B_ms], bias=col("eps"))
                            RECIP(rstd[:], rstd[:], [B_ms], [B_ms])
                            STT(nmr[:], mt[:], -1.0, rstd[:], ALU.mult, ALU.mult, [B_ms], [B_ms])
                            for cc in range(KC):
                                rc = r16[:, cc, :]
                                TT(dve, rc, rc, rstd[:], ALU.mult, [B_r16[cc], B_ms], [B_r16[cc]])
                                TT(pool, rc, rc, nmr[:], ALU.add, [B_r16[cc], B_ms], [B_r16[cc]])
                                if final:
                                    ACT(rc, rc, AF.Identity, [B_r16[cc], B_tab], [B_r16[cc]], bias=col(f"nb{l}{i}", cc), scale=col(f"ng{l}{i}", cc))
                                else:
                                    ACT(h16[:, cc, :], rc, AF.Identity, [B_r16[cc], B_tab], [B_h16[cc]],
                                        bias=col(f"hb{l}{i}{b}", cc), scale=col(f"hs{l}{i}{b}", cc))
                                    ACT(rc, rc, AF.Identity, [B_r16[cc], B_tab], [B_r16[cc]], bias=col(f"xsb{l}{i}", cc), scale=col(f"xss{l}{i}", cc))
                            if final:
                                for q in range(4):
                                    tt = tb * 4 + q
                                    ot, bot = otok[tt % 2], B_ot[tt % 2]
                                    for c4 in range(4):
                                        p_, bp = tpo[c4 % 2], B_tpo[c4 % 2]
                                        for ii in range(4):
                                            cc = c4 * 4 + ii
                                            TR(p_[:, ii, :], r16[:, cc, q * 128:(q + 1) * 128], ident[:], [B_r16[cc], B_const], [bp], tick=(ii == 3))
                                        CP(act if c4 % 2 else dve, ot[:, c4 * 512:(c4 + 1) * 512], p_[:].rearrange("p a b -> p (a b)"), [bp], [bot])
                                    k.dma(sp, out[tt * 128:(tt + 1) * 128, :], ot[:], reads=[bot])
                            else:
                                dma_blk(sp, r16, xs_d, tb, 0, KC, False, B_r16, [B_xs[tb]])
                                dma_blk(sp, h16, hT_d, tb, 0, KC, False, B_h16, [B_hT[tb]])

            def ffn1(l):
                JG = 4
                with scope() as sc:
                    hseq = sc.sb("hseq", [128, NSB, KC, 512], BF16)
                    B_hs = Buf()
                    wg = [sc.sb(f"wg{j}", [128, KC, JG * 128], BF16) for j in range(2)]
                    wu = [sc.sb(f"wu{j}", [128, KC, JG * 128], BF16) for j in range(2)]
                    B_wg = [Buf() for _ in range(2)]
                    sg = [sc.sb(f"sg{j}", [128, 512]) for j in range(2)]
                    B_sg = [Buf() for _ in range(2)]
                    aj = [sc.sb(f"aj{j}", [128, S], BF16) for j in range(2)]
                    B_aj = [Buf() for _ in range(2)]
                    pg = [sc.ps(f"pg{j}", [128, 512]) for j in range(2)]
                    pu = [sc.ps(f"pu{j}", [128, 512]) for j in range(2)]
                    B_pg = [Buf() for _ in range(2)]
                    B_pu = [Buf() for _ in range(2)]
                    nw = nj = np_ = 0
                    for b in range(NB):
                        for tbk in range(NSB):
                            tb = b * NSB + tbk
                            dma_blk(sp, hseq[:, tbk], hT_d, tb, 0, KC, True, [B_hT[tb]], [B_hs])
                        for jg in range(FC // JG):
                            wg_, wu_, bw = wg[nw % 2], wu[nw % 2], B_wg[nw % 2]
                            nw += 1
                            csl = slice(jg * JG * 128, (jg + 1) * JG * 128)
                            load_w_bf(wg_, ffn_w_gate[l][:, csl], KC, 8, [], bw)
                            load_w_bf(wu_, ffn_w_up[l][:, csl], KC, 8, [], bw)
                            for jj in range(JG):
                                j = jg * JG + jj
                                a_, ba = aj[nj % 2], B_aj[nj % 2]
                                nj += 1
                                for tbk in range(NSB):
                                    g_, bg = pg[np_ % 2], B_pg[np_ % 2]
                                    u_, bu = pu[np_ % 2], B_pu[np_ % 2]
                                    s_, bs_ = sg[np_ % 2], B_sg[np_ % 2]
                                    np_ += 1
                                    tsl = slice(tbk * 512, (tbk + 1) * 512)
                                    for kk in range(KC):
                                        MM(g_[:], wg_[:, kk, jj * 128:(jj + 1) * 128], hseq[:, tbk, kk, :], kk == 0, kk == KC - 1,
                                           [bw, B_hs], [bg], tick=(kk == KC - 1))
                                    for kk in range(KC):
                                        MM(u_[:], wu_[:, kk, jj * 128:(jj + 1) * 128], hseq[:, tbk, kk, :], kk == 0, kk == KC - 1,
                                           [bw, B_hs], [bu], tick=(kk == KC - 1))
                                    ACT(s_[:], g_[:], AF.Silu, [bg], [bs_])
                                    TT(dve, a_[:, tsl], s_[:], u_[:], ALU.mult, [bs_, bu], [ba])
                                dma_rows(sp, aT_d, b, j, a_, [ba], [B_aT[b][j]])

            def mixer_ab():
                w_in = ab_w_in[0]
                with scope() as sc:
                    hseq = sc.sb("hseq", [128, NSB, KC, 512], BF16)
                    B_hs = Buf()
                    wr = sc.sb("wr", [128, 8, 128], BF16)
                    wi = sc.sb("wi", [128, 8, 128], BF16)
                    wsT = sc.sb("wsT", [128, 8, 128], BF16)
                    bsrow = sc.sb("bsrow", [1, 1024], BF16)
                    onesrow = sc.sb("onesrow", [1, 128], BF16)
                    vng = sc.sb("vng", [128, 1024])
                    vnb = sc.sb("vnb", [128, 1024])
                    B_c0 = Buf("l0const")
                    k.dma(pool, wr[:], ab_w_r[0].rearrange("h d e -> d h e"), writes=[B_c0])
                    k.dma(pool, wi[:], ab_w_i[0].rearrange("h d e -> d h e"), writes=[B_c0])
                    k.dma(pool, bsrow[:], ab_b_s[0].rearrange("g t -> (g t)").rearrange("(o n) -> o n", o=1), writes=[B_c0])
                    MEMSET(dve, onesrow[:], 1.0, [B_c0])
                    k.dma(sp, vng[:], ab_vnorm_g[0].partition_broadcast(128), writes=[B_c0])
                    k.dma(sp, vnb[:], ab_vnorm_b[0].partition_broadcast(128), writes=[B_c0])
                    with scope() as s2:
                        wsf = s2.sb("wsf", [128, 8, 128])
                        B_wsf = Buf()
                        tps = s2.ps("tps", [128, 4, 128])
                        B_tps = Buf()
                        k.dma(sp, wsf[:], ab_w_s[0].rearrange("g t s -> t g s"), writes=[B_wsf])
                        for g4 in range(2):
                            for ii in range(4):
                                TR(tps[:, ii, :], wsf[:, g4 * 4 + ii, :], ident[:], [B_wsf, B_const], [B_tps], tick=(ii == 3))
                            CP(dve, wsT[:, g4 * 4:(g4 + 1) * 4, :], tps[:], [B_tps], [B_c0])
                        MEMSET(dve, wsT[64:128, :, 0:64], 0.0, [B_c0])

                    for b in range(NB):
                        for tbk in range(NSB):
                            tb = b * NSB + tbk
                            dma_blk(sp, hseq[:, tbk], hT_d, tb, 0, KC, True, [B_hT[tb]], [B_hs])
                        with scope() as s2:
                            wxa = [s2.sb(f"wxa{j}", [128, KC, 128], BF16) for j in range(2)]
                            wga = [s2.sb(f"wga{j}", [128, KC, 128], BF16) for j in range(2)]
                            B_wx = [Buf() for _ in range(2)]
                            xa = [s2.sb(f"xa{j}", [128, 3 + S]) for j in range(2)]
                            gg = [s2.sb(f"gg{j}", [128, S], BF16) for j in range(2)]
                            xc = [s2.sb(f"xc{j}", [128, S]) for j in range(2)]
                            xcb = [s2.sb(f"xcb{j}", [128, S], BF16) for j in range(2)]
                            rr = [s2.sb(f"rr{j}", [128, S]) for j in range(2)]
                            iu = [s2.sb(f"iu{j}", [128, S]) for j in range(2)]
                            aa = [s2.sb(f"aa{j}", [128, S]) for j in range(2)]
                            ya = [s2.sb(f"ya{j}", [128, S], BF16) for j in range(2)]
                            B_xa, B_gg, B_xc, B_xcb, B_rr, B_iu, B_aa = ([Buf() for _ in range(2)] for _ in range(7))
                            B_ya = [Buf() for _ in range(2)]
                            pxa = [s2.ps(f"pxa{j}", [128, 512]) for j in range(2)]
                            pga = [s2.ps(f"pga{j}", [128, 512]) for j in range(2)]
                            pr = [s2.ps(f"pr{j}", [128, 512]) for j in range(2)]
                            pi = [s2.ps(f"pi{j}", [128, 512]) for j in range(2)]
                            B_pxa, B_pga, B_pr, B_pi = ([Buf() for _ in range(2)] for _ in range(4))
                            for j in range(2):
                                MEMSET(dve, xa[j][:, 0:3], 0.0, [B_xa[j]])
                            cntl = {"px": 0, "pg": 0}

                            def stP(h):
                                st_ = h % 2
                                wx_, wg_, bw = wxa[st_], wga[st_], B_wx[st_]
                                load_w_bf(wx_, w_in[:, h * 128:(h + 1) * 128], KC, 16, [], bw)
                                load_w_bf(wg_, w_in[:, 1024 + h * 128:1024 + (h + 1) * 128], KC, 16, [], bw)
                                for tbk in range(NSB):
                                    tsl = slice(tbk * 512, (tbk + 1) * 512)
                                    j_ = cntl["px"] % 2
                                    cntl["px"] += 1
                                    p1, b1 = pxa[j_], B_pxa[j_]
                                    p2, b2 = pga[j_], B_pga[j_]
                                    for kk in range(KC):
                                        MM(p1[:], wx_[:, kk, :], hseq[:, tbk, kk, :], kk == 0, kk == KC - 1, [bw, B_hs], [b1], tick=(kk == KC - 1))
                                    for kk in range(KC):
                                        MM(p2[:], wg_[:, kk, :], hseq[:, tbk, kk, :], kk == 0, kk == KC - 1, [bw, B_hs], [b2], tick=(kk == KC - 1))
                                    ACT(xa[st_][:, 3 + tbk * 512:3 + (tbk + 1) * 512], p1[:], AF.Copy, [b1], [B_xa[st_]])
                                    ACT(gg[st_][:, tsl], p2[:], AF.Gelu_apprx_tanh, [b2], [B_gg[st_]])

                            def stE(h):
                                st_ = h % 2
                                xa_, gg_, xc_, xcb_, rr_, iu_, aa_ = xa[st_], gg[st_], xc[st_], xcb[st_], rr[st_], iu[st_], aa[st_]
                                bxa, bgg, bxc, bxcb, brr, biu, baa = B_xa[st_], B_gg[st_], B_xc[st_], B_xcb[st_], B_rr[st_], B_iu[st_], B_aa[st_]
                                ACT(xc_[:], xa_[:, 3:3 + S], AF.Identity, [bxa, B_tab], [bxc], scale=col("cw3", h), bias=col("cb", h))
                                for j in (2, 1, 0):
                                    STT(xc_[:], xa_[:, j:j + S], col(f"cw{j}", h), xc_[:], ALU.mult, ALU.add, [bxa, bxc, B_tab], [bxc])
                                CP(pool, xcb_[:], xc_[:], [bxc], [bxcb])
                                for tbk in range(NSB):
                                    tsl = slice(tbk * 512, (tbk + 1) * 512)
                                    j_ = cntl["pg"] % 2
                                    cntl["pg"] += 1
                                    p1, b1 = pr[j_], B_pr[j_]
                                    p2, b2 = pi[j_], B_pi[j_]
                                    MM(p1[:], wr[:, h, :], xcb_[:, tsl], True, True, [B_c0, bxcb], [b1], tick=True)
                                    MM(p2[:], wi[:, h, :], xcb_[:, tsl], True, True, [B_c0, bxcb], [b2], tick=True)
                                    ACT(rr_[:, tsl], p1[:], AF.Sigmoid, [b1, B_tab], [brr], bias=col("br", h))
                                    ACT(iu_[:, tsl], p2[:], AF.Sigmoid, [b2, B_tab], [biu], bias=col("bi", h))
                                ACT(aa_[:], rr_[:], AF.Exp, [brr, B_tab], [baa], scale=col("lam", h))
                                ACT(rr_[:], rr_[:], AF.Exp, [brr, B_tab], [brr], scale=col("lam2", h))
                                ACT(rr_[:], rr_[:], AF.Sqrt, [brr, B_tab], [brr], scale=-1.0, bias=col("one"))
                                TT(dve, iu_[:], iu_[:], xc_[:], ALU.mult, [biu, bxc], [biu])
                                TT(dve, iu_[:], iu_[:], rr_[:], ALU.mult, [biu, brr], [biu])
                                SCAN(xa_[:, 3:3 + S], aa_[:], iu_[:], col("zero"), [baa, biu, B_tab], [bxa])
                                y_, by = ya[st_], B_ya[st_]
                                TT(dve, y_[:], xa_[:, 3:3 + S], gg_[:], ALU.mult, [bxa, bgg], [by])
                                dma_rows(sp, yT_d, b, h, y_, [by], [B_yT[b][h]])

                            stP(0)
                            for h in range(8):
                                if h + 1 < 8:
                                    stP(h + 1)
                                stE(h)
                        with scope() as s2:
                            wub = s2.sb("wub", [128, KC, 1024], BF16)
                            wvb = s2.sb("wvb", [128, KC, 1024], BF16)
                            B_wb = Buf()
                            load_w_bf(wub, w_in[:, 2048:3072], KC, 4, [], B_wb)
                            load_w_bf(wvb, w_in[:, 3072:4096], KC, 4, [], B_wb)
                            ug = s2.sb("ug", [128, 8, 512])
                            B_ug = Buf()
                            vf = [s2.sb(f"vf{j}", [128, 1024]) for j in range(2)]
                            vt = [s2.sb(f"vt{j}", [128, 1024], BF16) for j in range(2)]
                            B_vf = [Buf() for _ in range(2)]
                            B_vt = [Buf() for _ in range(2)]
                            st6 = s2.sb("st6", [128, 2, 6])
                            mv = s2.sb("mv", [128, 2])
                            B_mv = Buf()
                            yb = [s2.sb(f"yb{j}", [128, 8, 512], BF16) for j in range(2)]
                            B_yb = [Buf() for _ in range(2)]
                            pu = [s2.ps(f"pu{j}", [128, 512]) for j in range(2)]
                            B_pu = [Buf() for _ in range(2)]
                            pv = [s2.ps(f"pv{j}", [128, 1024]) for j in range(2)]
                            B_pv = [Buf() for _ in range(2)]
                            psv = s2.ps("psv", [128, 8, 128])
                            B_psv = Buf()
                            npu = 0
                            for tbk in range(NSB):
                                tsl = slice(tbk * 512, (tbk + 1) * 512)
                                y_, by = yb[tbk % 2], B_yb[tbk % 2]
                                for g in range(8):
                                    p_, bp = pu[npu % 2], B_pu[npu % 2]
                                    npu += 1
                                    for kk in range(KC):
                                        MM(p_[:], wub[:, kk, g * 128:(g + 1) * 128], hseq[:, tbk, kk, :], kk == 0, kk == KC - 1,
                                           [B_wb, B_hs], [bp], tick=(kk == KC - 1))
                                    ACT(ug[:, g, :], p_[:], AF.Gelu_apprx_tanh, [bp], [B_ug])
                                def vmm(tt_):
                                    p__, bp__ = pv[tt_ % 2], B_pv[tt_ % 2]
                                    for hf in range(2):
                                        for kk in range(KC):
                                            MM(p__[:, hf * 512:(hf + 1) * 512], hseq[:, tt_ // 4, kk, (tt_ % 4) * 128:(tt_ % 4 + 1) * 128],
                                               wvb[:, kk, hf * 512:(hf + 1) * 512], kk == 0, kk == KC - 1, [B_wb, B_hs], [bp__],
                                               tick=(kk == KC - 1))

                                def stG(tt_):
                                    p__, bp__ = pv[tt_ % 2], B_pv[tt_ % 2]
                                    v__, bv__ = vf[tt_ % 2], B_vf[tt_ % 2]
                                    for hf in range(2):
                                        ACT(v__[:, hf * 512:(hf + 1) * 512], p__[:, hf * 512:(hf + 1) * 512], AF.Gelu_apprx_tanh, [bp__], [bv__])

                                def stN(tt_):
                                    v__, bv__ = vf[tt_ % 2], B_vf[tt_ % 2]
                                    vt__, bvt__ = vt[tt_ % 2], B_vt[tt_ % 2]
                                    for hf in range(2):
                                        BNS(st6[:, hf, :], v__[:, hf * 512:(hf + 1) * 512], [bv__], [B_mv])
                                    BNA(mv[:], st6[:].rearrange("p a b -> p (a b)"), [B_mv], [B_mv])
                                    ACT(mv[:, 1:2], mv[:, 1:2], AF.Sqrt, [B_mv, B_tab], [B_mv], bias=col("eps"))
                                    RECIP(mv[:, 1:2], mv[:, 1:2], [B_mv], [B_mv])
                                    TS(dve, v__[:], v__[:], mv[:, 0:1], None, ALU.subtract, ALU.bypass, [bv__, B_mv], [bv__])
                                    TS(dve, v__[:], v__[:], mv[:, 1:2], None, ALU.mult, ALU.bypass, [bv__, B_mv], [bv__])
                                    TT(pool, v__[:], v__[:], vng[:], ALU.mult, [bv__, B_c0], [bv__])
                                    TT(pool, vt__[:], v__[:], vnb[:], ALU.add, [bv__, B_c0], [bvt__])

                                if tbk == 0:
                                    vmm(0)
                                    if NST > 1:
                                        vmm(1)
                                    stG(0)
                                    stN(0)
                                for q in range(4):
                                    tt = tbk * 4 + q
                                    vt_, bvt = vt[tt % 2], B_vt[tt % 2]
                                    if tt + 1 < NST:
                                        stG(tt + 1)
                                    if tt + 2 < NST:
                                        vmm(tt + 2)
                                    if tt + 1 < NST:
                                        stN(tt + 1)
                                    for g in range(8):
                                        MM(psv[:, g, :], vt_[:, g * 128:(g + 1) * 128], wsT[:, g, :], True, False, [bvt, B_c0], [B_psv], tick=False)
                                        MM(psv[:, g, :], onesrow[0:1, :], bsrow[0:1, g * 128:(g + 1) * 128], False, True, [B_c0], [B_psv],
                                           tick=(g == 7))
                                    TT(dve, y_[:, :, q * 128:(q + 1) * 128], psv[:], ug[:, :, q * 128:(q + 1) * 128], ALU.mult,
                                       [B_psv, B_ug], [by])
                                dma_blk(sp, y_, yT_d, b * NSB + tbk, 8, 16, False, [by], [B_yT[b][8 + g] for g in range(8)])

            def mixer_cd():
                w_in = cd_w_in[0]
                zscale = 128 ** -0.5
                with scope() as sc:
                    hseq = sc.sb("hseq", [128, NSB, KC, 512], BF16)
                    B_hs = Buf()
                    wpl = sc.sb("wpl", [128, 4, 2, 256], BF16)
                    tri = sc.sb("tri", [128, 128])
                    ones_s = sc.sb("ones_s", [128, S], BF16)
                    invc = sc.sb("invc", [128, 16])
                    B_c1 = Buf("l1const")
                    k.dma(pool, wpl[:], cd_w_pool[0].rearrange("g (dc p) e -> p g dc e", p=128), writes=[B_c1])
                    MEMSET(pool, tri[:], 1.0, [B_c1])
                    k.op(pool, lambda: nc.gpsimd.affine_select(out=tri[:], in_=tri[:], compare_op=ALU.is_gt, fill=0.0,
                                                               base=0, pattern=[[-1, 128]], channel_multiplier=1), [B_c1], [B_c1])
                    MEMSET(pool, ones_s[:], 1.0, [B_c1])
                    for t_ in range(16):
                        MEMSET(dve, invc[:, t_:t_ + 1], 1.0 / (t_ + 1), [B_c1])
                    v_sb = sc.sb("v_sb", [128, NST, 1024], BF16)
                    B_v = Buf()
                    for b in range(NB):
                        for tbk in range(NSB):
                            tb = b * NSB + tbk
                            dma_blk(sp, hseq[:, tbk], hT_d, tb, 0, KC, True, [B_hT[tb]], [B_hs])
                        with scope() as s2:
                            wv = s2.sb("wv", [128, KC, 1024], BF16)
                            B_wv = Buf()
                            load_w_bf(wv, w_in[:, 2048:3072], KC, 4, [], B_wv)
                            pv = [s2.ps(f"pv{j}", [128, 1024]) for j in range(2)]
                            B_pv = [Buf() for _ in range(2)]
                            for tt in range(NST):
                                p_, bp = pv[tt % 2], B_pv[tt % 2]
                                for hf in range(2):
                                    for kk in range(KC):
                                        MM(p_[:, hf * 512:(hf + 1) * 512], hseq[:, tt // 4, kk, (tt % 4) * 128:(tt % 4 + 1) * 128],
                                           wv[:, kk, hf * 512:(hf + 1) * 512], kk == 0, kk == KC - 1, [B_wv, B_hs], [bp], tick=(kk == KC - 1))
                                ACT(v_sb[:, tt, 0:512], p_[:, 0:512], AF.Copy, [bp], [B_v])
                                ACT(v_sb[:, tt, 512:1024], p_[:, 512:1024], AF.Copy, [bp], [B_v])
                        with scope() as s2:
                            wp = [s2.sb(f"wp{j}", [128, KC, 128], BF16) for j in range(2)]
                            B_wp = [Buf() for _ in range(2)]
                            pp = s2.sb("pp", [128, 16 + S])
                            sa = s2.sb("sa", [128, 16 + S])
                            sbb = s2.sb("sbb", [128, 16 + S])
                            pt = s2.sb("pt", [128, S])
                            tmpc = s2.sb("tmpc", [128, 16])
                            pbf = s2.sb("pbf", [128, 2, S], BF16)
                            yd = [s2.sb(f"yd{j}", [128, S], BF16) for j in range(2)]
                            B_pp, B_sa, B_sbb, B_pt, B_pbf = [Buf() for _ in range(5)]
                            B_yd = [Buf() for _ in range(2)]
                            ppj = [s2.ps(f"ppj{j}", [128, 512]) for j in range(2)]
                            B_ppj = [Buf() for _ in range(2)]
                            pyd = [s2.ps(f"pyd{j}", [128, 512]) for j in range(2)]
                            B_pyd = [Buf() for _ in range(2)]
                            MEMSET(dve, pp[:, 0:16], 0.0, [B_pp])
                            MEMSET(dve, sa[:, 0:16], 0.0, [B_sa])
                            MEMSET(dve, sbb[:, 0:16], 0.0, [B_sbb])
                            npp = nyd = 0
                            for pc in range(8):
                                g = pc // 2
                                w = POOL_W[g]
                                w_, bw = wp[pc % 2], B_wp[pc % 2]
                                load_w_bf(w_, w_in[:, 3072 + pc * 128:3072 + (pc + 1) * 128], KC, 16, [], bw)
                                for tbk in range(NSB):
                                    p_, bp = ppj[npp % 2], B_ppj[npp % 2]
                                    npp += 1
                                    for kk in range(KC):
                                        MM(p_[:], w_[:, kk, :], hseq[:, tbk, kk, :], kk == 0, kk == KC - 1,
                                           [bw, B_hs], [bp], tick=(kk == KC - 1))
                                    ACT(pp[:, 16 + tbk * 512:16 + (tbk + 1) * 512], p_[:], AF.Copy, [bp], [B_pp])
                                TT(dve, sa[:, 16:], pp[:, 16:], pp[:, 15:15 + S], ALU.add, [B_pp], [B_sa])
                                cur, bcur = sa, B_sa
                                oth, both = sbb, B_sbb
                                sh = 2
                                while sh < w:
                                    TT(dve, oth[:, 16:], cur[:, 16:], cur[:, 16 - sh:16 - sh + S], ALU.add, [bcur], [both])
                                    cur, bcur, oth, both = oth, both, cur, bcur
                                    sh *= 2
                                STT(pt[:], cur[:, 16:], 1.0 / w, pp[:, 16:], ALU.mult, ALU.subtract, [bcur, B_pp], [B_pt])
                                TT(dve, tmpc[:, 0:w - 1], cur[:, 16:16 + w - 1], invc[:, 0:w - 1], ALU.mult, [bcur, B_c1], [B_pt])
                                TT(dve, pt[:, 0:w - 1], tmpc[:, 0:w - 1], pp[:, 16:16 + w - 1], ALU.subtract, [B_pt, B_pp], [B_pt])
                                CP(pool, pbf[:, pc % 2, :], pt[:], [B_pt], [B_pbf])
                                if pc % 2 == 1:
                                    for ec in range(2):
                                        y_, by = yd[nyd % 2], B_yd[nyd % 2]
                                        nyd += 1
                                        for tbk in range(NSB):
                                            tsl = slice(tbk * 512, (tbk + 1) * 512)
                                            p_, bp = pyd[tbk % 2], B_pyd[tbk % 2]
                                            MM(p_[:], wpl[:, g, 0, ec * 128:(ec + 1) * 128], pbf[:, 0, tsl], True, False, [B_c1, B_pbf], [bp], tick=False)
                                            MM(p_[:], wpl[:, g, 1, ec * 128:(ec + 1) * 128], pbf[:, 1, tsl], False, True, [B_c1, B_pbf], [bp], tick=True)
                                            ACT(y_[:, tsl], p_[:], AF.Identity, [bp, B_tab], [by], scale=col("psc", g * 2 + ec))
                                        ch = 8 + g * 2 + ec
                                        dma_rows(sp, yT_d, b, ch, y_, [by], [B_yT[b][ch]])
                        with scope() as s2:
                            wq = [s2.sb(f"wq{j}", [128, KC, 128], BF16) for j in range(2)]
                            wk = [s2.sb(f"wk{j}", [128, KC, 128], BF16) for j in range(2)]
                            B_wq = [Buf() for _ in range(2)]
                            qT = s2.sb("qT", [128, S], BF16)
                            kT = s2.sb("kT", [128, S], BF16)
                            B_q, B_k = Buf(), Buf()
                            ee = [s2.sb(f"ee{j}", [128, S]) for j in range(2)]
                            ff = [s2.sb(f"ff{j}", [128, S]) for j in range(2)]
                            zs = [s2.sb(f"zs{j}", [128, S]) for j in range(2)]
                            negt = [s2.sb(f"negt{j}", [128, 1]) for j in range(2)]
                            wbf = [s2.sb(f"wbf{j}", [128, S], BF16) for j in range(2)]
                            wT = [s2.sb(f"wT{j}", [128, NST, 128], BF16) for j in range(2)]
                            B_ee, B_ff, B_zs, B_nt, B_wbf, B_wT = ([Buf() for _ in range(2)] for _ in range(6))
                            yc = [s2.sb(f"yc{j}", [128, S], BF16) for j in range(2)]
                            B_yc = [Buf() for _ in range(2)]
                            zps = [s2.ps(f"zps{j}", [128, 2, 512]) for j in range(2)]
                            B_z = [Buf() for _ in range(2)]
                            pqk = s2.ps("pqk", [128, 512])
                            B_pqk = Buf()
                            wtp = [s2.ps(f"wtp{j}", [128, 4, 128], BF16) for j in range(2)]
                            B_wtp = [Buf() for _ in range(2)]
                            ycp = s2.ps("ycp", [128, 128])
                            B_ycp = Buf()
                            cnt = {"z": 0, "wt": 0}
                            for h in range(8):
                                wq_, wk_, bw = wq[h % 2], wk[h % 2], B_wq[h % 2]
                                load_w_bf(wq_, w_in[:, h * 128:(h + 1) * 128], KC, 16, [], bw)
                                load_w_bf(wk_, w_in[:, 1024 + h * 128:1024 + (h + 1) * 128], KC, 16, [], bw)
                                for tbk in range(NSB):
                                    tsl = slice(tbk * 512, (tbk + 1) * 512)
                                    for kk in range(KC):
                                        MM(pqk[:], wq_[:, kk, :], hseq[:, tbk, kk, :], kk == 0, kk == KC - 1, [bw, B_hs], [B_pqk], tick=(kk == KC - 1))
                                    ACT(qT[:, tsl], pqk[:], AF.Copy, [B_pqk], [B_q])
                                    for kk in range(KC):
                                        MM(pqk[:], wk_[:, kk, :], hseq[:, tbk, kk, :], kk == 0, kk == KC - 1, [bw, B_hs], [B_pqk], tick=(kk == KC - 1))
                                    ACT(kT[:, tsl], pqk[:], AF.Copy, [B_pqk], [B_k])
                                y_, by = yc[h % 2], B_yc[h % 2]

                                def stA(qi):
                                    st_ = qi % 2
                                    nk_ = (qi + 1) * 128
                                    dsl = slice(qi * 128, (qi + 1) * 128)
                                    for hf in range((nk_ + 1023) // 1024):
                                        zp, bz = zps[cnt["z"] % 2], B_z[cnt["z"] % 2]
                                        cnt["z"] += 1
                                        k0h = hf * 1024
                                        k1h = min(nk_, k0h + 1024)
                                        nchh = (k1h - k0h + 511) // 512
                                        for kc in range(nchh):
                                            a0 = k0h + kc * 512
                                            a1 = min(k1h, a0 + 512)
                                            MM(zp[:, kc, 0:a1 - a0], qT[:, dsl], kT[:, a0:a1], True, True, [B_q, B_k], [bz], tick=(kc == nchh - 1))
                                        for kc in range(nchh):
                                            a0 = k0h + kc * 512
                                            a1 = min(k1h, a0 + 512)
                                            ACT(ee[st_][:, a0:a1], zp[:, kc, 0:a1 - a0], AF.Exp, [bz], [B_ee[st_]], scale=zscale)
                                            ACT(zs[st_][:, a0:a1], zp[:, kc, 0:a1 - a0], AF.Copy, [bz], [B_zs[st_]], scale=zscale)
                                    ACT(ee[st_][:, 0:nk_], ee[st_][:, 0:nk_], AF.Ln, [B_ee[st_], B_tab], [B_ee[st_]], bias=col("one"))

                                def stB(qi):
                                    st_ = qi % 2
                                    nk_ = (qi + 1) * 128
                                    dsl = slice(qi * 128, (qi + 1) * 128)
                                    e_, f_, z_, w_ = ee[st_], ff[st_], zs[st_], wbf[st_]
                                    TT(pool, e_[:, dsl], e_[:, dsl], tri[:], ALU.mult, [B_ee[st_], B_c1], [B_ee[st_]])
                                    TT(pool, z_[:, 0:nk_], z_[:, 0:nk_], e_[:, 0:nk_], ALU.subtract, [B_zs[st_], B_ee[st_]], [B_zs[st_]])
                                    SCAN(f_[:, 0:nk_], ones_s[:, 0:nk_], e_[:, 0:nk_], col("zero"), [B_ee[st_], B_c1, B_tab], [B_ff[st_]])
                                    TS(pool, negt[st_][:], f_[:, nk_ - 1:nk_], -1.0, 0.0, ALU.mult, ALU.add, [B_ff[st_]], [B_nt[st_]])
                                    TT(pool, f_[:, 0:nk_], f_[:, 0:nk_], z_[:, 0:nk_], ALU.add, [B_zs[st_], B_ff[st_]], [B_ff[st_]])
                                    ACT(w_[:, 0:nk_], f_[:, 0:nk_], AF.Exp, [B_ff[st_], B_nt[st_]], [B_wbf[st_]], bias=negt[st_][:])
                                    TT(pool, w_[:, dsl], w_[:, dsl], tri[:], ALU.mult, [B_wbf[st_], B_c1], [B_wbf[st_]])

                                def stC1(qi):
                                    st_ = qi % 2
                                    for kb4 in range((qi + 4) // 4):
                                        j_ = cnt["wt"] % 2
                                        cnt["wt"] += 1
                                        p_, bp = wtp[j_], B_wtp[j_]
                                        nb_ = min(4, qi + 1 - kb4 * 4)
                                        for ii in range(nb_):
                                            kb = kb4 * 4 + ii
                                            TR(p_[:, ii, :], wbf[st_][:, kb * 128:(kb + 1) * 128], ident_bf[:], [B_wbf[st_], B_const], [bp], tick=(ii == nb_ - 1))
                                        CP(act if j_ else dve, wT[st_][:, kb4 * 4:kb4 * 4 + nb_, :], p_[:, 0:nb_, :], [bp], [B_wT[st_]])

                                def stC2(qi):
                                    st_ = qi % 2
                                    dsl = slice(qi * 128, (qi + 1) * 128)
                                    for kb in range(qi + 1):
                                        MM(ycp[:], v_sb[:, kb, h * 128:(h + 1) * 128], wT[st_][:, kb, :], kb == 0, kb == qi, [B_v, B_wT[st_]], [B_ycp],
                                           tick=(kb == qi))
                                    CP(dve, y_[:, dsl], ycp[:], [B_ycp], [by])

                                stA(0)
                                if NST > 1:
                                    stA(1)
                                stB(0)
                                for qi in range(NST):
                                    stC1(qi)
                                    if qi + 2 < NST:
                                        stA(qi + 2)
                                    if qi + 1 < NST:
                                        stB(qi + 1)
                                    stC2(qi)
                                dma_rows(sp, yT_d, b, h, y_, [by], [B_yT[b][h]])

            stages = []
            if stop_after != "xt":
                mixer_ab()
                if stop_after != "mix0":
                    gemm_res_ln("o0", ab_w_out[0], KC, yT_d, B_yT, 0, 0, False)
                    if stop_after != "ln00":
                        ffn1(0)
                        if stop_after != "ffn10":
                            gemm_res_ln("d0", ffn_w_down[0], FC, aT_d, B_aT, 0, 1, False)
                            if stop_after != "ln01":
                                mixer_cd()
                                if stop_after != "mix1":
                                    gemm_res_ln("o1", cd_w_out[0], KC, yT_d, B_yT, 1, 0, False)
                                    ffn1(1)
                                    gemm_res_ln("d1", ffn_w_down[1], FC, aT_d, B_aT, 1, 1, True)

        except StopBuild:
            pass
        k.finish()
    return nc


INPUT_NAMES = ["x", "c", "ada_w", "ada_b", "norm_g", "norm_b", "ffn_w_gate", "ffn_w_up", "ffn_w_down",
               "ab_w_in", "ab_conv_w", "ab_conv_b", "ab_w_r", "ab_b_r", "ab_w_i", "ab_b_i", "ab_lambda",
               "ab_vnorm_g", "ab_vnorm_b", "ab_w_s", "ab_b_s", "ab_w_out",
               "cd_w_in", "cd_w_pool", "cd_pool_scale", "cd_w_out"]


def kernel(**inputs):
    x = np.ascontiguousarray(np.asarray(inputs["x"], dtype=np.float32))
    B, S, D = x.shape
    NB = B // NCORES
    nc = build(NB, S)
    shared = {n: np.ascontiguousarray(np.asarray(inputs[n], dtype=np.float32)) for n in INPUT_NAMES if n not in ("x", "c")}
    cfull = np.ascontiguousarray(np.asarray(inputs["c"], dtype=np.float32))
    in_maps = []
    for i in range(NCORES):
        m = dict(shared)
        m["x"] = x[i * NB:(i + 1) * NB].reshape(NB * S, D)
        m["c"] = cfull[i * NB:(i + 1) * NB]
        in_maps.append(m)
    res = run_bass_kernel_spmd(nc, in_maps, core_ids=list(range(NCORES)))
    outs = [np.asarray(r["out"]).reshape(NB, S, D) for r in res.results]
    return np.concatenate(outs, axis=0).astype(np.float32)
```

```python
from contextlib import ExitStack, contextmanager
import numpy as np
import concourse.bass as bass
import concourse.mybir as mybir
from concourse.bass_utils import run_bass_kernel_spmd

F32 = mybir.dt.float32
BF16 = mybir.dt.bfloat16
AF = mybir.ActivationFunctionType
ALU = mybir.AluOpType

DM = 2048
KC = DM // 128
DFF = 5632
FC = DFF // 128
NL = 2
NCORES = 8
ALPHA = (2 * NL) ** 0.25
EPS = 1e-5
POOL_W = (2, 4, 8, 16)
SEM_LIMIT = 30000
import os
XTF = os.environ.get('XTF', '').split(',')


class StopBuild(Exception):
    pass


class Buf:
    __slots__ = ("name", "w", "r")

    def __init__(self, name=""):
        self.name = name
        self.w = None
        self.r = []


class Eng:
    def __init__(self, name, h, is_pe=False):
        self.name = name
        self.h = h
        self.semidx = None
        self.tick = 0
        self.waited = {}
        self.prog = []
        self.pending = []
        self.is_pe = is_pe
        self.ring = []
        self.ring_pos = 0


class K:
    def __init__(self, nc, same_sync=True, ring=6):
        self.nc = nc
        self.same_sync = same_sync
        self.sems = []
        self.ring_n = ring
        self.live = {}
        self.halted = False

    def newsem(self, name):
        s = self.stack.enter_context(self.nc.semaphore(name))
        self.sems.append(s)
        return len(self.sems) - 1

    def start(self, stack):
        nc = self.nc
        self.stack = stack
        self.pe = Eng("pe", nc.tensor, is_pe=True)
        self.act = Eng("act", nc.scalar)
        self.dve = Eng("dve", nc.vector)
        self.pool = Eng("pool", nc.gpsimd)
        self.sp = Eng("sp", nc.sync)
        self.engs = [self.pe, self.act, self.dve, self.pool, self.sp]
        for e in self.engs:
            e.semidx = self.newsem(f"t_{e.name}0")
            e.nsem = 1
        for e in (self.sp, self.act, self.pool):
            e.ring = [[self.newsem(f"r_{e.name}{i}"), 0] for i in range(self.ring_n)]

    def _deps(self, e, reads, writes):
        deps = {}

        def need(tok):
            if tok is None:
                return
            s, v = tok[0], tok[1]
            if tok[2] is e and (e.is_pe or not self.same_sync):
                return
            if v is None:
                raise RuntimeError(f"dependency on pending op ({tok[2].name}) from {e.name}")
            if deps.get(s, 0) < v:
                deps[s] = v

        for b in reads:
            need(b.w)
        for b in writes:
            need(b.w)
            for t in b.r:
                need(t)
        waits = []
        for s, v in deps.items():
            if e.waited.get(s, 0) < v:
                e.waited[s] = v
                waits.append((s, v))
        return waits

    def _record(self, tok, reads, writes):
        for b in reads:
            b.r.append(tok)
        for b in writes:
            b.w = tok
            b.r = []

    def op(self, e, fn, reads=(), writes=(), tick=True):
        if self.halted:
            return
        waits = self._deps(e, reads, writes)
        if tick:
            if e.tick >= SEM_LIMIT:
                if e.pending:
                    raise RuntimeError("sem rollover with pending ops")
                e.semidx = self.newsem(f"t_{e.name}{e.nsem}")
                e.nsem += 1
                e.tick = 0
            e.tick += 1
            tok = [e.semidx, e.tick, e]
            for p in e.pending:
                p[0] = e.semidx
                p[1] = e.tick
            e.pending = []
            inc = (e.semidx, 1)
            self.live[e.semidx] = e.tick
        else:
            tok = [e.semidx, None, e]
            e.pending.append(tok)
            inc = None
        e.prog.append((waits, fn, inc))
        self._record(tok, reads, writes)

    def dma(self, q, out, in_, reads=(), writes=(), **kw):
        self.dma_multi(q, [(out, in_)], reads, writes, **kw)

    def dma_multi(self, q, pairs, reads=(), writes=(), **kw):
        if self.halted:
            return
        waits = self._deps(q, reads, writes)
        slot = q.ring[q.ring_pos]
        if slot[1] + 16 * len(pairs) > SEM_LIMIT:
            slot = q.ring[q.ring_pos] = [self.newsem(f"r_{q.name}x{len(self.sems)}"), 0]
        q.ring_pos = (q.ring_pos + 1) % len(q.ring)
        s = slot[0]
        if slot[1] > 0 and q.waited.get(s, 0) < slot[1]:
            q.waited[s] = slot[1]
            waits.append((s, slot[1]))
        for i, (o, i_) in enumerate(pairs):
            slot[1] += 16

            def fn(o=o, i_=i_):
                return q.h.dma_start(out=o, in_=i_, **kw)
            q.prog.append((waits if i == 0 else [], fn, (s, 16)))
        self.live[s] = slot[1]
        tok = [s, slot[1], None]
        self._record(tok, reads, writes)

    def barrier(self):
        for e in self.engs:
            if e.pending:
                raise RuntimeError(f"barrier with pending ops on {e.name}")
        for e in self.engs:
            waits = []
            for s, v in self.live.items():
                if e.waited.get(s, 0) < v:
                    e.waited[s] = v
                    waits.append((s, v))
            if waits:
                e.prog.append((waits, None, None))

    def finish(self):
        nc = self.nc
        sems = self.sems
        self.barrier()

        def replay(e):
            def run(h):
                for waits, fn, inc in e.prog:
                    for s, v in waits:
                        h.wait_ge(sems[s], v)
                    if fn is None:
                        continue
                    ins = fn()
                    if inc is not None:
                        ins.then_inc(sems[inc[0]], inc[1])
            return run

        with nc.Block() as block:
            block.tensor(replay(self.pe))
            block.scalar(replay(self.act))
            block.vector(replay(self.dve))
            block.gpsimd(replay(self.pool))
            block.sync(replay(self.sp))


def build(NB, S, dbg=False, stop_after=None, same_sync=True):
    nc = bass.Bass("TRN2", target_bir_lowering=False)
    T = NB * S
    NSB = S // 512
    NST = S // 128
    skind = "ExternalOutput" if dbg else "Internal"

    def din(name, shape):
        return nc.dram_tensor(name, list(shape), F32, kind="ExternalInput").ap()

    x = din("x", [T, DM])
    c = din("c", [NB, DM])
    ada_w = din("ada_w", [NL, DM, 6 * DM])
    ada_b = din("ada_b", [NL, 6 * DM])
    norm_g = din("norm_g", [NL, 2, DM])
    norm_b = din("norm_b", [NL, 2, DM])
    ffn_w_gate = din("ffn_w_gate", [NL, DM, DFF])
    ffn_w_up = din("ffn_w_up", [NL, DM, DFF])
    ffn_w_down = din("ffn_w_down", [NL, DFF, DM])
    ab_w_in = din("ab_w_in", [1, DM, 4096])
    ab_conv_w = din("ab_conv_w", [1, 4, 1024])
    ab_conv_b = din("ab_conv_b", [1, 1024])
    ab_w_r = din("ab_w_r", [1, 8, 128, 128])
    ab_b_r = din("ab_b_r", [1, 8, 128])
    ab_w_i = din("ab_w_i", [1, 8, 128, 128])
    ab_b_i = din("ab_b_i", [1, 8, 128])
    ab_lambda = din("ab_lambda", [1, 1024])
    ab_vnorm_g = din("ab_vnorm_g", [1, 1024])
    ab_vnorm_b = din("ab_vnorm_b", [1, 1024])
    ab_w_s = din("ab_w_s", [1, 8, 128, 128])
    ab_b_s = din("ab_b_s", [1, 8, 128])
    ab_w_out = din("ab_w_out", [1, DM, DM])
    cd_w_in = din("cd_w_in", [1, DM, 4096])
    cd_w_pool = din("cd_w_pool", [1, 4, 256, 256])
    cd_pool_scale = din("cd_pool_scale", [1, 1024])
    cd_w_out = din("cd_w_out", [1, DM, DM])
    out = nc.dram_tensor("out", [T, DM], F32, kind="ExternalOutput").ap()

    xs_d = nc.dram_tensor("xs_d", [T // 512, 128, KC, 512], F32, kind=skind).ap()
    hT_d = nc.dram_tensor("hT_d", [T // 512, 128, KC, 512], BF16, kind=skind).ap()
    yT_d = nc.dram_tensor("yT_d", [T // 512, 128, KC, 512], BF16, kind=skind).ap()
    aT_d = nc.dram_tensor("aT_d", [T // 512, 128, FC, 512], BF16, kind=skind).ap()
    stat_d = nc.dram_tensor("stat_d", [T // 512, 128, 2, 512], F32, kind="Internal").ap()
    B_xs = [Buf(f"xs{i}") for i in range(T // 512)]
    B_hT = [Buf(f"hT{i}") for i in range(T // 512)]
    B_yT = [[Buf() for _ in range(KC)] for _ in range(NB)]
    B_aT = [[Buf() for _ in range(FC)] for _ in range(NB)]

    k = K(nc, same_sync=same_sync)
    pe, act, dve, pool, sp = None, None, None, None, None

    with ExitStack() as st:
        k.start(st)
        pe, act, dve, pool, sp = k.pe, k.act, k.dve, k.pool, k.sp

        try:
            uniq = [0]

            @contextmanager
            def scope():
                with ExitStack() as s2:
                    class Sc:
                        def sb(self, name, shape, dt=F32):
                            uniq[0] += 1
                            return s2.enter_context(nc.sbuf_tensor(f"{name}_{uniq[0]}", list(shape), dt))

                        def ps(self, name, shape, dt=F32):
                            uniq[0] += 1
                            return s2.enter_context(nc.psum_tensor(f"{name}_{uniq[0]}", list(shape), dt))
                    yield Sc()
                    k.barrier()

            def ACT(out_, in_, func, reads, writes, bias=None, scale=1.0):
                if bias is None:
                    k.op(act, lambda: nc.scalar.activation(out=out_, in_=in_, func=func, scale=scale), reads, writes)
                else:
                    k.op(act, lambda: nc.scalar.activation(out=out_, in_=in_, func=func, bias=bias, scale=scale), reads, writes)

            def MM(out_, lhsT, rhs, start, stop, reads, writes, tick):
                k.op(pe, lambda: nc.tensor.matmul(out_, lhsT=lhsT, rhs=rhs, start=start, stop=stop, skip_group_check=True),
                     reads, writes, tick=tick)

            def TR(out_, in_, ident, reads, writes, tick):
                k.op(pe, lambda: nc.tensor.transpose(out_, in_, ident), reads, writes, tick=tick)

            def TT(e, out_, in0, in1, op_, reads, writes):
                k.op(e, lambda: e.h.tensor_tensor(out=out_, in0=in0, in1=in1, op=op_), reads, writes)

            def TS(e, out_, in0, s1, s2, op0, op1, reads, writes):
                k.op(e, lambda: e.h.tensor_scalar(out=out_, in0=in0, scalar1=s1, scalar2=s2, op0=op0, op1=op1), reads, writes)

            def STT(out_, in0, scalar, in1, op0, op1, reads, writes):
                k.op(dve, lambda: nc.vector.scalar_tensor_tensor(out=out_, in0=in0, scalar=scalar, in1=in1, op0=op0, op1=op1),
                     reads, writes)

            def CP(e, out_, in_, reads, writes):
                if e is act:
                    k.op(e, lambda: nc.scalar.activation(out=out_, in_=in_, func=AF.Copy), reads, writes)
                else:
                    k.op(e, lambda: e.h.tensor_copy(out=out_, in_=in_), reads, writes)

            def SCAN(out_, d0, d1, init, reads, writes):
                k.op(dve, lambda: nc.vector.tensor_tensor_scan(out=out_, data0=d0, data1=d1, initial=init, op0=ALU.mult, op1=ALU.add),
                     reads, writes)

            def RECIP(out_, in_, reads, writes):
                k.op(dve, lambda: nc.vector.reciprocal(out=out_, in_=in_), reads, writes)

            def BNS(out_, in_, reads, writes):
                k.op(dve, lambda: nc.vector.bn_stats(out=out_, in_=in_), reads, writes)

            def BNA(out_, in_, reads, writes):
                k.op(dve, lambda: nc.vector.bn_aggr(out=out_, in_=in_), reads, writes)

            def MEMSET(e, ap, val, writes):
                k.op(e, lambda: e.h.memset(ap, val), [], writes)

            def dma_blk(q, sb_tile, X_d, tb, k0, k1, load, reads, writes, kg=16):
                pairs = []
                for a0 in range(k0, k1, kg):
                    a1 = min(k1, a0 + kg)
                    d = X_d[tb, :, a0:a1, :]
                    s_ = sb_tile[:, a0 - k0:a1 - k0, :]
                    pairs.append((s_, d) if load else (d, s_))
                k.dma_multi(q, pairs, reads, writes)

            def dma_rows(q, X_d, b, ch, sb_row, reads, writes):
                pairs = [(X_d[b * NSB + tbk, :, ch, :], sb_row[:, tbk * 512:(tbk + 1) * 512]) for tbk in range(NSB)]
                k.dma_multi(q, pairs, reads, writes)

            def checkpoint(name):
                if stop_after == name:
                    k.halted = True

            sbp = lambda name, shape, dt=F32: st.enter_context(nc.sbuf_tensor(name, list(shape), dt))
            cols = {}
            ncol = [0]

            def colgrp(name, n=16):
                cols[name] = ncol[0]
                ncol[0] += n
                return cols[name]

            raw_groups = []
            for l in range(NL):
                for i in range(2):
                    raw_groups.append((f"ng{l}{i}", norm_g[l, i], 16))
                    raw_groups.append((f"nb{l}{i}", norm_b[l, i], 16))
            for l in range(NL):
                raw_groups.append((f"adab{l}", ada_b[l], 96))
            for j in range(4):
                raw_groups.append((f"cw{j}", ab_conv_w[0, j], 8))
            raw_groups.append(("cb", ab_conv_b[0], 8))
            raw_groups.append(("br", ab_b_r[0].rearrange("h e -> (h e)"), 8))
            raw_groups.append(("bi", ab_b_i[0].rearrange("h e -> (h e)"), 8))
            raw_groups.append(("lam", ab_lambda[0], 8))
            raw_groups.append(("psc", cd_pool_scale[0], 8))
            for b in range(NB):
                raw_groups.append((f"cT{b}", c[b], 16))
            for nm, _, n_ in raw_groups:
                colgrp(nm, n_)
            NRAW = ncol[0]
            for l in range(NL):
                for i in range(2):
                    colgrp(f"xss{l}{i}")
                    colgrp(f"xsb{l}{i}")
                    for b in range(NB):
                        colgrp(f"hs{l}{i}{b}")
                        colgrp(f"hb{l}{i}{b}")
                for b in range(NB):
                    colgrp(f"g1{l}{b}")
                    colgrp(f"g2{l}{b}")
                    colgrp(f"s1p{l}{b}")
                    colgrp(f"s2p{l}{b}")
            for b in range(NB):
                colgrp(f"hs_in{b}")
            colgrp("lam2", 8)
            colgrp("zero", 1)
            colgrp("eps", 1)
            colgrp("one", 1)
            NTAB = ncol[0]
            tab = sbp("tab", [128, NTAB])
            B_tab = Buf("tab")
            modr = sbp("modr", [128, NL, NB, 96])
            B_modr = Buf("modr")
            ident = sbp("ident", [128, 128])
            ident_bf = sbp("ident_bf", [128, 128], BF16)
            ones_f = sbp("ones_f", [128, 128])
            B_const = Buf("const")

            def col(name, i=0, n=1):
                c0 = cols[name] + i
                return tab[:, c0:c0 + n]

            MEMSET(pool, ident[:], 1.0, [B_const])
            k.op(pool, lambda: nc.gpsimd.affine_select(out=ident[:], in_=ident[:], compare_op=ALU.is_equal, fill=0.0,
                                                       base=0, pattern=[[-1, 128]], channel_multiplier=1), [B_const], [B_const])
            CP(pool, ident_bf[:], ident[:], [B_const], [B_const])
            MEMSET(pool, ones_f[:], 1.0, [B_const])
            MEMSET(dve, col("zero"), 0.0, [B_tab])
            MEMSET(dve, col("eps"), EPS, [B_tab])

            MEMSET(dve, col("one"), 1.0, [B_tab])
            with scope() as sc:
                nst_ = (NRAW + 127) // 128
                stg = [sc.sb(f"stg{j}", [128, 128]) for j in range(nst_)]
                B_stg = [Buf() for _ in range(nst_)]
                tps_ = sc.ps("tps_", [128, 128])
                B_tps_ = Buf()
                for j in range(nst_):
                    MEMSET(dve, stg[j][:], 0.0, [B_stg[j]])
                for nm, vec, n_ in raw_groups:
                    c0 = cols[nm]
                    r = 0
                    while r < n_:
                        j, p0 = divmod(c0 + r, 128)
                        m_ = min(n_ - r, 128 - p0)
                        k.dma(sp, stg[j][p0:p0 + m_, :], vec.rearrange("(c p) -> c p", p=128)[r:r + m_, :], writes=[B_stg[j]])
                        r += m_
                for j in range(nst_):
                    w_ = min(128, NRAW - j * 128)
                    TR(tps_[:], stg[j][:], ident[:], [B_stg[j], B_const], [B_tps_], tick=True)
                    CP(dve, tab[:, j * 128:j * 128 + w_], tps_[:, 0:w_], [B_tps_], [B_tab])
            checkpoint("pro_a")
            ACT(col("lam2", 0, 8), col("lam", 0, 8), AF.Exp, [B_tab], [B_tab], scale=-1.0)
            ACT(col("lam2", 0, 8), col("lam2", 0, 8), AF.Ln, [B_tab], [B_tab], bias=col("one"))
            TS(dve, col("lam", 0, 8), col("lam2", 0, 8), -8.0, None, ALU.mult, ALU.bypass, [B_tab], [B_tab])
            TS(dve, col("lam2", 0, 8), col("lam2", 0, 8), -16.0, None, ALU.mult, ALU.bypass, [B_tab], [B_tab])

            checkpoint("pro_b")
            with scope() as sc:
                cT = sc.sb("cT", [128, KC, NB])
                B_cT = Buf("cT")
                for b in range(NB):
                    ACT(cT[:, :, b], col(f"cT{b}", 0, 16), AF.Silu, [B_tab], [B_cT])
                CBW = 3072
                aw = [sc.sb(f"aw{i}", [128, CBW]) for i in range(2)]
                B_aw = [Buf(f"aw{i}") for i in range(2)]
                mp = [sc.ps(f"mp{l}", [128, 96 * NB]) for l in range(NL)]
                B_mp = [Buf(f"mp{l}") for l in range(NL)]
                n = 0
                for l in range(NL):
                    for kk in range(KC):
                        for cb in range(6 * DM // CBW):
                            t = aw[n % 2]
                            bt = B_aw[n % 2]
                            n += 1
                            k.dma(sp if n % 2 else act, t[:], ada_w[l, kk * 128:(kk + 1) * 128, cb * CBW:(cb + 1) * CBW], writes=[bt])
                            for jj in range(CBW // 128):
                                j = cb * (CBW // 128) + jj
                                last = (kk == KC - 1) and (j == 95)
                                MM(mp[l][:, j * NB:(j + 1) * NB], t[:, jj * 128:(jj + 1) * 128], cT[:, kk, :],
                                   start=(kk == 0 and j == 0), stop=(kk == KC - 1), reads=[bt, B_cT], writes=[B_mp[l]],
                                   tick=(jj == CBW // 128 - 1))
                    for b in range(NB):
                        TT(dve, modr[:, l, b, :], mp[l][:].rearrange("p (j b) -> p j b", b=NB)[:, :, b],
                           col(f"adab{l}", 0, 96), ALU.add, [B_mp[l], B_tab], [B_modr])
                checkpoint("pro_c")
                for l in range(NL):
                    for b in range(NB):
                        TS(dve, col(f"g1{l}{b}", 0, 16), modr[:, l, b, 32:48], 1.0, None, ALU.add, ALU.bypass, [B_modr], [B_tab])
                        TS(dve, col(f"g2{l}{b}", 0, 16), modr[:, l, b, 80:96], 1.0, None, ALU.add, ALU.bypass, [B_modr], [B_tab])
                        TS(dve, col(f"s1p{l}{b}", 0, 16), modr[:, l, b, 16:32], 1.0, None, ALU.add, ALU.bypass, [B_modr], [B_tab])
                        TS(dve, col(f"s2p{l}{b}", 0, 16), modr[:, l, b, 64:80], 1.0, None, ALU.add, ALU.bypass, [B_modr], [B_tab])
                for l in range(NL):
                    for i in range(2):
                        g_ = col(f"ng{l}{i}", 0, 16)
                        b_ = col(f"nb{l}{i}", 0, 16)
                        TS(dve, col(f"xss{l}{i}", 0, 16), g_, ALPHA, None, ALU.mult, ALU.bypass, [B_tab], [B_tab])
                        TS(dve, col(f"xsb{l}{i}", 0, 16), b_, ALPHA, None, ALU.mult, ALU.bypass, [B_tab], [B_tab])
                        if i == 1 and l == NL - 1:
                            continue
                        for b in range(NB):
                            if i == 0:
                                sp_ = col(f"s2p{l}{b}", 0, 16)
                                sh_ = modr[:, l, b, 48:64]
                            else:
                                sp_ = col(f"s1p{l + 1}{b}", 0, 16)
                                sh_ = modr[:, l + 1, b, 0:16]
                            TT(dve, col(f"hs{l}{i}{b}", 0, 16), g_, sp_, ALU.mult, [B_tab], [B_tab])
                            TT(dve, col(f"hb{l}{i}{b}", 0, 16), b_, sp_, ALU.mult, [B_tab], [B_tab])
                            TT(dve, col(f"hb{l}{i}{b}", 0, 16), col(f"hb{l}{i}{b}", 0, 16), sh_, ALU.add, [B_tab, B_modr], [B_tab])

            for b in range(NB):
                TS(dve, col(f"hs_in{b}", 0, 16), col(f"s1p0{b}", 0, 16), 1.0 / ALPHA, None, ALU.mult, ALU.bypass, [B_tab], [B_tab])
            checkpoint("pro_d")
            with scope() as sc:
                xt = [sc.sb(f"xt{i}", [128, DM]) for i in range(2)]
                B_xt = [Buf() for _ in range(2)]
                xsb = sc.sb("xsb", [128, KC, 512])
                hb = sc.sb("hb", [128, KC, 512], F32 if "hbf32" in XTF else BF16)
                B_xsb = [[Buf() for _ in range(4)] for _ in range(KC)]
                B_hb = [[Buf() for _ in range(4)] for _ in range(KC)]
                allx = [B_xsb[cc_][q_] for cc_ in range(KC) for q_ in range(4)]
                allh = [B_hb[cc_][q_] for cc_ in range(KC) for q_ in range(4)]
                tp = [sc.ps(f"tp{i}", [128, 4, 128]) for i in range(4)]
                B_tp = [Buf() for _ in range(4)]
                n = 0
                for tb in range(T // 512):
                    b = tb // NSB
                    for q in range(4):
                        tt = tb * 4 + q
                        t, bt = xt[tt % 2], B_xt[tt % 2]
                        k.dma(sp, t[:], x[tt * 128:(tt + 1) * 128, :], writes=[bt])
                        for c4 in range(4):
                            p_, bp = tp[n % 4], B_tp[n % 4]
                            n += 1
                            for i in range(4):
                                cc = c4 * 4 + i
                                TR(p_[:, i, :], t[:, cc * 128:(cc + 1) * 128], ident[:], [bt, B_const], [bp], tick=(i == 3))
                            ACT(xsb[:, c4 * 4:(c4 + 1) * 4, q * 128:(q + 1) * 128], p_[:], AF.Copy, [bp], [B_xsb[c4 * 4 + i_][q] for i_ in range(4)], scale=ALPHA)
                            for i in range(4):
                                cc = c4 * 4 + i
                                if i % 2 == 1:
                                    ACT(hb[:, cc, q * 128:(q + 1) * 128], xsb[:, cc, q * 128:(q + 1) * 128], AF.Identity,
                                        [B_xsb[cc][q], B_tab, B_modr], [B_hb[cc][q]], scale=col(f"hs_in{b}", cc), bias=modr[:, 0, b, cc:cc + 1])
                                    continue
                                TS(dve, hb[:, cc, q * 128:(q + 1) * 128], xsb[:, cc, q * 128:(q + 1) * 128], col(f"hs_in{b}", cc), None,
                                   ALU.mult, ALU.bypass, [B_xsb[cc][q], B_tab], [B_hb[cc][q]])
                                TS(dve, hb[:, cc, q * 128:(q + 1) * 128], hb[:, cc, q * 128:(q + 1) * 128], modr[:, 0, b, cc:cc + 1], None,
                                   ALU.add, ALU.bypass, [B_hb[cc][q], B_modr], [B_hb[cc][q]])
                    if "nost" not in XTF:
                        dma_blk(sp, xsb, xs_d, tb, 0, KC, False, allx, [B_xs[tb]])
                        dma_blk(act, hb, hT_d, tb, 0, KC, False, allh, [B_hT[tb]])

            def load_w_bf(dst, src, nk, kstep, breads, bw):
                v = src.rearrange("(k p) n -> p k n", p=128)
                for k0 in range(0, nk, kstep):
                    k1 = min(nk, k0 + kstep)
                    k.dma(pool, dst[:, k0:k1, :], v[:, k0:k1, :], reads=breads, writes=[bw])

            def gemm_res_ln(name, W, nk, src_d, B_src, l, i, final):
                ND = 2 if nk <= 16 else 4
                CPD = KC // ND
                gname = "g1" if i == 0 else "g2"
                with scope() as so:
                    B_st = [Buf() for _ in range(T // 512)]
                    with scope() as sc:
                        NWB = 2
                        stg = [sc.sb("stg0", [128, 2, 512])]
                        B_stg = [Buf()]
                        Wsbs = [sc.sb(f"Wsb{j}", [128, nk, CPD * 128], BF16) for j in range(NWB)]
                        B_Ws = [Buf("W") for _ in range(NWB)]
                        ab = [sc.sb(f"ab{j}", [128, nk, 512], BF16) for j in range(2)]
                        B_ab = [Buf() for _ in range(2)]
                        rb = [sc.sb(f"rb{j}", [128, CPD, 512]) for j in range(2)]
                        B_rb = [[Buf() for _ in range(CPD)] for _ in range(2)]
                        sq = [sc.sb(f"sq{j}", [128, 512]) for j in range(2)]
                        B_sq = [Buf() for _ in range(2)]
                        acc = [sc.ps(f"acc{j}", [128, 512]) for j in range(3)]
                        B_acc = [Buf() for _ in range(3)]
                        pss = [sc.ps(f"pss{j}", [128, 512]) for j in range(2)]
                        psq = [sc.ps(f"psq{j}", [128, 512]) for j in range(2)]
                        B_pst = [Buf() for _ in range(2)]
                        nacc = nst = 0
                        items = [(dq, tb) for dq in range(ND) for tb in range(T // 512)]

                        def issue_loads(n):
                            dq, tb = items[n]
                            if tb == 0 and (NWB == 2 or dq == 0):
                                load_w_bf(Wsbs[dq % NWB], W[:, dq * CPD * 128:(dq + 1) * CPD * 128], nk, 4, [], B_Ws[dq % NWB])
                            b_ = tb // NSB
                            srcb = [B_src[b_][kk] for kk in range(nk)]
                            dma_blk(sp, ab[n % 2], src_d, tb, 0, nk, True, srcb, [B_ab[n % 2]])
                            dma_blk(sp, rb[n % 2], xs_d, tb, dq * CPD, (dq + 1) * CPD, True, [B_xs[tb]], B_rb[n % 2])

                        issue_loads(0)
                        for n, (dq, tb) in enumerate(items):
                            if n + 1 < len(items):
                                issue_loads(n + 1)
                            Wsb, B_W = Wsbs[dq % NWB], B_Ws[dq % NWB]
                            b = tb // NSB
                            a_, ba = ab[n % 2], B_ab[n % 2]
                            r_, br_ = rb[n % 2], B_rb[n % 2]
                            tsl = slice(tb * 512, (tb + 1) * 512)
                            ps_, pq_, bst = pss[nst % 2], psq[nst % 2], B_pst[nst % 2]
                            nst += 1
                            prev = None
                            for c8 in range(CPD + 1):
                                if c8 < CPD:
                                    cc = dq * CPD + c8
                                    ac, bac = acc[nacc % 3], B_acc[nacc % 3]
                                    s_, bs_ = sq[nacc % 2], B_sq[nacc % 2]
                                    nacc += 1
                                    for kk in range(nk):
                                        MM(ac[:], Wsb[:, kk, c8 * 128:(c8 + 1) * 128], a_[:, kk, :], kk == 0, kk == nk - 1,
                                           [B_W, ba], [bac], tick=(kk == nk - 1))
                                if prev is not None:
                                    pc8, ps_sq, pbs = prev
                                    MM(ps_[:], ones_f[:], r_[:, pc8, :], pc8 == 0, pc8 == CPD - 1, [B_const, br_[pc8]], [bst], tick=False)
                                    MM(pq_[:], ones_f[:], ps_sq[:], pc8 == 0, pc8 == CPD - 1, [B_const, pbs], [bst], tick=True)
                                if c8 < CPD:
                                    STT(r_[:, c8, :], ac[:], col(f"{gname}{l}{b}", cc), r_[:, c8, :], ALU.mult, ALU.add,
                                        [bac, br_[c8], B_tab], [br_[c8]])
                                    ACT(s_[:], r_[:, c8, :], AF.Square, [br_[c8]], [bs_])
                                    prev = (c8, s_, bs_)
                            sg_, bsg = stg[0], B_stg[0]
                            if dq == 0:
                                CP(act, sg_[:, 0, :], ps_[:], [bst], [bsg])
                                CP(dve, sg_[:, 1, :], pq_[:], [bst], [bsg])
                            else:
                                TT(dve, sg_[:, 0, :], sg_[:, 0, :], ps_[:], ALU.add, [bst, bsg], [bsg])
                                TT(dve, sg_[:, 1, :], sg_[:, 1, :], pq_[:], ALU.add, [bst, bsg], [bsg])
                            k.dma(sp, stat_d[tb], sg_[:], reads=[bsg], writes=[B_st[tb]])
                            if n + 1 < len(items) and items[n + 1][0] > 0:
                                tbn = items[n + 1][1]
                                k.dma(sp, sg_[:], stat_d[tbn], reads=[B_st[tbn]], writes=[bsg])
                            dma_blk(sp, r_, xs_d, tb, dq * CPD, (dq + 1) * CPD, False, br_, [B_xs[tb]])
                            if NWB == 1 and n + 1 < len(items) and items[n + 1][1] == 0:
                                dq2 = items[n + 1][0]
                                load_w_bf(Wsbs[0], W[:, dq2 * CPD * 128:(dq2 + 1) * CPD * 128], nk, 4, [], B_Ws[0])
                    with scope() as sc:
                        r16s = [sc.sb(f"r16{j}", [128, KC, 512]) for j in range(2)]
                        B_r16s = [[Buf() for _ in range(KC)] for _ in range(2)]
                        h16s = [sc.sb(f"h16{j}", [128, KC, 512], BF16) for j in range(2)]
                        B_h16s = [[Buf() for _ in range(KC)] for _ in range(2)]
                        mt = sc.sb("mt", [128, 512])
                        rstd = sc.sb("rstd", [128, 512])
                        nmr = sc.sb("nmr", [128, 512])
                        B_ms = Buf()
                        if final:
                            otok = [sc.sb(f"otok{j}", [128, DM]) for j in range(2)]
                            B_ot = [Buf() for _ in range(2)]
                            tpo = [sc.ps(f"tpo{j}", [128, 4, 128]) for j in range(2)]
                            B_tpo = [Buf() for _ in range(2)]
                        stb = [sc.sb(f"stb{j}", [128, 2, 512]) for j in range(2)]
                        B_stb = [Buf() for _ in range(2)]
                        dma_blk(sp, r16s[0], xs_d, 0, 0, KC, True, [B_xs[0]], B_r16s[0])
                        k.dma(sp, stb[0][:], stat_d[0], reads=[B_st[0]], writes=[B_stb[0]])
                        for tb in range(T // 512):
                            b = tb // NSB
                            tsl = slice(tb * 512, (tb + 1) * 512)
                            r16, B_r16 = r16s[tb % 2], B_r16s[tb % 2]
                            h16, B_h16 = h16s[tb % 2], B_h16s[tb % 2]
                            if tb + 1 < T // 512:
                                dma_blk(sp, r16s[(tb + 1) % 2], xs_d, tb + 1, 0, KC, True, [B_xs[tb + 1]], B_r16s[(tb + 1) % 2])
                                k.dma(sp, stb[(tb + 1) % 2][:], stat_d[tb + 1], reads=[B_st[tb + 1]], writes=[B_stb[(tb + 1) % 2]])
                            ssum_, ssq_, bstb = stb[tb % 2][:, 0, :], stb[tb % 2][:, 1, :], B_stb[tb % 2]
                            TS(dve, mt[:], ssum_, 1.0 / DM, None, ALU.mult, ALU.bypass, [bstb], [B_ms])
                            TT(dve, nmr[:], mt[:], mt[:], ALU.mult, [B_ms], [B_ms])
                            STT(rstd[:], ssq_, 1.0 / DM, nmr[:], ALU.mult, ALU.subtract, [bstb, B_ms], [B_ms])
                            ACT(rstd[:], rstd[:], AF.Sqrt, [B_ms, B_tab], [B_ms], bias=col("eps"))
                            RECIP(rstd[:], rstd[:], [B_ms], [B_ms])
                            STT(nmr[:], mt[:], -1.0, rstd[:], ALU.mult, ALU.mult, [B_ms], [B_ms])
                            for cc in range(KC):
                                rc = r16[:, cc, :]
                                TT(dve, rc, rc, rstd[:], ALU.mult, [B_r16[cc], B_ms], [B_r16[cc]])
                                TT(pool, rc, rc, nmr[:], ALU.add, [B_r16[cc], B_ms], [B_r16[cc]])
                                if final:
                                    ACT(rc, rc, AF.Identity, [B_r16[cc], B_tab], [B_r16[cc]], bias=col(f"nb{l}{i}", cc), scale=col(f"ng{l}{i}", cc))
                                else:
                                    ACT(h16[:, cc, :], rc, AF.Identity, [B_r16[cc], B_tab], [B_h16[cc]],
                                        bias=col(f"hb{l}{i}{b}", cc), scale=col(f"hs{l}{i}{b}", cc))
                                    ACT(rc, rc, AF.Identity, [B_r16[cc], B_tab], [B_r16[cc]], bias=col(f"xsb{l}{i}", cc), scale=col(f"xss{l}{i}", cc))
                            if final:
                                for q in range(4):
                                    tt = tb * 4 + q
                                    ot, bot = otok[tt % 2], B_ot[tt % 2]
                                    for c4 in range(4):
                                        p_, bp = tpo[c4 % 2], B_tpo[c4 % 2]
                                        for ii in range(4):
                                            cc = c4 * 4 + ii
                                            TR(p_[:, ii, :], r16[:, cc, q * 128:(q + 1) * 128], ident[:], [B_r16[cc], B_const], [bp], tick=(ii == 3))
                                        CP(act if c4 % 2 else dve, ot[:, c4 * 512:(c4 + 1) * 512], p_[:].rearrange("p a b -> p (a b)"), [bp], [bot])
                                    k.dma(sp, out[tt * 128:(tt + 1) * 128, :], ot[:], reads=[bot])
                            else:
                                dma_blk(sp, r16, xs_d, tb, 0, KC, False, B_r16, [B_xs[tb]])
                                dma_blk(sp, h16, hT_d, tb, 0, KC, False, B_h16, [B_hT[tb]])

            def ffn1(l):
                JG = 4
                with scope() as sc:
                    hseq = sc.sb("hseq", [128, NSB, KC, 512], BF16)
                    B_hs = Buf()
                    wg = [sc.sb(f"wg{j}", [128, KC, JG * 128], BF16) for j in range(2)]
                    wu = [sc.sb(f"wu{j}", [128, KC, JG * 128], BF16) for j in range(2)]
                    B_wg = [Buf() for _ in range(2)]
                    sg = [sc.sb(f"sg{j}", [128, 512]) for j in range(2)]
                    B_sg = [Buf() for _ in range(2)]
                    aj = [sc.sb(f"aj{j}", [128, S], BF16) for j in range(2)]
                    B_aj = [Buf() for _ in range(2)]
                    pg = [sc.ps(f"pg{j}", [128, 512]) for j in range(2)]
                    pu = [sc.ps(f"pu{j}", [128, 512]) for j in range(2)]
                    B_pg = [Buf() for _ in range(2)]
                    B_pu = [Buf() for _ in range(2)]
                    nw = nj = np_ = 0
                    for b in range(NB):
                        for tbk in range(NSB):
                            tb = b * NSB + tbk
                            dma_blk(sp, hseq[:, tbk], hT_d, tb, 0, KC, True, [B_hT[tb]], [B_hs])
                        for jg in range(FC // JG):
                            wg_, wu_, bw = wg[nw % 2], wu[nw % 2], B_wg[nw % 2]
                            nw += 1
                            csl = slice(jg * JG * 128, (jg + 1) * JG * 128)
                            load_w_bf(wg_, ffn_w_gate[l][:, csl], KC, 8, [], bw)
                            load_w_bf(wu_, ffn_w_up[l][:, csl], KC, 8, [], bw)
                            for jj in range(JG):
                                j = jg * JG + jj
                                a_, ba = aj[nj % 2], B_aj[nj % 2]
                                nj += 1
                                for tbk in range(NSB):
                                    g_, bg = pg[np_ % 2], B_pg[np_ % 2]
                                    u_, bu = pu[np_ % 2], B_pu[np_ % 2]
                                    s_, bs_ = sg[np_ % 2], B_sg[np_ % 2]
                                    np_ += 1
                                    tsl = slice(tbk * 512, (tbk + 1) * 512)
                                    for kk in range(KC):
                                        MM(g_[:], wg_[:, kk, jj * 128:(jj + 1) * 128], hseq[:, tbk, kk, :], kk == 0, kk == KC - 1,
                                           [bw, B_hs], [bg], tick=(kk == KC - 1))
                                    for kk in range(KC):
                                        MM(u_[:], wu_[:, kk, jj * 128:(jj + 1) * 128], hseq[:, tbk, kk, :], kk == 0, kk == KC - 1,
                                           [bw, B_hs], [bu], tick=(kk == KC - 1))
                                    ACT(s_[:], g_[:], AF.Silu, [bg], [bs_])
                                    TT(dve, a_[:, tsl], s_[:], u_[:], ALU.mult, [bs_, bu], [ba])
                                dma_rows(sp, aT_d, b, j, a_, [ba], [B_aT[b][j]])

            def mixer_ab():
                w_in = ab_w_in[0]
                with scope() as sc:
                    hseq = sc.sb("hseq", [128, NSB, KC, 512], BF16)
                    B_hs = Buf()
                    wr = sc.sb("wr", [128, 8, 128], BF16)
                    wi = sc.sb("wi", [128, 8, 128], BF16)
                    wsT = sc.sb("wsT", [128, 8, 128], BF16)
                    bsrow = sc.sb("bsrow", [1, 1024], BF16)
                    onesrow = sc.sb("onesrow", [1, 128], BF16)
                    vng = sc.sb("vng", [128, 1024])
                    vnb = sc.sb("vnb", [128, 1024])
                    B_c0 = Buf("l0const")
                    k.dma(pool, wr[:], ab_w_r[0].rearrange("h d e -> d h e"), writes=[B_c0])
                    k.dma(pool, wi[:], ab_w_i[0].rearrange("h d e -> d h e"), writes=[B_c0])
                    k.dma(pool, bsrow[:], ab_b_s[0].rearrange("g t -> (g t)").rearrange("(o n) -> o n", o=1), writes=[B_c0])
                    MEMSET(dve, onesrow[:], 1.0, [B_c0])
                    k.dma(sp, vng[:], ab_vnorm_g[0].partition_broadcast(128), writes=[B_c0])
                    k.dma(sp, vnb[:], ab_vnorm_b[0].partition_broadcast(128), writes=[B_c0])
                    with scope() as s2:
                        wsf = s2.sb("wsf", [128, 8, 128])
                        B_wsf = Buf()
                        tps = s2.ps("tps", [128, 4, 128])
                        B_tps = Buf()
                        k.dma(sp, wsf[:], ab_w_s[0].rearrange("g t s -> t g s"), writes=[B_wsf])
                        for g4 in range(2):
                            for ii in range(4):
                                TR(tps[:, ii, :], wsf[:, g4 * 4 + ii, :], ident[:], [B_wsf, B_const], [B_tps], tick=(ii == 3))
                            CP(dve, wsT[:, g4 * 4:(g4 + 1) * 4, :], tps[:], [B_tps], [B_c0])
                        MEMSET(dve, wsT[64:128, :, 0:64], 0.0, [B_c0])

                    for b in range(NB):
                        for tbk in range(NSB):
                            tb = b * NSB + tbk
                            dma_blk(sp, hseq[:, tbk], hT_d, tb, 0, KC, True, [B_hT[tb]], [B_hs])
                        with scope() as s2:
                            wxa = [s2.sb(f"wxa{j}", [128, KC, 128], BF16) for j in range(2)]
                            wga = [s2.sb(f"wga{j}", [128, KC, 128], BF16) for j in range(2)]
                            B_wx = [Buf() for _ in range(2)]
                            xa = [s2.sb(f"xa{j}", [128, 3 + S]) for j in range(2)]
                            gg = [s2.sb(f"gg{j}", [128, S], BF16) for j in range(2)]
                            xc = [s2.sb(f"xc{j}", [128, S]) for j in range(2)]
                            xcb = [s2.sb(f"xcb{j}", [128, S], BF16) for j in range(2)]
                            rr = [s2.sb(f"rr{j}", [128, S]) for j in range(2)]
                            iu = [s2.sb(f"iu{j}", [128, S]) for j in range(2)]
                            aa = [s2.sb(f"aa{j}", [128, S]) for j in range(2)]
                            ya = [s2.sb(f"ya{j}", [128, S], BF16) for j in range(2)]
                            B_xa, B_gg, B_xc, B_xcb, B_rr, B_iu, B_aa = ([Buf() for _ in range(2)] for _ in range(7))
                            B_ya = [Buf() for _ in range(2)]
                            pxa = [s2.ps(f"pxa{j}", [128, 512]) for j in range(2)]
                            pga = [s2.ps(f"pga{j}", [128, 512]) for j in range(2)]
                            pr = [s2.ps(f"pr{j}", [128, 512]) for j in range(2)]
                            pi = [s2.ps(f"pi{j}", [128, 512]) for j in range(2)]
                            B_pxa, B_pga, B_pr, B_pi = ([Buf() for _ in range(2)] for _ in range(4))
                            for j in range(2):
                                MEMSET(dve, xa[j][:, 0:3], 0.0, [B_xa[j]])
                            cntl = {"px": 0, "pg": 0}

                            def stP(h):
                                st_ = h % 2
                                wx_, wg_, bw = wxa[st_], wga[st_], B_wx[st_]
                                load_w_bf(wx_, w_in[:, h * 128:(h + 1) * 128], KC, 16, [], bw)
                                load_w_bf(wg_, w_in[:, 1024 + h * 128:1024 + (h + 1) * 128], KC, 16, [], bw)
                                for tbk in range(NSB):
                                    tsl = slice(tbk * 512, (tbk + 1) * 512)
                                    j_ = cntl["px"] % 2
                                    cntl["px"] += 1
                                    p1, b1 = pxa[j_], B_pxa[j_]
                                    p2, b2 = pga[j_], B_pga[j_]
                                    for kk in range(KC):
                                        MM(p1[:], wx_[:, kk, :], hseq[:, tbk, kk, :], kk == 0, kk == KC - 1, [bw, B_hs], [b1], tick=(kk == KC - 1))
                                    for kk in range(KC):
                                        MM(p2[:], wg_[:, kk, :], hseq[:, tbk, kk, :], kk == 0, kk == KC - 1, [bw, B_hs], [b2], tick=(kk == KC - 1))
                                    ACT(xa[st_][:, 3 + tbk * 512:3 + (tbk + 1) * 512], p1[:], AF.Copy, [b1], [B_xa[st_]])
                                    ACT(gg[st_][:, tsl], p2[:], AF.Gelu_apprx_tanh, [b2], [B_gg[st_]])

                            def stE(h):
                                st_ = h % 2
                                xa_, gg_, xc_, xcb_, rr_, iu_, aa_ = xa[st_], gg[st_], xc[st_], xcb[st_], rr[st_], iu[st_], aa[st_]
                                bxa, bgg, bxc, bxcb, brr, biu, baa = B_xa[st_], B_gg[st_], B_xc[st_], B_xcb[st_], B_rr[st_], B_iu[st_], B_aa[st_]
                                ACT(xc_[:], xa_[:, 3:3 + S], AF.Identity, [bxa, B_tab], [bxc], scale=col("cw3", h), bias=col("cb", h))
                                for j in (2, 1, 0):
                                    STT(xc_[:], xa_[:, j:j + S], col(f"cw{j}", h), xc_[:], ALU.mult, ALU.add, [bxa, bxc, B_tab], [bxc])
                                CP(pool, xcb_[:], xc_[:], [bxc], [bxcb])
                                for tbk in range(NSB):
                                    tsl = slice(tbk * 512, (tbk + 1) * 512)
                                    j_ = cntl["pg"] % 2
                                    cntl["pg"] += 1
                                    p1, b1 = pr[j_], B_pr[j_]
                                    p2, b2 = pi[j_], B_pi[j_]
                                    MM(p1[:], wr[:, h, :], xcb_[:, tsl], True, True, [B_c0, bxcb], [b1], tick=True)
                                    MM(p2[:], wi[:, h, :], xcb_[:, tsl], True, True, [B_c0, bxcb], [b2], tick=True)
                                    ACT(rr_[:, tsl], p1[:], AF.Sigmoid, [b1, B_tab], [brr], bias=col("br", h))
                                    ACT(iu_[:, tsl], p2[:], AF.Sigmoid, [b2, B_tab], [biu], bias=col("bi", h))
                                ACT(aa_[:], rr_[:], AF.Exp, [brr, B_tab], [baa], scale=col("lam", h))
                                ACT(rr_[:], rr_[:], AF.Exp, [brr, B_tab], [brr], scale=col("lam2", h))
                                ACT(rr_[:], rr_[:], AF.Sqrt, [brr, B_tab], [brr], scale=-1.0, bias=col("one"))
                                TT(dve, iu_[:], iu_[:], xc_[:], ALU.mult, [biu, bxc], [biu])
                                TT(dve, iu_[:], iu_[:], rr_[:], ALU.mult, [biu, brr], [biu])
                                SCAN(xa_[:, 3:3 + S], aa_[:], iu_[:], col("zero"), [baa, biu, B_tab], [bxa])
                                y_, by = ya[st_], B_ya[st_]
                                TT(dve, y_[:], xa_[:, 3:3 + S], gg_[:], ALU.mult, [bxa, bgg], [by])
                                dma_rows(sp, yT_d, b, h, y_, [by], [B_yT[b][h]])

                            stP(0)
                            for h in range(8):
                                if h + 1 < 8:
                                    stP(h + 1)
                                stE(h)
                        with scope() as s2:
                            wub = s2.sb("wub", [128, KC, 1024], BF16)
                            wvb = s2.sb("wvb", [128, KC, 1024], BF16)
                            B_wb = Buf()
                            load_w_bf(wub, w_in[:, 2048:3072], KC, 4, [], B_wb)
                            load_w_bf(wvb, w_in[:, 3072:4096], KC, 4, [], B_wb)
                            ug = s2.sb("ug", [128, 8, 512])
                            B_ug = Buf()
                            vf = [s2.sb(f"vf{j}", [128, 1024]) for j in range(2)]
                            vt = [s2.sb(f"vt{j}", [128, 1024], BF16) for j in range(2)]
                            B_vf = [Buf() for _ in range(2)]
                            B_vt = [Buf() for _ in range(2)]
                            st6 = s2.sb("st6", [128, 2, 6])
                            mv = s2.sb("mv", [128, 2])
                            B_mv = Buf()
                            yb = [s2.sb(f"yb{j}", [128, 8, 512], BF16) for j in range(2)]
                            B_yb = [Buf() for _ in range(2)]
                            pu = [s2.ps(f"pu{j}", [128, 512]) for j in range(2)]
                            B_pu = [Buf() for _ in range(2)]
                            pv = [s2.ps(f"pv{j}", [128, 1024]) for j in range(2)]
                            B_pv = [Buf() for _ in range(2)]
                            psv = s2.ps("psv", [128, 8, 128])
                            B_psv = Buf()
                            npu = 0
                            for tbk in range(NSB):
                                tsl = slice(tbk * 512, (tbk + 1) * 512)
                                y_, by = yb[tbk % 2], B_yb[tbk % 2]
                                for g in range(8):
                                    p_, bp = pu[npu % 2], B_pu[npu % 2]
                                    npu += 1
                                    for kk in range(KC):
                                        MM(p_[:], wub[:, kk, g * 128:(g + 1) * 128], hseq[:, tbk, kk, :], kk == 0, kk == KC - 1,
                                           [B_wb, B_hs], [bp], tick=(kk == KC - 1))
                                    ACT(ug[:, g, :], p_[:], AF.Gelu_apprx_tanh, [bp], [B_ug])
                                def vmm(tt_):
                                    p__, bp__ = pv[tt_ % 2], B_pv[tt_ % 2]
                                    for hf in range(2):
                                        for kk in range(KC):
                                            MM(p__[:, hf * 512:(hf + 1) * 512], hseq[:, tt_ // 4, kk, (tt_ % 4) * 128:(tt_ % 4 + 1) * 128],
                                               wvb[:, kk, hf * 512:(hf + 1) * 512], kk == 0, kk == KC - 1, [B_wb, B_hs], [bp__],
                                               tick=(kk == KC - 1))

                                def stG(tt_):
                                    p__, bp__ = pv[tt_ % 2], B_pv[tt_ % 2]
                                    v__, bv__ = vf[tt_ % 2], B_vf[tt_ % 2]
                                    for hf in range(2):
                                        ACT(v__[:, hf * 512:(hf + 1) * 512], p__[:, hf * 512:(hf + 1) * 512], AF.Gelu_apprx_tanh, [bp__], [bv__])

                                def stN(tt_):
                                    v__, bv__ = vf[tt_ % 2], B_vf[tt_ % 2]
                                    vt__, bvt__ = vt[tt_ % 2], B_vt[tt_ % 2]
                                    for hf in range(2):
                                        BNS(st6[:, hf, :], v__[:, hf * 512:(hf + 1) * 512], [bv__], [B_mv])
                                    BNA(mv[:], st6[:].rearrange("p a b -> p (a b)"), [B_mv], [B_mv])
                                    ACT(mv[:, 1:2], mv[:, 1:2], AF.Sqrt, [B_mv, B_tab], [B_mv], bias=col("eps"))
                                    RECIP(mv[:, 1:2], mv[:, 1:2], [B_mv], [B_mv])
                                    TS(dve, v__[:], v__[:], mv[:, 0:1], None, ALU.subtract, ALU.bypass, [bv__, B_mv], [bv__])
                                    TS(dve, v__[:], v__[:], mv[:, 1:2], None, ALU.mult, ALU.bypass, [bv__, B_mv], [bv__])
                                    TT(pool, v__[:], v__[:], vng[:], ALU.mult, [bv__, B_c0], [bv__])
                                    TT(pool, vt__[:], v__[:], vnb[:], ALU.add, [bv__, B_c0], [bvt__])

                                if tbk == 0:
                                    vmm(0)
                                    if NST > 1:
                                        vmm(1)
                                    stG(0)
                                    stN(0)
                                for q in range(4):
                                    tt = tbk * 4 + q
                                    vt_, bvt = vt[tt % 2], B_vt[tt % 2]
                                    if tt + 1 < NST:
                                        stG(tt + 1)
                                    if tt + 2 < NST:
                                        vmm(tt + 2)
                                    if tt + 1 < NST:
                                        stN(tt + 1)
                                    for g in range(8):
                                        MM(psv[:, g, :], vt_[:, g * 128:(g + 1) * 128], wsT[:, g, :], True, False, [bvt, B_c0], [B_psv], tick=False)
                                        MM(psv[:, g, :], onesrow[0:1, :], bsrow[0:1, g * 128:(g + 1) * 128], False, True, [B_c0], [B_psv],
                                           tick=(g == 7))
                                    TT(dve, y_[:, :, q * 128:(q + 1) * 128], psv[:], ug[:, :, q * 128:(q + 1) * 128], ALU.mult,
                                       [B_psv, B_ug], [by])
                                dma_blk(sp, y_, yT_d, b * NSB + tbk, 8, 16, False, [by], [B_yT[b][8 + g] for g in range(8)])

            def mixer_cd():
                w_in = cd_w_in[0]
                zscale = 128 ** -0.5
                with scope() as sc:
                    hseq = sc.sb("hseq", [128, NSB, KC, 512], BF16)
                    B_hs = Buf()
                    wpl = sc.sb("wpl", [128, 4, 2, 256], BF16)
                    tri = sc.sb("tri", [128, 128])
                    ones_s = sc.sb("ones_s", [128, S], BF16)
                    invc = sc.sb("invc", [128, 16])
                    B_c1 = Buf("l1const")
                    k.dma(pool, wpl[:], cd_w_pool[0].rearrange("g (dc p) e -> p g dc e", p=128), writes=[B_c1])
                    MEMSET(pool, tri[:], 1.0, [B_c1])
                    k.op(pool, lambda: nc.gpsimd.affine_select(out=tri[:], in_=tri[:], compare_op=ALU.is_gt, fill=0.0,
                                                               base=0, pattern=[[-1, 128]], channel_multiplier=1), [B_c1], [B_c1])
                    MEMSET(pool, ones_s[:], 1.0, [B_c1])
                    for t_ in range(16):
                        MEMSET(dve, invc[:, t_:t_ + 1], 1.0 / (t_ + 1), [B_c1])
                    v_sb = sc.sb("v_sb", [128, NST, 1024], BF16)
                    B_v = Buf()
                    for b in range(NB):
                        for tbk in range(NSB):
                            tb = b * NSB + tbk
                            dma_blk(sp, hseq[:, tbk], hT_d, tb, 0, KC, True, [B_hT[tb]], [B_hs])
                        with scope() as s2:
                            wv = s2.sb("wv", [128, KC, 1024], BF16)
                            B_wv = Buf()
                            load_w_bf(wv, w_in[:, 2048:3072], KC, 4, [], B_wv)
                            pv = [s2.ps(f"pv{j}", [128, 1024]) for j in range(2)]
                            B_pv = [Buf() for _ in range(2)]
                            for tt in range(NST):
                                p_, bp = pv[tt % 2], B_pv[tt % 2]
                                for hf in range(2):
                                    for kk in range(KC):
                                        MM(p_[:, hf * 512:(hf + 1) * 512], hseq[:, tt // 4, kk, (tt % 4) * 128:(tt % 4 + 1) * 128],
                                           wv[:, kk, hf * 512:(hf + 1) * 512], kk == 0, kk == KC - 1, [B_wv, B_hs], [bp], tick=(kk == KC - 1))
                                ACT(v_sb[:, tt, 0:512], p_[:, 0:512], AF.Copy, [bp], [B_v])
                                ACT(v_sb[:, tt, 512:1024], p_[:, 512:1024], AF.Copy, [bp], [B_v])
                        with scope() as s2:
                            wp = [s2.sb(f"wp{j}", [128, KC, 128], BF16) for j in range(2)]
                            B_wp = [Buf() for _ in range(2)]
                            pp = s2.sb("pp", [128, 16 + S])
                            sa = s2.sb("sa", [128, 16 + S])
                            sbb = s2.sb("sbb", [128, 16 + S])
                            pt = s2.sb("pt", [128, S])
                            tmpc = s2.sb("tmpc", [128, 16])
                            pbf = s2.sb("pbf", [128, 2, S], BF16)
                            yd = [s2.sb(f"yd{j}", [128, S], BF16) for j in range(2)]
                            B_pp, B_sa, B_sbb, B_pt, B_pbf = [Buf() for _ in range(5)]
                            B_yd = [Buf() for _ in range(2)]
                            ppj = [s2.ps(f"ppj{j}", [128, 512]) for j in range(2)]
                            B_ppj = [Buf() for _ in range(2)]
                            pyd = [s2.ps(f"pyd{j}", [128, 512]) for j in range(2)]
                            B_pyd = [Buf() for _ in range(2)]
                            MEMSET(dve, pp[:, 0:16], 0.0, [B_pp])
                            MEMSET(dve, sa[:, 0:16], 0.0, [B_sa])
                            MEMSET(dve, sbb[:, 0:16], 0.0, [B_sbb])
                            npp = nyd = 0
                            for pc in range(8):
                                g = pc // 2
                                w = POOL_W[g]
                                w_, bw = wp[pc % 2], B_wp[pc % 2]
                                load_w_bf(w_, w_in[:, 3072 + pc * 128:3072 + (pc + 1) * 128], KC, 16, [], bw)
                                for tbk in range(NSB):
                                    p_, bp = ppj[npp % 2], B_ppj[npp % 2]
                                    npp += 1
                                    for kk in range(KC):
                                        MM(p_[:], w_[:, kk, :], hseq[:, tbk, kk, :], kk == 0, kk == KC - 1,
                                           [bw, B_hs], [bp], tick=(kk == KC - 1))
                                    ACT(pp[:, 16 + tbk * 512:16 + (tbk + 1) * 512], p_[:], AF.Copy, [bp], [B_pp])
                                TT(dve, sa[:, 16:], pp[:, 16:], pp[:, 15:15 + S], ALU.add, [B_pp], [B_sa])
                                cur, bcur = sa, B_sa
                                oth, both = sbb, B_sbb
                                sh = 2
                                while sh < w:
                                    TT(dve, oth[:, 16:], cur[:, 16:], cur[:, 16 - sh:16 - sh + S], ALU.add, [bcur], [both])
                                    cur, bcur, oth, both = oth, both, cur, bcur
                                    sh *= 2
                                STT(pt[:], cur[:, 16:], 1.0 / w, pp[:, 16:], ALU.mult, ALU.subtract, [bcur, B_pp], [B_pt])
                                TT(dve, tmpc[:, 0:w - 1], cur[:, 16:16 + w - 1], invc[:, 0:w - 1], ALU.mult, [bcur, B_c1], [B_pt])
                                TT(dve, pt[:, 0:w - 1], tmpc[:, 0:w - 1], pp[:, 16:16 + w - 1], ALU.subtract, [B_pt, B_pp], [B_pt])
                                CP(pool, pbf[:, pc % 2, :], pt[:], [B_pt], [B_pbf])
                                if pc % 2 == 1:
                                    for ec in range(2):
                                        y_, by = yd[nyd % 2], B_yd[nyd % 2]
                                        nyd += 1
                                        for tbk in range(NSB):
                                            tsl = slice(tbk * 512, (tbk + 1) * 512)
                                            p_, bp = pyd[tbk % 2], B_pyd[tbk % 2]
                                            MM(p_[:], wpl[:, g, 0, ec * 128:(ec + 1) * 128], pbf[:, 0, tsl], True, False, [B_c1, B_pbf], [bp], tick=False)
                                            MM(p_[:], wpl[:, g, 1, ec * 128:(ec + 1) * 128], pbf[:, 1, tsl], False, True, [B_c1, B_pbf], [bp], tick=True)
                                            ACT(y_[:, tsl], p_[:], AF.Identity, [bp, B_tab], [by], scale=col("psc", g * 2 + ec))
                                        ch = 8 + g * 2 + ec
                                        dma_rows(sp, yT_d, b, ch, y_, [by], [B_yT[b][ch]])
                        with scope() as s2:
                            wq = [s2.sb(f"wq{j}", [128, KC, 128], BF16) for j in range(2)]
                            wk = [s2.sb(f"wk{j}", [128, KC, 128], BF16) for j in range(2)]
                            B_wq = [Buf() for _ in range(2)]
                            qT = s2.sb("qT", [128, S], BF16)
                            kT = s2.sb("kT", [128, S], BF16)
                            B_q, B_k = Buf(), Buf()
                            ee = [s2.sb(f"ee{j}", [128, S]) for j in range(2)]
                            ff = [s2.sb(f"ff{j}", [128, S]) for j in range(2)]
                            zs = [s2.sb(f"zs{j}", [128, S]) for j in range(2)]
                            negt = [s2.sb(f"negt{j}", [128, 1]) for j in range(2)]
                            wbf = [s2.sb(f"wbf{j}", [128, S], BF16) for j in range(2)]
                            wT = [s2.sb(f"wT{j}", [128, NST, 128], BF16) for j in range(2)]
                            B_ee, B_ff, B_zs, B_nt, B_wbf, B_wT = ([Buf() for _ in range(2)] for _ in range(6))
                            yc = [s2.sb(f"yc{j}", [128, S], BF16) for j in range(2)]
                            B_yc = [Buf() for _ in range(2)]
                            zps = [s2.ps(f"zps{j}", [128, 2, 512]) for j in range(2)]
                            B_z = [Buf() for _ in range(2)]
                            pqk = s2.ps("pqk", [128, 512])
                            B_pqk = Buf()
                            wtp = [s2.ps(f"wtp{j}", [128, 4, 128], BF16) for j in range(2)]
                            B_wtp = [Buf() for _ in range(2)]
                            ycp = s2.ps("ycp", [128, 128])
                            B_ycp = Buf()
                            cnt = {"z": 0, "wt": 0}
                            for h in range(8):
                                wq_, wk_, bw = wq[h % 2], wk[h % 2], B_wq[h % 2]
                                load_w_bf(wq_, w_in[:, h * 128:(h + 1) * 128], KC, 16, [], bw)
                                load_w_bf(wk_, w_in[:, 1024 + h * 128:1024 + (h + 1) * 128], KC, 16, [], bw)
                                for tbk in range(NSB):
                                    tsl = slice(tbk * 512, (tbk + 1) * 512)
                                    for kk in range(KC):
                                        MM(pqk[:], wq_[:, kk, :], hseq[:, tbk, kk, :], kk == 0, kk == KC - 1, [bw, B_hs], [B_pqk], tick=(kk == KC - 1))
                                    ACT(qT[:, tsl], pqk[:], AF.Copy, [B_pqk], [B_q])
                                    for kk in range(KC):
                                        MM(pqk[:], wk_[:, kk, :], hseq[:, tbk, kk, :], kk == 0, kk == KC - 1, [bw, B_hs], [B_pqk], tick=(kk == KC - 1))
                                    ACT(kT[:, tsl], pqk[:], AF.Copy, [B_pqk], [B_k])
                                y_, by = yc[h % 2], B_yc[h % 2]

                                def stA(qi):
                                    st_ = qi % 2
                                    nk_ = (qi + 1) * 128
                                    dsl = slice(qi * 128, (qi + 1) * 128)
                                    for hf in range((nk_ + 1023) // 1024):
                                        zp, bz = zps[cnt["z"] % 2], B_z[cnt["z"] % 2]
                                        cnt["z"] += 1
                                        k0h = hf * 1024
                                        k1h = min(nk_, k0h + 1024)
                                        nchh = (k1h - k0h + 511) // 512
                                        for kc in range(nchh):
                                            a0 = k0h + kc * 512
                                            a1 = min(k1h, a0 + 512)
                                            MM(zp[:, kc, 0:a1 - a0], qT[:, dsl], kT[:, a0:a1], True, True, [B_q, B_k], [bz], tick=(kc == nchh - 1))
                                        for kc in range(nchh):
                                            a0 = k0h + kc * 512
                                            a1 = min(k1h, a0 + 512)
                                            ACT(ee[st_][:, a0:a1], zp[:, kc, 0:a1 - a0], AF.Exp, [bz], [B_ee[st_]], scale=zscale)
                                            ACT(zs[st_][:, a0:a1], zp[:, kc, 0:a1 - a0], AF.Copy, [bz], [B_zs[st_]], scale=zscale)
                                    ACT(ee[st_][:, 0:nk_], ee[st_][:, 0:nk_], AF.Ln, [B_ee[st_], B_tab], [B_ee[st_]], bias=col("one"))

                                def stB(qi):
                                    st_ = qi % 2
                                    nk_ = (qi + 1) * 128
                                    dsl = slice(qi * 128, (qi + 1) * 128)
                                    e_, f_, z_, w_ = ee[st_], ff[st_], zs[st_], wbf[st_]
                                    TT(pool, e_[:, dsl], e_[:, dsl], tri[:], ALU.mult, [B_ee[st_], B_c1], [B_ee[st_]])
                                    TT(pool, z_[:, 0:nk_], z_[:, 0:nk_], e_[:, 0:nk_], ALU.subtract, [B_zs[st_], B_ee[st_]], [B_zs[st_]])
                                    SCAN(f_[:, 0:nk_], ones_s[:, 0:nk_], e_[:, 0:nk_], col("zero"), [B_ee[st_], B_c1, B_tab], [B_ff[st_]])
                                    TS(pool, negt[st_][:], f_[:, nk_ - 1:nk_], -1.0, 0.0, ALU.mult, ALU.add, [B_ff[st_]], [B_nt[st_]])
                                    TT(dve, f_[:, 0:nk_], f_[:, 0:nk_], z_[:, 0:nk_], ALU.add, [B_zs[st_], B_ff[st_]], [B_ff[st_]])
                                    ACT(w_[:, 0:nk_], f_[:, 0:nk_], AF.Exp, [B_ff[st_], B_nt[st_]], [B_wbf[st_]], bias=negt[st_][:])
                                    TT(pool, w_[:, dsl], w_[:, dsl], tri[:], ALU.mult, [B_wbf[st_], B_c1], [B_wbf[st_]])

                                def stC1(qi):
                                    st_ = qi % 2
                                    for kb4 in range((qi + 4) // 4):
                                        j_ = cnt["wt"] % 2
                                        cnt["wt"] += 1
                                        p_, bp = wtp[j_], B_wtp[j_]
                                        nb_ = min(4, qi + 1 - kb4 * 4)
                                        for ii in range(nb_):
                                            kb = kb4 * 4 + ii
                                            TR(p_[:, ii, :], wbf[st_][:, kb * 128:(kb + 1) * 128], ident_bf[:], [B_wbf[st_], B_const], [bp], tick=(ii == nb_ - 1))
                                        CP(act if j_ else dve, wT[st_][:, kb4 * 4:kb4 * 4 + nb_, :], p_[:, 0:nb_, :], [bp], [B_wT[st_]])

                                def stC2(qi):
                                    st_ = qi % 2
                                    dsl = slice(qi * 128, (qi + 1) * 128)
                                    for kb in range(qi + 1):
                                        MM(ycp[:], v_sb[:, kb, h * 128:(h + 1) * 128], wT[st_][:, kb, :], kb == 0, kb == qi, [B_v, B_wT[st_]], [B_ycp],
                                           tick=(kb == qi))
                                    CP(dve, y_[:, dsl], ycp[:], [B_ycp], [by])

                                stA(0)
                                if NST > 1:
                                    stA(1)
                                stB(0)
                                for qi in range(NST):
                                    stC1(qi)
                                    if qi + 2 < NST:
                                        stA(qi + 2)
                                    if qi + 1 < NST:
                                        stB(qi + 1)
                                    stC2(qi)
                                dma_rows(sp, yT_d, b, h, y_, [by], [B_yT[b][h]])

            stages = []
            if stop_after != "xt":
                mixer_ab()
                if stop_after != "mix0":
                    gemm_res_ln("o0", ab_w_out[0], KC, yT_d, B_yT, 0, 0, False)
                    if stop_after != "ln00":
                        ffn1(0)
                        if stop_after != "ffn10":
                            gemm_res_ln("d0", ffn_w_down[0], FC, aT_d, B_aT, 0, 1, False)
                            if stop_after != "ln01":
                                mixer_cd()
                                if stop_after != "mix1":
                                    gemm_res_ln("o1", cd_w_out[0], KC, yT_d, B_yT, 1, 0, False)
                                    ffn1(1)
                                    gemm_res_ln("d1", ffn_w_down[1], FC, aT_d, B_aT, 1, 1, True)

        except StopBuild:
            pass
        k.finish()
    return nc


INPUT_NAMES = ["x", "c", "ada_w", "ada_b", "norm_g", "norm_b", "ffn_w_gate", "ffn_w_up", "ffn_w_down",
               "ab_w_in", "ab_conv_w", "ab_conv_b", "ab_w_r", "ab_b_r", "ab_w_i", "ab_b_i", "ab_lambda",
               "ab_vnorm_g", "ab_vnorm_b", "ab_w_s", "ab_b_s", "ab_w_out",
               "cd_w_in", "cd_w_pool", "cd_pool_scale", "cd_w_out"]


def kernel(**inputs):
    x = np.ascontiguousarray(np.asarray(inputs["x"], dtype=np.float32))
    B, S, D = x.shape
    NB = B // NCORES
    nc = build(NB, S)
    shared = {n: np.ascontiguousarray(np.asarray(inputs[n], dtype=np.float32)) for n in INPUT_NAMES if n not in ("x", "c")}
    cfull = np.ascontiguousarray(np.asarray(inputs["c"], dtype=np.float32))
    in_maps = []
    for i in range(NCORES):
        m = dict(shared)
        m["x"] = x[i * NB:(i + 1) * NB].reshape(NB * S, D)
        m["c"] = cfull[i * NB:(i + 1) * NB]
        in_maps.append(m)
    res = run_bass_kernel_spmd(nc, in_maps, core_ids=list(range(NCORES)))
    outs = [np.asarray(r["out"]).reshape(NB, S, D) for r in res.results]
    return np.concatenate(outs, axis=0).astype(np.float32)
```
